# Optimizing a Trainium2 kernel written in Bass

```python
import jax, jax.numpy as jnp
from jax import lax
import numpy as np

D_MODEL = 1024
BATCH = 8
SEQ = 8192
DEPTH = 1

MEM_LEN = 256
HEAD_DIM = 64
NSA_HEADS = 8
NSA_GROUPS = 2
NSA_HPG = NSA_HEADS // NSA_GROUPS
NSA_WIDTH = NSA_HEADS * HEAD_DIM
KV_WIDTH = NSA_GROUPS * HEAD_DIM
CMP_BLOCK = 32
CMP_STRIDE = 16
CMP_HIDDEN = 256
SLC_BLOCK = 64
SLC_TOPK = 16
WINDOW = 512
NSA_Q_BLOCK = 64
FORCE = 1e6
CONV_WIDTH = 512
CONV_KSIZE = 31
MEM_HEADS = 4
MEM_HEAD_DIM = 128
MEM_WIDTH = MEM_HEADS * MEM_HEAD_DIM
N_BRANCH = 3
D_FF = -(-8 * D_MODEL // (3 * 256)) * 256
IN_COLS = NSA_WIDTH + 6 * KV_WIDTH + 3 * NSA_HEADS + 2 * CONV_WIDTH + MEM_WIDTH + N_BRANCH * D_MODEL
RMS_EPS = 1e-6
LN_EPS = 1e-5

kernel_name = "nsa_conformer_memxattn_gated_hybrid"


def rms_norm(x, g):
    xf = x.astype(jnp.float32)
    y = xf * lax.rsqrt(jnp.mean(xf * xf, axis=-1, keepdims=True) + RMS_EPS)
    return (y * g.astype(jnp.float32)).astype(x.dtype)


def layer_norm(x, g, b):
    xf = x.astype(jnp.float32)
    mu = jnp.mean(xf, axis=-1, keepdims=True)
    var = jnp.mean(jnp.square(xf - mu), axis=-1, keepdims=True)
    y = (xf - mu) * lax.rsqrt(var + LN_EPS) * g.astype(jnp.float32) + b.astype(jnp.float32)
    return y.astype(x.dtype)


def masked_softmax(s, mask):
    s = jnp.where(mask, s.astype(jnp.float32), -jnp.inf)
    m = jnp.max(s, axis=-1, keepdims=True)
    m = jnp.where(jnp.isfinite(m), m, 0.0)
    e = jnp.exp(s - m)
    den = jnp.sum(e, axis=-1, keepdims=True)
    return e / jnp.where(den > 0, den, 1.0)


def compress_blocks(kv, pos, w1, w2):
    b, g, s, dh = kv.shape
    chunks = kv.reshape(b, g, s // CMP_STRIDE, CMP_STRIDE, dh)
    blocks = jnp.concatenate([chunks[:, :, :-1], chunks[:, :, 1:]], axis=3)
    blocks = (blocks + pos).reshape(b, g, blocks.shape[2], CMP_BLOCK * dh)
    return jax.nn.gelu(blocks @ w1) @ w2


def nsa_attention(q, kc, vc, ks, vs, kw, vw, gate):
    b, g, hpg, s, dh = q.shape
    n_c = kc.shape[2]
    n_sel = ks.shape[2]
    topk = min(SLC_TOPK, n_sel)
    ratio = SLC_BLOCK // CMP_STRIDE
    scale = dh ** -0.5
    qb_len = NSA_Q_BLOCK
    cmp_end = jnp.arange(n_c) * CMP_STRIDE + CMP_BLOCK - 1
    sel_ids = jnp.arange(n_sel)
    bi = jnp.arange(b)[:, None, None, None]
    gi = jnp.arange(g)[None, :, None, None]

    def block(qb):
        q0 = qb * qb_len
        t = q0 + jnp.arange(qb_len)
        qq = lax.dynamic_slice_in_dim(q, q0, qb_len, axis=3) * scale
        gg = lax.dynamic_slice_in_dim(gate, q0, qb_len, axis=3)

        s_c = jnp.einsum('bghqd,bgcd->bghqc', qq, kc)
        p_c = masked_softmax(s_c, cmp_end[None, :] <= t[:, None])
        o_c = jnp.einsum('bghqc,bgcd->bghqd', p_c.astype(vc.dtype), vc)

        imp = jnp.sum(p_c, axis=2)
        imp = jnp.pad(imp, ((0, 0), (0, 0), (0, 0), (1, ratio * (n_sel + 1) - n_c - 1)))
        rows = imp.reshape(b, g, qb_len, n_sel + 1, ratio)
        imp_s = jnp.sum(rows[..., :-1, :], axis=-1) + rows[..., 1:, 0]
        cur = t // SLC_BLOCK
        forced = (sel_ids[None, :] == 0) | (sel_ids[None, :] == cur[:, None]) | (sel_ids[None, :] == cur[:, None] - 1)
        future = sel_ids[None, :] * SLC_BLOCK > t[:, None]
        imp_s = jnp.where(forced, FORCE, jnp.where(future, -FORCE, imp_s))
        _, idx = lax.top_k(imp_s, topk)

        k_g = ks[bi, gi, idx]
        v_g = vs[bi, gi, idx]
        tok = idx[..., None] * SLC_BLOCK + jnp.arange(SLC_BLOCK)
        s_s = jnp.einsum('bghqd,bgqkld->bghqkl', qq, k_g).reshape(b, g, hpg, qb_len, topk * SLC_BLOCK)
        m_s = (tok <= t[:, None, None]).reshape(b, g, 1, qb_len, topk * SLC_BLOCK)
        p_s = masked_softmax(s_s, m_s).reshape(b, g, hpg, qb_len, topk, SLC_BLOCK)
        o_s = jnp.einsum('bghqkl,bgqkld->bghqd', p_s.astype(v_g.dtype), v_g)

        kwb = lax.dynamic_slice_in_dim(kw, q0, WINDOW + qb_len, axis=2)
        vwb = lax.dynamic_slice_in_dim(vw, q0, WINDOW + qb_len, axis=2)
        pos = q0 - WINDOW + jnp.arange(WINDOW + qb_len)
        diff = t[:, None] - pos[None, :]
        m_w = (diff >= 0) & (diff < WINDOW) & (pos[None, :] >= 0)
        s_w = jnp.einsum('bghqd,bgkd->bghqk', qq, kwb)
        p_w = masked_softmax(s_w, m_w)
        o_w = jnp.einsum('bghqk,bgkd->bghqd', p_w.astype(vwb.dtype), vwb)

        return gg[..., 0:1] * o_c + gg[..., 1:2] * o_s + gg[..., 2:3] * o_w

    out = lax.map(block, jnp.arange(s // qb_len))
    out = jnp.transpose(out, (1, 0, 4, 2, 3, 5))
    return out.reshape(b, s, g * hpg * dh)


def hybrid_layer(x, mem, norm_mix, w_in, nsa_qk_norm, cmp_pos, cmp_w1, cmp_w2, w_nsa_out,
                 conv_w, conv_b, conv_ln_g, conv_ln_b, w_conv_out, norm_mem, w_mem_kv,
                 mem_qk_norm, w_mem_out, w_out, norm_ffn, w_gate, w_up, w_down):
    b, s, _ = x.shape
    xn = rms_norm(x, norm_mix)
    proj = xn @ w_in
    sizes = [NSA_WIDTH, 6 * KV_WIDTH, 3 * NSA_HEADS, 2 * CONV_WIDTH, MEM_WIDTH, N_BRANCH * D_MODEL]
    points = [int(p) for p in np.cumsum(sizes)[:-1]]
    q_nsa, kv_nsa, g_nsa, conv_in, q_mem, g_merge = jnp.split(proj, points, axis=-1)

    def heads(t_, n):
        return jnp.transpose(t_.reshape(b, s, n, HEAD_DIM), (0, 2, 1, 3))
    q = rms_norm(heads(q_nsa, NSA_HEADS), nsa_qk_norm[0]).reshape(b, NSA_GROUPS, NSA_HPG, s, HEAD_DIM)
    k_c, v_c, k_s, v_s, k_w, v_w = [heads(t_, NSA_GROUPS) for t_ in jnp.split(kv_nsa, 6, axis=-1)]
    kc = rms_norm(compress_blocks(k_c, cmp_pos[0], cmp_w1[0], cmp_w2[0]), nsa_qk_norm[1])
    vc = compress_blocks(v_c, cmp_pos[1], cmp_w1[1], cmp_w2[1])
    n_sel = s // SLC_BLOCK
    ks = rms_norm(k_s, nsa_qk_norm[2]).reshape(b, NSA_GROUPS, n_sel, SLC_BLOCK, HEAD_DIM)
    vs = v_s.reshape(b, NSA_GROUPS, n_sel, SLC_BLOCK, HEAD_DIM)
    pad = ((0, 0), (0, 0), (WINDOW, 0), (0, 0))
    kw = jnp.pad(rms_norm(k_w, nsa_qk_norm[3]), pad)
    vw = jnp.pad(v_w, pad)
    gate = jnp.transpose(jax.nn.sigmoid(g_nsa).reshape(b, s, NSA_GROUPS, NSA_HPG, 3), (0, 2, 3, 1, 4))
    o_nsa = nsa_attention(q, kc, vc, ks, vs, kw, vw, gate)
    br_nsa = o_nsa @ w_nsa_out

    a, ag = jnp.split(conv_in, 2, axis=-1)
    u = a * jax.nn.sigmoid(ag)
    u = jnp.pad(u, ((0, 0), (CONV_KSIZE - 1, 0), (0, 0)))
    u = lax.conv_general_dilated(u, conv_w, (1,), 'VALID', dimension_numbers=('NWC', 'WIO', 'NWC'),
                                 feature_group_count=CONV_WIDTH) + conv_b
    u = jax.nn.silu(layer_norm(u, conv_ln_g, conv_ln_b))
    br_conv = u @ w_conv_out

    qm = rms_norm(q_mem.reshape(b, s, MEM_HEADS, MEM_HEAD_DIM), mem_qk_norm[0])
    kvm = rms_norm(mem, norm_mem) @ w_mem_kv
    km, vm = jnp.split(kvm, 2, axis=-1)
    km = rms_norm(km.reshape(b, -1, MEM_HEADS, MEM_HEAD_DIM), mem_qk_norm[1])
    vm = vm.reshape(b, -1, MEM_HEADS, MEM_HEAD_DIM)
    s_m = jnp.einsum('bshd,bmhd->bhsm', qm, km).astype(jnp.float32) * (MEM_HEAD_DIM ** -0.5)
    p_m = jax.nn.softmax(s_m, axis=-1)
    o_m = jnp.einsum('bhsm,bmhd->bshd', p_m.astype(vm.dtype), vm).reshape(b, s, MEM_WIDTH)
    br_mem = o_m @ w_mem_out

    gm = jax.nn.sigmoid(g_merge).reshape(b, s, N_BRANCH, D_MODEL)
    merged = gm[:, :, 0] * br_nsa + gm[:, :, 1] * br_conv + gm[:, :, 2] * br_mem
    h = x + merged @ w_out

    hn = rms_norm(h, norm_ffn)
    return h + (jax.nn.silu(hn @ w_gate) * (hn @ w_up)) @ w_down


def setup_inputs(seed: int = 0) -> dict:
    key = jax.random.key(seed)
    k = jax.random.split(key, 24)
    f32 = jnp.float32
    L = DEPTH

    def nrm(kk, shape, fan_in):
        return jax.random.normal(kk, shape, f32) * (fan_in ** -0.5)

    def gain(kk, shape):
        return 1.0 + 0.02 * jax.random.normal(kk, shape, f32)

    def small(kk, shape):
        return 0.01 * jax.random.normal(kk, shape, f32)

    return {
        "x": jax.random.normal(k[0], (BATCH, SEQ, D_MODEL), f32),
        "mem": jax.random.normal(k[1], (BATCH, MEM_LEN, D_MODEL), f32),
        "norm_mix": gain(k[2], (L, D_MODEL)),
        "w_in": nrm(k[3], (L, D_MODEL, IN_COLS), D_MODEL),
        "nsa_qk_norm": gain(k[4], (L, 4, HEAD_DIM)),
        "cmp_pos": 2.0 * small(k[5], (L, 2, CMP_BLOCK, HEAD_DIM)),
        "cmp_w1": nrm(k[6], (L, 2, CMP_BLOCK * HEAD_DIM, CMP_HIDDEN), CMP_BLOCK * HEAD_DIM),
        "cmp_w2": nrm(k[7], (L, 2, CMP_HIDDEN, HEAD_DIM), CMP_HIDDEN),
        "w_nsa_out": nrm(k[8], (L, NSA_WIDTH, D_MODEL), NSA_WIDTH),
        "conv_w": nrm(k[9], (L, CONV_KSIZE, 1, CONV_WIDTH), CONV_KSIZE),
        "conv_b": small(k[10], (L, CONV_WIDTH)),
        "conv_ln_g": gain(k[11], (L, CONV_WIDTH)),
        "conv_ln_b": small(k[12], (L, CONV_WIDTH)),
        "w_conv_out": nrm(k[13], (L, CONV_WIDTH, D_MODEL), CONV_WIDTH),
        "norm_mem": gain(k[14], (L, D_MODEL)),
        "w_mem_kv": nrm(k[15], (L, D_MODEL, 2 * MEM_WIDTH), D_MODEL),
        "mem_qk_norm": gain(k[16], (L, 2, MEM_HEAD_DIM)),
        "w_mem_out": nrm(k[17], (L, MEM_WIDTH, D_MODEL), MEM_WIDTH),
        "w_out": nrm(k[18], (L, D_MODEL, D_MODEL), D_MODEL),
        "norm_ffn": gain(k[19], (L, D_MODEL)),
        "w_gate": nrm(k[20], (L, D_MODEL, D_FF), D_MODEL),
        "w_up": nrm(k[21], (L, D_MODEL, D_FF), D_MODEL),
        "w_down": nrm(k[22], (L, D_FF, D_MODEL), D_FF),
    }


def reference(x, mem, norm_mix, w_in, nsa_qk_norm, cmp_pos, cmp_w1, cmp_w2, w_nsa_out,
              conv_w, conv_b, conv_ln_g, conv_ln_b, w_conv_out, norm_mem, w_mem_kv,
              mem_qk_norm, w_mem_out, w_out, norm_ffn, w_gate, w_up, w_down):
    h = x
    for l in range(DEPTH):
        h = hybrid_layer(h, mem, norm_mix[l], w_in[l], nsa_qk_norm[l], cmp_pos[l], cmp_w1[l], cmp_w2[l],
                         w_nsa_out[l], conv_w[l], conv_b[l], conv_ln_g[l], conv_ln_b[l], w_conv_out[l],
                         norm_mem[l], w_mem_kv[l], mem_qk_norm[l], w_mem_out[l], w_out[l], norm_ffn[l],
                         w_gate[l], w_up[l], w_down[l])
    return h
```

```python
import numpy as np
import ml_dtypes
from contextlib import ExitStack
import concourse.bass as bass
import concourse.mybir as mybir
from concourse.bass_utils import run_bass_kernel_spmd

F32 = mybir.dt.float32
BF16 = mybir.dt.bfloat16
AF = mybir.ActivationFunctionType
ALU = mybir.AluOpType
ENGS = ("pe", "act", "dve", "pool", "sp")
T = 256
NEG = -30000.0
IN_COLS = 5912
DFF = 2816


class Op:
    __slots__ = ("eng", "fn", "reads", "writes", "dma", "deps", "flag", "ordinal", "dval", "idx", "waitall")

    def __init__(self, eng, fn, reads, writes, dma, idx):
        self.eng = eng; self.fn = fn; self.reads = reads; self.writes = writes
        self.dma = dma; self.deps = []; self.flag = False; self.ordinal = 0
        self.dval = 0; self.idx = idx; self.waitall = False


class Prog:
    def __init__(self):
        self.ops = []; self.last_w = {}; self.readers = {}; self.dma_groups = {}

    def op(self, eng, fn, reads=(), writes=(), dma=None, waitall=False):
        o = Op(eng, fn, tuple(reads), tuple(writes), dma, len(self.ops))
        o.waitall = waitall
        deps = {}
        wset = set(o.writes)
        for k in o.reads:
            d = self.last_w.get(k)
            if d is not None:
                deps[d.idx] = (d, True)
        for k in o.writes:
            d = self.last_w.get(k)
            if d is not None and d.idx not in deps:
                deps[d.idx] = (d, False)
            for r in self.readers.get(k, ()):
                if r.idx not in deps:
                    deps[r.idx] = (r, False)
        for d, raw in deps.values():
            if d.dma is None and o.dma is None and d.eng == o.eng:
                if o.eng == "pe":
                    continue
            o.deps.append(d)
        for k in o.reads:
            if k not in wset:
                self.readers.setdefault(k, []).append(o)
        for k in o.writes:
            self.last_w[k] = o
            self.readers[k] = []
        if dma is not None:
            g = self.dma_groups.setdefault(dma, [0])
            g[0] += 1
            o.dval = 16 * g[0]
        self.ops.append(o)
        return o

    def emit(self, nc, stack):
        for o in self.ops:
            for d in o.deps:
                if d.dma is None:
                    d.flag = True
        cnt = {e: 0 for e in ENGS}
        for o in self.ops:
            if o.dma is None and o.flag:
                cnt[o.eng] += 1
                o.ordinal = cnt[o.eng]
        esem = {e: stack.enter_context(nc.semaphore("s_" + e)) for e in ENGS}
        dsem = {}
        for g in self.dma_groups:
            dsem[g] = stack.enter_context(nc.semaphore("d_%d" % len(dsem)))
        per = {e: [] for e in ENGS}
        for o in self.ops:
            per[o.eng].append(o)
        groups = self.dma_groups

        def run(engname, eng):
            seen = {}
            for o in per[engname]:
                need = {}
                for d in o.deps:
                    if d.dma is not None:
                        key = ("d", d.dma)
                        val = 16 * groups[d.dma][0] if d.waitall else d.dval
                    else:
                        key = ("e", d.eng)
                        val = d.ordinal
                    if val > need.get(key, 0):
                        need[key] = val
                for key, val in need.items():
                    if seen.get(key, 0) >= val:
                        continue
                    seen[key] = val
                    s = dsem[key[1]] if key[0] == "d" else esem[key[1]]
                    eng.wait_ge(s, val)
                ins = o.fn(eng)
                if o.dma is not None:
                    ins.then_inc(dsem[o.dma], 16)
                elif o.flag:
                    ins.then_inc(esem[o.eng], 1)
            fin = {}
            for o in per[engname]:
                if o.dma is not None:
                    v = 16 * groups[o.dma][0] if o.waitall else o.dval
                    fin[o.dma] = max(fin.get(o.dma, 0), v)
            for g, v in fin.items():
                eng.wait_ge(dsem[g], v)

        with nc.Block() as block:
            @block.tensor
            def _(e): run("pe", e)

            @block.scalar
            def _(e): run("act", e)

            @block.vector
            def _(e): run("dve", e)

            @block.gpsimd
            def _(e): run("pool", e)

            @block.sync
            def _(e): run("sp", e)


def make_consts(S):
    bf = ml_dtypes.bfloat16
    NCC = max(1, S // 2048)
    p = np.arange(128)[:, None]
    c = {}
    c["ident"] = np.eye(128, dtype=np.float32).astype(bf)
    c["ones"] = np.ones((128, 128), np.float32).astype(bf)
    bo = np.zeros((128, 128), np.float32); bo[:64, :64] = 1; bo[64:, 64:] = 1
    c["bones"] = bo.astype(bf)
    x = np.arange(S)[None, :]
    c["wind"] = (x // 64 == p).astype(np.float32).astype(bf)
    y = np.arange(384)[None, :]
    c["cw"] = np.where(p > y - 128, NEG, 0.0).astype(np.float32).astype(bf)
    c["dw"] = np.where(y - p >= 128, NEG, 0.0).astype(np.float32).astype(bf)
    y = np.arange(2304)[None, :]
    c["gw"] = np.where(y < 16 * p + 31, NEG, 0.0).astype(np.float32).astype(bf)
    sel = np.zeros((128, 12, 128), np.float32)
    for m in range(4):
        for b in range(3):
            sel[3 * m + b, m * 3 + b, :64] = 1
            sel[3 * (m + 4) + b, m * 3 + b, 64:] = 1
    c["sel"] = sel.astype(bf)
    A = np.zeros((128, NCC, 129), np.float32)
    for cc in range(NCC):
        for pp in range(128):
            cidx = cc * 128 + pp
            for j in range(128):
                if 4 * j - 1 <= cidx <= 4 * j + 3:
                    A[pp, cc, j] = 1
            A[pp, cc, 128] = 1
    c["aaug"] = A.astype(bf)
    xx = np.arange(256)[None, :]
    rel = xx - 128 - p // 64
    c["fm"] = np.where((rel == 0) | (rel == -1), 1e6, np.where(rel > 0, -1e6, 0.0)).astype(np.float32)
    return c


NPRM = 8 * 3 + 4 + 2 + 124 + 12 + 32
PO_MIX, PO_FFN, PO_MEM, PO_NSA, PO_MQK, PO_CW, PO_CB, PO_LG, PO_LB, PO_POS = 0, 8, 16, 24, 28, 30, 154, 158, 162, 166


def pack_params(inp):
    prm = np.zeros((128, NPRM), np.float32)
    prm[:, PO_MIX:PO_MIX + 8] = inp["norm_mix"][0].reshape(8, 128).T
    prm[:, PO_FFN:PO_FFN + 8] = inp["norm_ffn"][0].reshape(8, 128).T
    prm[:, PO_MEM:PO_MEM + 8] = inp["norm_mem"][0].reshape(8, 128).T
    prm[:, PO_NSA:PO_NSA + 4] = np.concatenate([inp["nsa_qk_norm"][0].T, inp["nsa_qk_norm"][0].T], 0)
    prm[:, PO_MQK:PO_MQK + 2] = inp["mem_qk_norm"][0].T
    cw = inp["conv_w"][0][:, 0, :]
    prm[:, PO_CW:PO_CW + 124] = cw.T.reshape(4, 128, 31).transpose(1, 0, 2).reshape(128, 124)
    prm[:, PO_CB:PO_CB + 4] = inp["conv_b"][0].reshape(4, 128).T
    prm[:, PO_LG:PO_LG + 4] = inp["conv_ln_g"][0].reshape(4, 128).T
    prm[:, PO_LB:PO_LB + 4] = inp["conv_ln_b"][0].reshape(4, 128).T
    pos = inp["cmp_pos"][0]
    prm[:, PO_POS:PO_POS + 32] = pos.reshape(2, 16, 2, 64).transpose(2, 3, 0, 1).reshape(128, 32)
    return prm


class _Stop(Exception):
    pass


def build(S, stop=None):
    NT = S // T
    NKT = S // 128
    NCC = max(1, S // 2048)
    nc = bass.Bass("TRN2", target_bir_lowering=False)
    P = Prog()

    def din(name, shape, dt=F32):
        return nc.dram_tensor(name, list(shape), dt, kind="ExternalInput").ap()

    def dscr(name, shape, dt=BF16):
        return nc.dram_tensor(name, list(shape), dt, kind="Internal").ap()

    xT = din("xT", [1024, S]); memT = din("memT", [1024, 256])
    w_in = din("w_in", [1024, IN_COLS]); w_gate = din("w_gate", [1024, DFF]); w_up = din("w_up", [1024, DFF])
    w_down = din("w_down", [DFF, 1024]); w_out = din("w_out", [1024, 1024])
    w_nsa_out = din("w_nsa_out", [512, 1024]); w_conv_out = din("w_conv_out", [512, 1024])
    w_mem_out = din("w_mem_out", [512, 1024]); w_mem_kv = din("w_mem_kv", [1024, 1024])
    cmp_w1 = din("cmp_w1", [2, 2048, 256]); cmp_w2 = din("cmp_w2", [2, 256, 64])
    prm_d = din("prm", [128, NPRM])
    c_ident = din("ident", [128, 128], BF16); c_ones = din("ones", [128, 128], BF16); c_bones = din("bones", [128, 128], BF16)
    c_wind = din("wind", [128, S], BF16); c_cw = din("cw", [128, 384], BF16); c_dw = din("dw", [128, 384], BF16)
    c_gw = din("gw", [128, 2304], BF16); c_sel = din("sel", [128, 12, 128], BF16)
    c_aaug = din("aaug", [128, NCC, 129], BF16); c_fm = din("fm", [128, 256])
    outT = nc.dram_tensor("outT", [1024, S], F32, kind="ExternalOutput").ap()
    wA = dscr("wA", [1024, IN_COLS]); wG = dscr("wG", [1024, DFF]); wU = dscr("wU", [1024, DFF])
    wD = dscr("wD", [DFF, 1024]); wO = dscr("wO", [1024, 1024]); wN = dscr("wN", [512, 1024])
    wC = dscr("wC", [512, 1024]); wM = dscr("wM", [512, 1024]); wKV = dscr("wKV", [1024, 1024])
    w1b = dscr("w1b", [2, 2048, 256]); w2b = dscr("w2b", [2, 256, 64])

    with ExitStack() as st:
        def sb(name, shape, dt):
            return st.enter_context(nc.sbuf_tensor("sb_" + name, list(shape), dt))

        def psb(name):
            return st.enter_context(nc.psum_tensor(name, [128, 512], F32))

        ident = sb("ident", [128, 128], BF16); ones = sb("ones", [128, 128], BF16); bones = sb("bones", [128, 128], BF16)
        wind = sb("wind", [128, S], BF16); cw = sb("cw", [128, 384], BF16); dw = sb("dw", [128, 384], BF16)
        gw = sb("gw", [128, 2304], BF16); sel = sb("sel", [128, 12, 128], BF16)
        aaug = sb("aaug", [128, NCC, 129], BF16); fm = sb("fm", [128, 256], F32)
        prm = sb("prm", [128, NPRM], F32); prm2 = sb("prm2", [128, 16], F32)
        posb = sb("posb", [128, 32], BF16)
        expo = sb("expo", [128, T], F32)
        KS = sb("KS", [128, S], BF16); VS = sb("VS", [128, NKT, 192], BF16)
        KW = sb("KW", [128, 16, 128], BF16); VW = sb("VW", [128, 16, 192], BF16)
        KC = sb("KC", [128, NCC * 128], BF16); VCT = sb("VCT", [128, NCC * 128], BF16); VC = sb("VC", [128, NCC, 192], BF16)
        KM = sb("KM", [128, 4, 256], BF16); VM = sb("VM", [128, 2, 512], BF16)
        w2p = sb("w2p", [128, 8, 128], BF16); w2f = sb("w2f", [128, 4, 64], BF16)
        b1 = sb("b1", [128, 4], F32)
        ring = sb("ring", [128, 2, 4096], BF16)
        xh2 = sb("xh2", [128, 2, 8, T], F32)
        a8x = sb("a8x", [128, 2, 8, T], BF16)
        a8 = sb("a8", [128, 8, T], BF16)
        actb = sb("actb", [128, 22, T], BF16)
        QP = sb("QP", [128, 8, T], BF16); QM = sb("QM", [128, 4, T], BF16)
        onsa = sb("onsa", [128, 4, T], F32); onsab = sb("onsab", [128, 4, T], BF16)
        cu = sb("cu", [128, 4, T], BF16); om = sb("om", [128, 4, T], BF16)
        XX = sb("XX", [128, 4, 16 + T], BF16)
        ub = sb("ub", [128, 4, 30 + T], BF16)
        PB = sb("PB", [128, 4, 2 * T], BF16)
        lnb = sb("lnb", [128, 3, T], F32)
        sq = sb("sq", [128, 2, T], BF16)
        tf = sb("tf", [128, 8, T], F32)
        rs = sb("rs", [128, 2, T], F32)
        sg = sb("sg", [128, T], BF16)
        cacc = sb("cacc", [128, 4, T], F32); mgb = sb("mgb", [128, 4, T], F32)
        negq = sb("negq", [128, 2, 128], BF16); NEGT = sb("NEGT", [128, 2, T], BF16)
        imp = sb("imp", [128, 2, 2, 128], F32); impf = sb("impf", [128, 2, 128], F32)
        m8 = sb("m8", [128, 16], F32); rden = sb("rden", [128, 2], F32)
        hidf = sb("hidf", [128, 8, 16], F32); hidg = sb("hidg", [128, 8, 16], F32); hidb = sb("hidb", [128, 8, 16], BF16)
        kcf = sb("kcf", [128, 16], F32); kcsq = sb("kcsq", [128, 16], BF16)
        memf = sb("memf", [128, 8, 256], F32)
        _b = {n: psb(n) for n in ("pj0", "pj1", "st0", "st1", "oa0", "oa1", "ms0", "ms1")}
        pools = {"pj": ["pj0", "pj1", "st0", "st1"], "st": ["st0", "st1", "pj0"],
                 "oa": ["oa0", "oa1"], "ms": ["ms0", "ms1"]}
        pctr = {k: 0 for k in pools}

        def bank(pool):
            i = pctr[pool] % len(pools[pool]); pctr[pool] += 1
            nm = pools[pool][i]
            return _b[nm], ("bank", nm)

        tctr = [0]

        def tmpf():
            i = tctr[0] % 8; tctr[0] += 1
            return tf[:, i, :], ("tf", i)

        rctr = [0]

        def rsf():
            i = rctr[0] % 2; rctr[0] += 1
            return rs[:, i, :], ("rs", i)

        sqc = [0]

        def sqf():
            i = sqc[0] % 2; sqc[0] += 1
            return sq[:, i, :], ("sq", i)

        pbc = [0]

        def pbf():
            i = pbc[0] % 4; pbc[0] += 1
            return PB[:, i, :], ("PB", i)

        def MM(out, lhsT, rhs, start, stop, reads, writes, sgc=False):
            P.op("pe", lambda e: e.matmul(out, lhsT=lhsT, rhs=rhs, start=start, stop=stop, skip_group_check=sgc), reads, writes)

        def ACT(out, in_, func, reads, writes, scale=None, bias=None):
            kw = {}
            if scale is not None: kw["scale"] = scale
            if bias is not None: kw["bias"] = bias
            P.op("act", lambda e: e.activation(out=out, in_=in_, func=func, **kw), reads, writes)

        def TS(eng, out, in0, s1, s2, op0, op1, reads, writes):
            if op1 is None:
                P.op(eng, lambda e: e.tensor_scalar(out=out, in0=in0, scalar1=s1, scalar2=None, op0=op0), reads, writes)
            else:
                P.op(eng, lambda e: e.tensor_scalar(out=out, in0=in0, scalar1=s1, scalar2=s2, op0=op0, op1=op1), reads, writes)

        def STT(out, in0, scalar, in1, op0, op1, reads, writes):
            P.op("dve", lambda e: e.scalar_tensor_tensor(out=out, in0=in0, scalar=scalar, in1=in1, op0=op0, op1=op1), reads, writes)

        def TT(eng, out, in0, in1, op, reads, writes):
            P.op(eng, lambda e: e.tensor_tensor(out=out, in0=in0, in1=in1, op=op), reads, writes)

        def CP(eng, out, in_, reads, writes):
            if eng == "act":
                P.op("act", lambda e: e.activation(out=out, in_=in_, func=AF.Copy), reads, writes)
            else:
                P.op(eng, lambda e: e.tensor_copy(out=out, in_=in_), reads, writes)

        def MS(eng, ap, val, writes):
            P.op(eng, lambda e: e.memset(ap, val), (), writes)

        def DMA(eng, out, in_, reads, writes, grp, waitall=False, **kw):
            P.op(eng, lambda e: e.dma_start(out=out, in_=in_, **kw), reads, writes, dma=grp, waitall=waitall)

        def rstd_from(ps_ap, pskey, scale, eps, n=T):
            r, rk = rsf()
            t, tk = tmpf()
            ACT(t[:, 0:n], ps_ap, AF.Sqrt, [pskey, "prm2"], [tk], scale=scale, bias=prm2[:, 8:9] if eps == 1e-6 else prm2[:, 9:10])
            P.op("dve", lambda e, o=r[:, 0:n], i_=t[:, 0:n]: e.reciprocal(out=o, in_=i_), [tk], [rk])
            return r, rk

        def chk(name):
            if stop == name:
                raise _Stop()

        def body():
            for dst, src, k in ((ident, c_ident, "ident"), (ones, c_ones, "ones"), (bones, c_bones, "bones"), (wind, c_wind, "wind"),
                                (cw, c_cw, "cw"), (dw, c_dw, "dw"), (gw, c_gw, "gw"), (sel, c_sel, "sel"), (aaug, c_aaug, "aaug"),
                                (fm, c_fm, "fm"), (prm, prm_d, "prm")):
                DMA("sp", dst[:], src, [], [k], "const", waitall=True)
            MS("pool", expo[:], -0.5, ["expo"])
            MS("dve", VS[:], 1.0, [("VS", i) for i in range(NKT)]); MS("dve", VW[:], 1.0, [("VW", i) for i in range(16)]); MS("dve", VC[:], 1.0, ["VC"])
            MS("dve", KC[:], 0.0, ["KC"]); MS("dve", VCT[:], 0.0, ["VCT"]); MS("dve", KW[:], 0.0, [("KW", i) for i in range(16)])
            MS("pool", QP[:], 0.0, [("QP", h) for h in range(8)])
            MS("pool", XX[:], 0.0, [("XX", i) for i in range(4)])
            MS("pool", ub[:], 0.0, [("ub", i) for i in range(4)])
            MS("pool", w2p[:], 0.0, ["w2p"])
            TS("dve", prm2[:, 0:1], prm[:, PO_NSA:PO_NSA + 1], 0.125, None, ALU.mult, None, ["prm"], ["prm2"])
            TS("dve", prm2[:, 1:2], prm[:, PO_MQK:PO_MQK + 1], 128.0 ** -0.5, None, ALU.mult, None, ["prm"], ["prm2"])
            TS("dve", prm2[:, 2:6], prm[:, PO_LG:PO_LG + 4], 0.5, None, ALU.mult, None, ["prm"], ["prm2"])
            TS("dve", prm2[:, 10:14], prm[:, PO_LB:PO_LB + 4], 0.5, None, ALU.mult, None, ["prm"], ["prm2"])
            MS("dve", prm2[:, 8:9], 1e-6, ["prm2"]); MS("dve", prm2[:, 9:10], 1e-5, ["prm2"])
            TS("dve", prm[:, PO_CW:PO_CW + 124], prm[:, PO_CW:PO_CW + 124], 0.5, None, ALU.mult, None, ["prm"], ["prm"])
            CP("dve", posb[:], prm[:, PO_POS:PO_POS + 32], ["prm"], ["posb"])

            stage32 = [memf[:].rearrange("p k n -> p (k n)"), xh2[:, 1, :, :].rearrange("p k n -> p (k n)"),
                       xh2[:, 0, :, :].rearrange("p k n -> p (k n)"), tf[:].rearrange("p k n -> p (k n)")]
            stage16 = [actb[:, 0:8, :].rearrange("p k n -> p (k n)"), actb[:, 8:16, :].rearrange("p k n -> p (k n)"),
                       a8x[:, 0, :, :].rearrange("p k n -> p (k n)"), a8x[:, 1, :, :].rearrange("p k n -> p (k n)")]
            k32 = [[("st32", 0)], [("st32", 1)], [("st32", 2), ("xh", 0)], [("st32", 3)] + [("tf", q) for q in range(8)]]
            k16 = [[("st16", 0)], [("st16", 1)], [("st16", 2)] + [("a8x", 0, q) for q in range(8)],
                   [("st16", 3)] + [("a8x", 1, q) for q in range(8)]]
            cst = [0]

            cast_list = []

            def cast_piece(dst_ap, src_ap, rows, cols, key, perm=False):
                cast_list.append((dst_ap, src_ap, rows, cols, key, perm))

            def cast_load(idx):
                dst_ap, src_ap, rows, cols, key, perm = cast_list[idx]
                i = idx % 4
                DMA("sp", stage32[i][0:rows, 0:cols], src_ap, [], k32[i], ("st32", i))

            def cast_cp_store(idx):
                dst_ap, src_ap, rows, cols, key, perm = cast_list[idx]
                i = idx % 4
                s32 = stage32[i][0:rows, 0:cols]
                s16 = stage16[i][0:rows, 0:cols]
                eng = ("dve", "act", "dve", "act", "pool")[idx % 5]
                if perm:
                    CP("dve", s16.rearrange("p (m hf d) -> p m hf d", m=4, hf=2), s32.rearrange("p (hf m d) -> p m hf d", m=4, hf=2),
                       k32[i], k16[i])
                else:
                    CP(eng, s16, s32, k32[i], k16[i])
                DMA("sp", dst_ap, s16, k16[i], [(key, i)], ("st16o", i))

            def cast_flush(depth=3):
                n = len(cast_list)
                for idx in range(min(depth, n)):
                    cast_load(idx)
                for idx in range(n):
                    cast_cp_store(idx)
                    if idx + depth < n:
                        cast_load(idx + depth)

            def cast2d(dst, src, R, key, c_lo=0, c_hi=None):
                C = src.shape[1] if c_hi is None else c_hi
                for r0 in range(0, R, 128):
                    r1 = min(R, r0 + 128)
                    for c0 in range(c_lo, C, 2048):
                        c1 = min(C, c0 + 2048)
                        cast_piece(dst[r0:r1, c0:c1], src[r0:r1, c0:c1], r1 - r0, c1 - c0, key)

            for r0 in range(0, 1024, 128):
                cast_piece(wA[r0:r0 + 128, 0:512], w_in[r0:r0 + 128, 0:512], 128, 512, "wA", perm=True)
            cast2d(wA, w_in, 1024, "wA", 512, IN_COLS)
            cast2d(wKV, w_mem_kv, 1024, "wKV")
            for kv in range(2):
                cast2d(w1b[kv], cmp_w1[kv], 2048, "w1b")
                cast2d(w2b[kv], cmp_w2[kv], 256, "w2b")
            for m in range(4):
                cast_piece(wN[m * 128:m * 128 + 64, :], w_nsa_out[m * 64:(m + 1) * 64, :], 64, 1024, "wN")
                cast_piece(wN[m * 128 + 64:m * 128 + 128, :], w_nsa_out[(m + 4) * 64:(m + 5) * 64, :], 64, 1024, "wN")
            cast2d(wC, w_conv_out, 512, "wC"); cast2d(wM, w_mem_out, 512, "wM")
            cast2d(wO, w_out, 1024, "wO")
            cast2d(wG, w_gate, 1024, "wG"); cast2d(wU, w_up, 1024, "wU"); cast2d(wD, w_down, DFF, "wD")
            cast_flush()
            chk("cast")

            rctr2 = [0]

            ring_n = [2]
            memb = memf[:].rearrange("p k n -> p (k n)").bitcast(BF16)

            def wload(src_ap, KC_, ncols, srckey):
                assert KC_ * ncols <= 4096
                i = rctr2[0] % ring_n[0]; rctr2[0] += 1
                if i < 2:
                    view = ring[:, i, 0:KC_ * ncols].rearrange("p (k n) -> p k n", k=KC_)
                    wk = [("ring", i)]
                else:
                    view = memb[:, 0:KC_ * ncols].rearrange("p (k n) -> p k n", k=KC_)
                    wk = [("ring", i), "memf", ("st32", 0)]
                DMA("sp", view, src_ap.rearrange("(k p) n -> p k n", p=128), [(srckey, q) for q in range(4)], wk, ("ring", i))
                return view, ("ring", i)

            def linear(wscr, wkey, KC_, col0, ncols, rhs_fn, rhs_keys, epi, piece=512, chunk=128):
                j = 0
                for c0 in range(col0, col0 + ncols, piece):
                    pc = min(piece, col0 + ncols - c0)
                    view, rk = wload(wscr[0:KC_ * 128, c0:c0 + pc], KC_, pc, wkey)
                    for cc0 in range(0, pc, chunk):
                        ps, pk = bank("pj")
                        for k in range(KC_):
                            MM(ps[:, 0:T], view[:, k, cc0:cc0 + chunk], rhs_fn(k), k == 0, k == KC_ - 1, [rk] + rhs_keys(k), [pk])
                        epi(j, ps, pk)
                        j += 1

            for kv in range(2):
                for ncn in range(2):
                    DMA("sp", tf[:, kv * 2 + ncn, 0:64], cmp_w2[kv, ncn * 128:(ncn + 1) * 128, :], [], [("tf", kv * 2 + ncn)], "w2f", waitall=True)
                    CP("dve", w2f[:, kv * 2 + ncn, :], tf[:, kv * 2 + ncn, 0:64], [("tf", kv * 2 + ncn)], ["w2f"])
            chk("s0")
            for kv in range(2):
                for g in range(2):
                    for ncn in range(2):
                        CP("dve", w2p[:, kv * 4 + g * 2 + ncn, g * 64:(g + 1) * 64], w2f[:, kv * 2 + ncn, :], ["w2f"], ["w2p"])
            chk("s1")
            for kv in range(2):
                psb1, pkb1 = bank("ms")
                for half in range(2):
                    view, rk = wload(w1b[kv, half * 1024:(half + 1) * 1024, :], 8, 256, "w1b")
                    for jp8 in range(8):
                        jp = half * 8 + jp8
                        for ncn in range(2):
                            MM(psb1[:, ncn:ncn + 1], view[:, jp8, ncn * 128:(ncn + 1) * 128], posb[:, kv * 16 + jp:kv * 16 + jp + 1],
                               jp == 0 and ncn == 0, jp == 15, [rk, "posb"], [pkb1], sgc=True)
                CP("dve", b1[:, kv * 2:kv * 2 + 2], psb1[:, 0:2], [pkb1], ["b1"])
            chk("s2")
            DMA("sp", memf[:], memT.rearrange("(k p) n -> p k n", p=128), [], ["memf", ("st32", 0)], "memf")
            psm, pkm = bank("ms")
            for k in range(8):
                s_, sk = sqf()
                ACT(s_, memf[:, k, :], AF.Square, ["memf"], [sk])
                MM(psm[:, 0:256], ones[:], s_, k == 0, k == 7, [sk, "ones"], [pkm])
            rm, rmk = rstd_from(psm[:, 0:256], pkm, 1.0 / 1024, 1e-6, 256)
            for k in range(8):
                STT(a8[:, k, :], memf[:, k, :], prm[:, PO_MEM + k:PO_MEM + k + 1], rm[:, 0:256], ALU.mult, ALU.mult,
                    ["memf", "prm", rmk], [("a8", k)])

            chk("s3")
            def epi_km(j, ps, pk):
                s_, sk = sqf()
                ACT(s_, ps[:, 0:256], AF.Square, [pk], [sk])
                ps2, pk2 = bank("ms")
                MM(ps2[:, 0:256], ones[:], s_, True, True, [sk, "ones"], [pk2])
                r, rk_ = rstd_from(ps2[:, 0:256], pk2, 1.0 / 128, 1e-6, 256)
                STT(KM[:, j, :], ps[:, 0:256], prm[:, PO_MQK + 1:PO_MQK + 2], r[:, 0:256], ALU.mult, ALU.mult, [pk, "prm", rk_], ["KM"])
            linear(wKV, "wKV", 8, 0, 512, lambda k: a8[:, k, :], lambda k: [("a8", k)], epi_km)
            chk("s4")
            for c0 in range(0, 512, 256):
                view, rk = wload(wKV[0:1024, 512 + c0:512 + c0 + 256], 8, 256, "wKV")
                for mc in range(2):
                    ps, pk = bank("pj")
                    for k in range(8):
                        MM(ps[:, 0:256], a8[:, k, mc * 128:(mc + 1) * 128], view[:, k, :], k == 0, k == 7, [rk, ("a8", k)], [pk])
                    CP("dve", VM[:, mc, c0:c0 + 256], ps[:, 0:256], [pk], ["VM"])

            chk("setup")
            ring_n[0] = 3
            def xload(tj):
                par_ = tj % 2
                wk = [("xh", par_)] + ([("st32", 1)] if par_ == 1 else [])
                DMA("sp", xh2[:, par_], xT[:, tj * T:(tj + 1) * T].rearrange("(k p) n -> p k n", p=128), [], wk, ("xh", par_))

            def xnorm(tj):
                par_ = tj % 2
                psn, pkn = bank("ms")
                for k in range(8):
                    s_, sk = sqf()
                    ACT(s_, xh2[:, par_, k, :], AF.Square, [("xh", par_)], [sk])
                    MM(psn[:, 0:T], ones[:], s_, k == 0, k == 7, [sk, "ones"], [pkn])
                rx, rxk = rstd_from(psn[:, 0:T], pkn, 1.0 / 1024, 1e-6)
                for k in range(8):
                    STT(a8x[:, par_, k, :], xh2[:, par_, k, :], prm[:, PO_MIX + k:PO_MIX + k + 1], rx, ALU.mult, ALU.mult,
                        [("xh", par_), "prm", rxk], [("a8x", par_, k)])
            xload(0)
            xnorm(0)
            for ti in range(NT):
                t0 = ti * T
                par = ti % 2
                xh = xh2[:, par]
                xhk = ("xh", par)
                xr = lambda k, par=par: a8x[:, par, k, :]
                xk = lambda k, par=par: [("a8x", par, k)]

                chk("A%d" % ti)
                def epi_q(j, ps, pk):
                    s_, sk = sqf()
                    ACT(s_, ps[:, 0:T], AF.Square, [pk], [sk])
                    ps2, pk2 = bank("ms")
                    MM(ps2[:, 0:T], bones[:], s_, True, True, [sk, "bones"], [pk2])
                    r, rk_ = rstd_from(ps2[:, 0:T], pk2, 1.0 / 64, 1e-6)
                    STT(QP[0:64, j, :], ps[0:64, 0:T], prm2[0:64, 0:1], r[0:64, :], ALU.mult, ALU.mult, [pk, "prm2", rk_], [("QP", j)])
                    STT(QP[64:128, j + 4, :], ps[64:128, 0:T], prm2[64:128, 0:1], r[64:128, :], ALU.mult, ALU.mult, [pk, "prm2", rk_], [("QP", j + 4)])
                linear(wA, "wA", 8, 0, 512, xr, xk, epi_q)

                CP("pool", XX[:, :, 0:16], XX[:, :, T:T + 16], [("XX", i) for i in range(4)], [("XX", i) for i in range(4)])

                def epi_kv(j, ps, pk):
                    if j < 2:
                        kv = j
                        CP("act", XX[0:64, kv * 2 + 0, 16:16 + T], ps[0:64, 0:T], [pk], [("XX", kv * 2)])
                        CP("dve", XX[64:128, kv * 2 + 0, 15:15 + T], ps[0:64, 0:T], [pk], [("XX", kv * 2)])
                        CP("dve", XX[0:64, kv * 2 + 1, 16:16 + T], ps[64:128, 0:T], [pk], [("XX", kv * 2 + 1)])
                        CP("act", XX[64:128, kv * 2 + 1, 15:15 + T], ps[64:128, 0:T], [pk], [("XX", kv * 2 + 1)])
                    else:
                        s_, sk = sqf()
                        ACT(s_, ps[:, 0:T], AF.Square, [pk], [sk])
                        ps2, pk2 = bank("ms")
                        MM(ps2[:, 0:T], bones[:], s_, True, True, [sk, "bones"], [pk2])
                        r, rk_ = rstd_from(ps2[:, 0:T], pk2, 1.0 / 64, 1e-6)
                        if j == 2:
                            STT(KS[:, t0:t0 + T], ps[:, 0:T], prm[:, PO_NSA + 2:PO_NSA + 3], r, ALU.mult, ALU.mult, [pk, "prm", rk_], [("KS", 2 * ti), ("KS", 2 * ti + 1)])
                        else:
                            for s2 in range(2):
                                slot = (2 * ti + s2) % 16
                                STT(KW[:, slot, :], ps[:, s2 * 128:(s2 + 1) * 128], prm[:, PO_NSA + 3:PO_NSA + 4], r[:, s2 * 128:(s2 + 1) * 128],
                                    ALU.mult, ALU.mult, [pk, "prm", rk_], [("KW", slot)])
                viewA, rkA = wload(wA[0:1024, 512:1024], 8, 512, "wA")
                viewB, rkB = wload(wA[0:1024, 1024:1408], 8, 384, "wA")

                def fm_chunk(view, rk, c, epi, j):
                    ps, pk = bank("pj")
                    for k in range(8):
                        MM(ps[:, 0:T], view[:, k, c:c + 128], xr(k), k == 0, k == 7, [rk] + xk(k), [pk])
                    epi(j, ps, pk)

                def vtok(view, rk, c, isw):
                    for s2 in range(2):
                        ps, pk = bank("pj")
                        for k in range(8):
                            MM(ps[:, 0:128], a8x[:, par, k, s2 * 128:(s2 + 1) * 128], view[:, k, c:c + 128], k == 0, k == 7, [rk, ("a8x", par, k)], [pk])
                        kt = 2 * ti + s2
                        if isw:
                            CP("dve", VW[:, kt % 16, 0:64], ps[:, 0:64], [pk], [("VW", kt % 16)])
                            CP("act", VW[:, kt % 16, 128:192], ps[:, 64:128], [pk], [("VW", kt % 16)])
                        else:
                            CP("dve", VS[:, kt, 0:64], ps[:, 0:64], [pk], [("VS", kt)])
                            CP("act", VS[:, kt, 128:192], ps[:, 64:128], [pk], [("VS", kt)])
                fm_chunk(viewA, rkA, 0, epi_kv, 0)
                fm_chunk(viewA, rkA, 128, epi_kv, 1)
                fm_chunk(viewA, rkA, 256, epi_kv, 2)
                vtok(viewA, rkA, 384, False)
                fm_chunk(viewB, rkB, 0, epi_kv, 3)
                vtok(viewB, rkB, 128, True)

                def epi_g(j, ps, pk):
                    t, tk = tmpf()
                    ACT(t, ps[:, 0:T], AF.Tanh, [pk], [tk], scale=0.5)
                    TS("dve", sg[:], t, 0.5, 0.5, ALU.mult, ALU.add, [tk], ["sg"])
                fm_chunk(viewB, rkB, 256, epi_g, 0)

                psh, pkh = bank("ms")
                for kv in range(2):
                    for half in range(1):
                        view, rk = wload(w1b[kv, :, :], 16, 256, "w1b")
                        for g in range(2):
                            for ncn in range(2):
                                col = ((kv * 2 + g) * 2 + ncn) * 16
                                for jp8 in range(16):
                                    jp = jp8
                                    MM(psh[:, col:col + 16], view[:, jp8, ncn * 128:(ncn + 1) * 128],
                                       XX[:, kv * 2 + g, 2 * jp:2 * jp + T - 15:16], jp == 0 and col == 0, jp == 15, [rk, ("XX", kv * 2 + g)], [pkh], sgc=True)
                hv = psh[:, 0:128].rearrange("p (a g n c) -> p a g n c", a=2, g=2, n=2)
                hf = hidf[:].rearrange("p (a g n) c -> p a g n c", a=2, g=2)
                for kv in range(2):
                    for ncn in range(2):
                        ACT(hf[:, kv, :, ncn, :], hv[:, kv, :, ncn, :], AF.Identity, [pkh, "b1"], ["hidf"], bias=b1[:, kv * 2 + ncn:kv * 2 + ncn + 1])
                lo = 1 if ti == 0 else 0
                cb = 16 * ti - 1

                def cstep_B():
                    TT("dve", hidg[:], hidf[:], hidf[:], ALU.mult, ["hidf"], ["hidg"])
                    TS("dve", hidg[:], hidg[:], 0.044715, 1.0, ALU.mult, ALU.add, ["hidg"], ["hidg"])
                    TT("dve", hidg[:], hidg[:], hidf[:], ALU.mult, ["hidg", "hidf"], ["hidg"])

                def cstep_C():
                    ACT(hidg[:], hidg[:], AF.Tanh, ["hidg"], ["hidg"], scale=0.7978845608028654)
                    STT(hidg[:], hidg[:], 1.0, hidf[:], ALU.add, ALU.mult, ["hidg", "hidf"], ["hidg"])
                    TS("dve", hidb[:], hidg[:], 0.5, None, ALU.mult, None, ["hidg"], ["hidb"])

                def cstep_kv(kv):
                    psc, pkc = bank("ms")
                    n = 0
                    for g in range(2):
                        for ncn in range(2):
                            MM(psc[:, 0:16], w2p[:, kv * 4 + g * 2 + ncn, :], hidb[:, (kv * 2 + g) * 2 + ncn, :], n == 0, n == 3, ["w2p", "hidb"], [pkc])
                            n += 1
                    if kv == 0:
                        ACT(kcsq[:], psc[:, 0:16], AF.Square, [pkc], ["kcsq"])
                        ps2, pk2 = bank("ms")
                        MM(ps2[:, 0:16], bones[:], kcsq[:], True, True, ["kcsq", "bones"], [pk2])
                        r, rk_ = rstd_from(ps2[:, 0:16], pk2, 1.0 / 64, 1e-6, 16)
                        STT(KC[:, cb + lo:cb + 16], psc[:, lo:16], prm[:, PO_NSA + 1:PO_NSA + 2], r[:, lo:16], ALU.mult, ALU.mult, [pkc, "prm", rk_], ["KC"])
                    else:
                        CP("dve", VCT[:, cb + lo:cb + 16], psc[:, lo:16], [pkc], ["VCT"])

                def cstep_F():
                    for ccn in sorted({max(0, 16 * ti - 1) // 128, (16 * ti + 14) // 128}):
                        pst, pkt = bank("ms")
                        ptv = pst[:].bitcast(BF16)[:, 0:128]
                        P.op("pe", lambda e, o=ptv, i=VCT[:, ccn * 128:(ccn + 1) * 128]: e.transpose(out=o, in_=i, identity=ident[:]), ["VCT", "ident"], [pkt])
                        CP("dve", VC[:, ccn, 0:64], ptv[:, 0:64], [pkt], ["VC"])
                        CP("act", VC[:, ccn, 128:192], ptv[:, 64:128], [pkt], ["VC"])
                csteps = [cstep_B, cstep_C, lambda: cstep_kv(0), lambda: cstep_kv(1), cstep_F]

                csteps[0]()

                CP("pool", ub[:, :, 0:30], ub[:, :, T:T + 30], [("ub", i) for i in range(4)], [("ub", i) for i in range(4)])
                tg = []

                def epi_ag(j, ps, pk):
                    t, tk = tmpf()
                    ACT(t, ps[:, 0:T], AF.Tanh, [pk], [tk], scale=0.5)
                    tg.append((t, tk))
                linear(wA, "wA", 8, 1816, 512, xr, xk, epi_ag)
                csteps[1]()

                def epi_a(j, ps, pk):
                    t, tk = tg[j]
                    STT(ub[:, j, 30:30 + T], t, 1.0, ps[:, 0:T], ALU.add, ALU.mult, [tk, pk], [("ub", j)])
                linear(wA, "wA", 8, 1304, 512, xr, xk, epi_a)
                csteps[2]()

                def epi_qm(j, ps, pk):
                    s_, sk = sqf()
                    ACT(s_, ps[:, 0:T], AF.Square, [pk], [sk])
                    ps2, pk2 = bank("ms")
                    MM(ps2[:, 0:T], ones[:], s_, True, True, [sk, "ones"], [pk2])
                    r, rk_ = rstd_from(ps2[:, 0:T], pk2, 1.0 / 128, 1e-6)
                    STT(QM[:, j, :], ps[:, 0:T], prm2[:, 1:2], r, ALU.mult, ALU.mult, [pk, "prm2", rk_], [("QM", j)])
                linear(wA, "wA", 8, 2328, 512, xr, xk, epi_qm)
                csteps[3]()
                csteps[4]()

                chk("B%d" % ti)
                chk("C%d" % ti)
                def attn_stream(h, tiles, bkey):
                    po, pko = bank("oa")
                    n = len(tiles)
                    groups = [tiles[i:i + 2] for i in range(0, n, 2)]

                    def issue_qk(grp):
                        ps, pk = bank("st")
                        for gi, (kl, kk, masks, vl, vk, extra) in enumerate(grp):
                            o = ps[:, gi * T:(gi + 1) * T]
                            MM(o, kl, QP[:, h, :], True, len(masks) == 0, kk + [("QP", h)], [pk])
                            for mi, (ml, mr, mk) in enumerate(masks):
                                MM(o, ml, mr, False, mi == len(masks) - 1, mk, [pk])
                        return ps, pk
                    nxt = issue_qk(groups[0])
                    cnt = 0
                    for gidx, grp in enumerate(groups):
                        ps, pk = nxt
                        if gidx + 1 < len(groups):
                            nxt = issue_qk(groups[gidx + 1])
                        pb, pbk = pbf()
                        w = len(grp) * T
                        ACT(pb[:, 0:w], ps[:, 0:w], AF.Exp, [pk], [pbk])
                        for gi, (kl, kk, masks, vl, vk, extra) in enumerate(grp):
                            ph = pb[:, gi * T:(gi + 1) * T]
                            MM(po[:, 0:T], vl, ph, cnt == 0, cnt == n - 1, vk + [pbk], [pko])
                            if extra is not None:
                                extra(ph, pbk, cnt, n)
                            cnt += 1
                    return po, pko

                onsa_init = set()

                def finish(h, br, po, pko):
                    m = h % 4
                    lo_ = h < 4
                    nr = slice(0, 64) if lo_ else slice(64, 128)
                    dr = slice(64, 128) if lo_ else slice(0, 64)
                    t, tk = tmpf()
                    if ti == 0 and br == 0:
                        TS("dve", t[nr, :], po[dr, 0:T], 1e-30, None, ALU.add, None, [pko], [tk])
                        P.op("dve", lambda e, o=t[nr, :]: e.reciprocal(out=o, in_=o), [tk], [tk])
                    else:
                        P.op("dve", lambda e, o=t[nr, :], i_=po[dr, 0:T]: e.reciprocal(out=o, in_=i_), [pko], [tk])
                    pg, pkg = bank("ms")
                    MM(pg[:, 0:T], sel[:, m * 3 + br, :], sg[:], True, True, ["sel", "sg"], [pkg])
                    TT("dve", t[nr, :], t[nr, :], pg[nr, 0:T], ALU.mult, [tk, pkg], [tk])
                    if h not in onsa_init:
                        onsa_init.add(h)
                        TT("dve", onsa[nr, m, :], po[nr, 0:T], t[nr, :], ALU.mult, [pko, tk], [("onsa", m, lo_)])
                    else:
                        t2, tk2 = tmpf()
                        TT("dve", t2[nr, :], po[nr, 0:T], t[nr, :], ALU.mult, [pko, tk], [tk2])
                        TT("pool", onsa[nr, m, :], onsa[nr, m, :], t2[nr, :], ALU.add, [tk2, ("onsa", m, lo_)], [("onsa", m, lo_)])

                def vsel(arr, idx, h):
                    return arr[:, idx, 0:128] if h < 4 else arr[:, idx, 64:192]

                mean, mk_ = lnb[:, 0, :], ("lnb", 0)
                var, vk_ = lnb[:, 1, :], ("lnb", 1)
                rl, rlk = lnb[:, 2, :], ("lnb", 2)
                psE = {}

                def conv_tap(c4, kk):
                    acc, ak = cacc[:, c4, :], ("cacc", c4)
                    wcol = prm[:, PO_CW + c4 * 31 + kk:PO_CW + c4 * 31 + kk + 1]
                    if kk == 0:
                        TS("dve", acc, ub[:, c4, 0:T], wcol, prm[:, PO_CB + c4:PO_CB + c4 + 1], ALU.mult, ALU.add, [("ub", c4), "prm"], [ak])
                    else:
                        STT(acc, ub[:, c4, kk:kk + T], wcol, acc, ALU.mult, ALU.add, [("ub", c4), "prm", ak], [ak])

                def conv_post(c4):
                    acc, ak = cacc[:, c4, :], ("cacc", c4)
                    CP("act", cu[:, c4, :], acc, [ak], [("cu", c4)])
                    ACT(actb[:, 16 + c4, :], acc, AF.Square, [ak], [("actb", 16 + c4)])
                conv_q = []
                for c4_ in range(4):
                    for kk_ in range(31):
                        conv_q.append(lambda c4=c4_, kk=kk_: conv_tap(c4, kk))
                    conv_q.append(("post", c4_))
                conv_def = []

                def conv_drain(k):
                    while conv_def:
                        conv_post(conv_def.pop(0))
                    for _ in range(min(k, len(conv_q))):
                        it = conv_q.pop(0)
                        if isinstance(it, tuple):
                            conv_def.append(it[1])
                        else:
                            it()
                _nw = len(range(max(0, 2 * ti - 4), 2 * ti + 2))
                _ncp = (16 * ti + 14) // 128 + 1
                _ns = 2 * ti + 2
                _units = 4 * _ns
                _done = [0, 0]

                def conv_share(n):
                    _done[0] += n
                    tgt = (128 * _done[0] + _units - 1) // _units
                    conv_drain(tgt - _done[1]); _done[1] = tgt

                def conv_E2():
                    psE1, pkE1 = bank("ms")
                    psE2, pkE2 = bank("ms")
                    for c4 in range(4):
                        MM(psE1[:, 0:T], ones[:], cu[:, c4, :], c4 == 0, c4 == 3, [("cu", c4), "ones"], [pkE1])
                    for c4 in range(4):
                        MM(psE2[:, 0:T], ones[:], actb[:, 16 + c4, :], c4 == 0, c4 == 3, [("actb", 16 + c4), "ones"], [pkE2])
                    ACT(mean, psE1[:, 0:T], AF.Copy, [pkE1], [mk_], scale=1.0 / 512)
                    STT(var, mean, -1.0, mean, ALU.mult, ALU.mult, [mk_], [vk_])
                    STT(var, psE2[:, 0:T], 1.0 / 512, var, ALU.mult, ALU.add, [pkE2, vk_], [vk_])
                    ACT(var, var, AF.Sqrt, [vk_, "prm2"], [vk_], bias=prm2[:, 9:10])
                    P.op("dve", lambda e, o=rl, i_=var: e.reciprocal(out=o, in_=i_), [vk_], [rlk])

                def conv_E3(c4):
                    acc, ak = cacc[:, c4, :], ("cacc", c4)
                    TT("dve", acc, acc, mean, ALU.subtract, [ak, mk_], [ak])
                    TT("dve", acc, acc, rl, ALU.mult, [ak, rlk], [ak])
                    t, tk = tmpf()
                    ACT(t, acc, AF.Tanh, [ak, "prm2"], [tk], scale=prm2[:, 2 + c4:3 + c4], bias=prm2[:, 10 + c4:11 + c4])
                    TS("dve", acc, acc, prm2[:, 2 + c4:3 + c4], prm2[:, 10 + c4:11 + c4], ALU.mult, ALU.add, [ak, "prm2"], [ak])
                    STT(cu[:, c4, :], t, 1.0, acc, ALU.add, ALU.mult, [tk, ak], [("cu", c4)])

                chk("Dc%d" % ti)
                def w_job(h):
                    tiles = []
                    for kt in range(max(0, 2 * ti - 4), 2 * ti + 2):
                        r_ = kt - 2 * ti
                        masks = []
                        if r_ >= 0:
                            masks = [(ident[:], cw[:, 128 - 128 * r_:128 - 128 * r_ + T], ["ident", "cw"])]
                        elif r_ == -3:
                            masks = [(ident[:], dw[:, 0:T], ["ident", "dw"])]
                        elif r_ == -4:
                            masks = [(ident[:], dw[:, 128:128 + T], ["ident", "dw"])]
                        tiles.append((KW[:, kt % 16, :], [("KW", kt % 16)], masks, vsel(VW, kt % 16, h), [("VW", kt % 16)], None))
                    po, pko = yield (h, tiles)
                    finish(h, 2, po, pko)
                    if h >= 8:
                        conv_share(_nw)
                ncc_t = (16 * ti + 14) // 128 + 1

                def c_job(h):
                    g = h // 4
                    pu_box = {}

                    def extra_u(pb, pbk, i, n):
                        if i == 0:
                            pu, pku_ = bank("ms")
                            pu_box["uv"] = pu[:].rearrange("p (s c) -> p s c", s=2)
                            pu_box["k"] = pku_
                        uv, pku = pu_box["uv"], pu_box["k"]
                        for s2 in range(2):
                            MM(uv[:, s2, 0:129], pb[:, s2 * 128:(s2 + 1) * 128], aaug[:, i, :], i == 0 and s2 == 0, i == n - 1, [pbk, "aaug"], [pku], sgc=True)
                    tiles = []
                    for cc in range(ncc_t):
                        s_ = ti - 8 * cc
                        masks = [(ident[:], gw[:, 256 * s_:256 * s_ + T], ["ident", "gw"])] if s_ <= 8 else []
                        tiles.append((KC[:, cc * 128:(cc + 1) * 128], ["KC"], masks, vsel(VC, cc, h), ["VC"], extra_u))
                    po, pko = yield (h, tiles)
                    uv, pku = pu_box["uv"], pu_box["k"]
                    finish(h, 0, po, pko)
                    TS("dve", rden[:], uv[:, :, 128], 1e-30, None, ALU.add, None, [pku], ["rden"])
                    P.op("dve", lambda e: e.reciprocal(out=rden[:], in_=rden[:]), ["rden"], ["rden"])
                    for s2 in range(2):
                        if h % 4 == 0:
                            TS("dve", imp[:, g, s2, :], uv[:, s2, 0:128], rden[:, s2:s2 + 1], None, ALU.mult, None, [pku, "rden"], [("imp", g)])
                        else:
                            STT(imp[:, g, s2, :], uv[:, s2, 0:128], rden[:, s2:s2 + 1], imp[:, g, s2, :], ALU.mult, ALU.add,
                                [pku, "rden", ("imp", g)], [("imp", g)])
                    if h % 4 == 3:
                        for s2 in range(2):
                            B = 4 * ti + 2 * s2
                            TT("dve", impf[:, 0, :], imp[:, g, s2, :], fm[:, 128 - B:256 - B], ALU.add, [("imp", g), "fm"], ["impf"])
                            MS("dve", impf[:, 0, 0:1], 2e6, ["impf"])
                            P.op("dve", lambda e: e.max(out=m8[:, 0:8], in_=impf[:, 0, :]), ["impf"], ["m8"])
                            P.op("dve", lambda e: e.match_replace(out=impf[:, 1, :], in_to_replace=m8[:, 0:8], in_values=impf[:, 0, :],
                                                                  imm_value=-3.0e38), ["impf", "m8"], ["impf2"])
                            P.op("dve", lambda e: e.max(out=m8[:, 8:16], in_=impf[:, 1, :]), ["impf2"], ["m8"])
                            TS("dve", negq[:, s2, :], impf[:, 0, :], m8[:, 15:16], NEG, ALU.is_lt, ALU.mult, ["impf", "m8"], [("negq", s2)])
                            pt2, pkt2 = bank("ms")
                            pv = pt2[:].bitcast(BF16)[:, 0:128]
                            P.op("pe", lambda e, o=pv, i_=negq[:, s2, :]: e.transpose(out=o, in_=i_, identity=ident[:]), [("negq", s2), "ident"], [pkt2])
                            CP("act", NEGT[:, g, s2 * 128:(s2 + 1) * 128], pv, [pkt2], [("NEGT", g)])
                    if h >= 8:
                        conv_share(_ncp)
                chk("Dw%d" % ti)
                def s_job(h):
                    g = h // 4
                    tiles = []
                    for kt in range(0, 2 * ti + 2):
                        r_ = kt - 2 * ti
                        masks = [(wind[:, kt * 128:(kt + 1) * 128], NEGT[:, g, :], ["wind", ("NEGT", g)])]
                        if r_ >= 0:
                            masks.append((ident[:], cw[:, 128 - 128 * r_:128 - 128 * r_ + T], ["ident", "cw"]))
                        tiles.append((KS[:, kt * 128:(kt + 1) * 128], [("KS", kt)], masks, vsel(VS, kt, h), [("VS", kt)], None))
                    po, pko = yield (h, tiles)
                    finish(h, 1, po, pko)

                jobs = [("C", 0), ("W", 0), ("C", 1), ("W", 1), ("C", 2), ("W", 2), ("C", 3), ("W", 3),
                        ("C", 4), ("W", 4), ("C", 5), ("W", 5), ("C", 6), ("W", 6), ("C", 7), ("W", 7),
                        ("S", 0), ("S", 1), ("S", 2), ("S", 3), ("S", 4), ("S", 5), ("S", 6), ("S", 7)]
                nS = [0]

                def job_gen(kind, h):
                    if kind == "W":
                        yield from w_job(h)
                    elif kind == "C":
                        yield from c_job(h)
                    else:
                        yield from s_job(h)
                        nS[0] += 1
                        if nS[0] <= 3:
                            conv_share(_ns)
                        elif nS[0] == 4:
                            conv_drain(1000)
                        elif nS[0] == 5:
                            conv_drain(0)
                            conv_E2()
                        elif nS[0] == 6:
                            conv_E3(0); conv_E3(1)
                        elif nS[0] == 7:
                            conv_E3(2); conv_E3(3)

                reqs = []
                for kind, h in jobs:
                    g_ = job_gen(kind, h)
                    h_, tiles_ = next(g_)
                    reqs.append({"h": h_, "tiles": tiles_, "gen": g_, "po": None, "cnt": 0, "n": len(tiles_)})
                seq = []
                for ji, rq in enumerate(reqs):
                    for i0 in range(0, rq["n"], 2):
                        seq.append((ji, rq["tiles"][i0:i0 + 2]))

                def issue_qk2(ji, grp):
                    hq = reqs[ji]["h"]
                    ps, pk = bank("st")
                    for gi, (kl, kk, masks, vl, vk, extra) in enumerate(grp):
                        o = ps[:, gi * T:(gi + 1) * T]
                        MM(o, kl, QP[:, hq, :], True, len(masks) == 0, kk + [("QP", hq)], [pk])
                        for mi, (ml, mr, mk) in enumerate(masks):
                            MM(o, ml, mr, False, mi == len(masks) - 1, mk, [pk])
                    return ps, pk
                nxt = issue_qk2(*seq[0])
                for si, (ji, grp) in enumerate(seq):
                    ps, pk = nxt
                    if si + 1 < len(seq):
                        nxt = issue_qk2(*seq[si + 1])
                    rq = reqs[ji]
                    pb, pbk = pbf()
                    w = len(grp) * T
                    ACT(pb[:, 0:w], ps[:, 0:w], AF.Exp, [pk], [pbk])
                    if rq["po"] is None:
                        rq["po"], rq["pko"] = bank("oa")
                    for gi, (kl, kk, masks, vl, vk, extra) in enumerate(grp):
                        ph = pb[:, gi * T:(gi + 1) * T]
                        MM(rq["po"][:, 0:T], vl, ph, rq["cnt"] == 0, rq["cnt"] == rq["n"] - 1, vk + [pbk], [rq["pko"]])
                        if extra is not None:
                            extra(ph, pbk, rq["cnt"], rq["n"])
                        rq["cnt"] += 1
                    if rq["cnt"] == rq["n"]:
                        try:
                            rq["gen"].send((rq["po"], rq["pko"]))
                        except StopIteration:
                            pass
                for m in range(4):
                    CP("pool", onsab[:, m, :], onsa[:, m, :], [("onsa", m, True), ("onsa", m, False)], [("onsab", m)])
                chk("E%d" % ti)
                for hd in range(4):
                    po, pko = bank("oa")
                    pd, pkd = bank("ms")
                    for mc in range(2):
                        ps, pk = bank("st")
                        MM(ps[:, 0:T], KM[:, hd, mc * 128:(mc + 1) * 128], QM[:, hd, :], True, True, ["KM", ("QM", hd)], [pk])
                        pb, pbk = pbf()
                        ACT(pb[:, 0:T], ps[:, 0:T], AF.Exp, [pk], [pbk])
                        MM(po[:, 0:T], VM[:, mc, hd * 128:(hd + 1) * 128], pb[:, 0:T], mc == 0, mc == 1, ["VM", pbk], [pko])
                        MM(pd[:, 0:T], ones[:], pb[:, 0:T], mc == 0, mc == 1, ["ones", pbk], [pkd])
                    t, tk = tmpf()
                    P.op("dve", lambda e, o=t, i_=pd[:, 0:T]: e.reciprocal(out=o, in_=i_), [pkd], [tk])
                    TT("dve", om[:, hd, :], po[:, 0:T], t, ALU.mult, [pko, tk], [("om", hd)])

                chk("F%d" % ti)
                if ti + 1 < NT:
                    xload(ti + 1)
                br_src = [(wN, "wN", lambda k: onsab[:, k, :], lambda k: [("onsab", k)]),
                          (wC, "wC", lambda k: cu[:, k, :], lambda k: [("cu", k)]),
                          (wM, "wM", lambda k: om[:, k, :], lambda k: [("om", k)])]
                for half in range(2):
                    mg = {}
                    for b in range(3):
                        gts = []

                        def epi_gm(j, ps, pk, gts=gts):
                            t, tk = tmpf()
                            ACT(t, ps[:, 0:T], AF.Tanh, [pk], [tk], scale=0.5)
                            gts.append((t, tk))
                        linear(wA, "wA", 8, 2840 + b * 1024 + half * 512, 512, xr, xk, epi_gm)
                        wsc, wk_, rf, rkf = br_src[b]

                        def epi_br(j, ps, pk, gts=gts, b=b):
                            t, tk = gts[j]
                            oc = half * 4 + j
                            if b == 0:
                                STT(mgb[:, j, :], t, 1.0, ps[:, 0:T], ALU.add, ALU.mult, [tk, pk], [("mgb", j)])
                            else:
                                STT(t, t, 1.0, ps[:, 0:T], ALU.add, ALU.mult, [tk, pk], [tk])
                                if b == 1:
                                    TT("pool", mgb[:, j, :], mgb[:, j, :], t, ALU.add, [("mgb", j), tk], [("mgb", j)])
                                else:
                                    TT("pool", actb[:, oc, :], mgb[:, j, :], t, ALU.add, [("mgb", j), tk], [("actb", oc), ("st16", 0), ("st16", 1)] if ti == 0 else [("actb", oc)])
                        linear(wsc, wk_, 4, half * 512, 512, rf, rkf, epi_br, piece=512)
                def epi_out(j, ps, pk):
                    STT(xh[:, j, :], ps[:, 0:T], 0.5, xh[:, j, :], ALU.mult, ALU.add, [pk, xhk], [xhk])
                linear(wO, "wO", 8, 0, 1024, lambda k: actb[:, k, :], lambda k: [("actb", k)], epi_out)

                chk("G%d" % ti)
                if ti + 1 < NT:
                    xnorm(ti + 1)
                psn, pkn = bank("ms")
                for k in range(8):
                    s_, sk = sqf()
                    ACT(s_, xh[:, k, :], AF.Square, [xhk], [sk])
                    MM(psn[:, 0:T], ones[:], s_, k == 0, k == 7, [sk, "ones"], [pkn])
                rh, rhk = rstd_from(psn[:, 0:T], pkn, 1.0 / 1024, 1e-6)
                for k in range(8):
                    STT(a8[:, k, :], xh[:, k, :], prm[:, PO_FFN + k:PO_FFN + k + 1], rh, ALU.mult, ALU.mult, [xhk, "prm", rhk], [("a8", k)])
                hr = lambda k: a8[:, k, :]
                hk = lambda k: [("a8", k)]
                for c0 in range(0, DFF, 512):
                    pc_ = min(512, DFF - c0)
                    gl = []

                    def epi_gate(j, ps, pk, gl=gl):
                        t, tk = tmpf()
                        ACT(t, ps[:, 0:T], AF.Tanh, [pk], [tk], scale=0.5)
                        STT(t, t, 1.0, ps[:, 0:T], ALU.add, ALU.mult, [tk, pk], [tk])
                        gl.append((t, tk))
                    linear(wG, "wG", 8, c0, pc_, hr, hk, epi_gate)

                    def epi_up(j, ps, pk, gl=gl, c0=c0):
                        t, tk = gl[j]
                        f = c0 // 128 + j
                        TT("dve", actb[:, f, :], t, ps[:, 0:T], ALU.mult, [tk, pk], [("actb", f)])
                    linear(wU, "wU", 8, c0, pc_, hr, hk, epi_up)
                for half in range(2):
                    bks = [bank("pj") for _ in range(4)]
                    for (r0, kc_) in ((0, 8), (1024, 8), (2048, 6)):
                        view, rk = wload(wD[r0:r0 + kc_ * 128, half * 512:(half + 1) * 512], kc_, 512, "wD")
                        for j in range(4):
                            ps, pk = bks[j]
                            for k in range(kc_):
                                f = r0 // 128 + k
                                MM(ps[:, 0:T], view[:, k, j * 128:(j + 1) * 128], actb[:, f, :], f == 0, f == 21, [rk, ("actb", f)], [pk])
                    for j in range(4):
                        oc = half * 4 + j
                        ps, pk = bks[j]
                        STT(xh[:, oc, :], ps[:, 0:T], 0.5, xh[:, oc, :], ALU.mult, ALU.add, [pk, xhk], [xhk])
                DMA("act", outT[:, t0:t0 + T].rearrange("(k p) n -> p k n", p=128), xh, [xhk], [], ("out", par))

        try:
            body()
        except _Stop:
            pass
        P.emit(nc, st)
    return nc


_CACHE = {}


def kernel(**inputs):
    x = np.asarray(inputs["x"], np.float32)
    B, S, D = x.shape
    mem = np.asarray(inputs["mem"], np.float32)
    if S not in _CACHE:
        _CACHE[S] = build(S)
    nc = _CACHE[S]
    consts = make_consts(S)
    prm = pack_params({k: np.asarray(v, np.float32) for k, v in inputs.items()})
    shared = {
        "w_in": np.ascontiguousarray(inputs["w_in"][0], np.float32), "w_gate": np.ascontiguousarray(inputs["w_gate"][0], np.float32),
        "w_up": np.ascontiguousarray(inputs["w_up"][0], np.float32), "w_down": np.ascontiguousarray(inputs["w_down"][0], np.float32),
        "w_out": np.ascontiguousarray(inputs["w_out"][0], np.float32), "w_nsa_out": np.ascontiguousarray(inputs["w_nsa_out"][0], np.float32),
        "w_conv_out": np.ascontiguousarray(inputs["w_conv_out"][0], np.float32), "w_mem_out": np.ascontiguousarray(inputs["w_mem_out"][0], np.float32),
        "w_mem_kv": np.ascontiguousarray(inputs["w_mem_kv"][0], np.float32), "cmp_w1": np.ascontiguousarray(inputs["cmp_w1"][0], np.float32),
        "cmp_w2": np.ascontiguousarray(inputs["cmp_w2"][0], np.float32), "prm": prm,
    }
    shared.update(consts)
    in_maps = []
    for b in range(B):
        m = dict(shared)
        m["xT"] = np.ascontiguousarray(x[b].T)
        m["memT"] = np.ascontiguousarray(mem[b].T)
        in_maps.append(m)
    res = run_bass_kernel_spmd(nc, in_maps, core_ids=list(range(B)))
    out = np.empty((B, S, D), np.float32)
    for b in range(B):
        out[b] = res.results[b]["outT"].T
    return out
```

```python
import numpy as np
import ml_dtypes
from contextlib import ExitStack
import concourse.bass as bass
import concourse.mybir as mybir
from concourse.bass_utils import run_bass_kernel_spmd

F32 = mybir.dt.float32
BF16 = mybir.dt.bfloat16
AF = mybir.ActivationFunctionType
ALU = mybir.AluOpType
ENGS = ("pe", "act", "dve", "pool", "sp")
T = 256
NEG = -30000.0
IN_COLS = 5912
DFF = 2816


class Op:
    __slots__ = ("eng", "fn", "reads", "writes", "dma", "deps", "flag", "ordinal", "dval", "idx", "waitall")

    def __init__(self, eng, fn, reads, writes, dma, idx):
        self.eng = eng; self.fn = fn; self.reads = reads; self.writes = writes
        self.dma = dma; self.deps = []; self.flag = False; self.ordinal = 0
        self.dval = 0; self.idx = idx; self.waitall = False


class Prog:
    def __init__(self):
        self.ops = []; self.last_w = {}; self.readers = {}; self.dma_groups = {}

    def op(self, eng, fn, reads=(), writes=(), dma=None, waitall=False):
        o = Op(eng, fn, tuple(reads), tuple(writes), dma, len(self.ops))
        o.waitall = waitall
        deps = {}
        wset = set(o.writes)
        for k in o.reads:
            d = self.last_w.get(k)
            if d is not None:
                deps[d.idx] = (d, True)
        for k in o.writes:
            d = self.last_w.get(k)
            if d is not None and d.idx not in deps:
                deps[d.idx] = (d, False)
            for r in self.readers.get(k, ()):
                if r.idx not in deps:
                    deps[r.idx] = (r, False)
        for d, raw in deps.values():
            if d.dma is None and o.dma is None and d.eng == o.eng:
                if o.eng == "pe":
                    continue
            o.deps.append(d)
        for k in o.reads:
            if k not in wset:
                self.readers.setdefault(k, []).append(o)
        for k in o.writes:
            self.last_w[k] = o
            self.readers[k] = []
        if dma is not None:
            g = self.dma_groups.setdefault(dma, [0])
            g[0] += 1
            o.dval = 16 * g[0]
        self.ops.append(o)
        return o

    def emit(self, nc, stack):
        for o in self.ops:
            for d in o.deps:
                if d.dma is None:
                    d.flag = True
        cnt = {e: 0 for e in ENGS}
        for o in self.ops:
            if o.dma is None and o.flag:
                cnt[o.eng] += 1
                o.ordinal = cnt[o.eng]
        esem = {e: stack.enter_context(nc.semaphore("s_" + e)) for e in ENGS}
        dsem = {}
        for g in self.dma_groups:
            dsem[g] = stack.enter_context(nc.semaphore("d_%d" % len(dsem)))
        per = {e: [] for e in ENGS}
        for o in self.ops:
            per[o.eng].append(o)
        groups = self.dma_groups

        def run(engname, eng):
            seen = {}
            for o in per[engname]:
                need = {}
                for d in o.deps:
                    if d.dma is not None:
                        key = ("d", d.dma)
                        val = 16 * groups[d.dma][0] if d.waitall else d.dval
                    else:
                        key = ("e", d.eng)
                        val = d.ordinal
                    if val > need.get(key, 0):
                        need[key] = val
                for key, val in need.items():
                    if seen.get(key, 0) >= val:
                        continue
                    seen[key] = val
                    s = dsem[key[1]] if key[0] == "d" else esem[key[1]]
                    eng.wait_ge(s, val)
                ins = o.fn(eng)
                if o.dma is not None:
                    ins.then_inc(dsem[o.dma], 16)
                elif o.flag:
                    ins.then_inc(esem[o.eng], 1)
            fin = {}
            for o in per[engname]:
                if o.dma is not None:
                    v = 16 * groups[o.dma][0] if o.waitall else o.dval
                    fin[o.dma] = max(fin.get(o.dma, 0), v)
            for g, v in fin.items():
                eng.wait_ge(dsem[g], v)

        with nc.Block() as block:
            @block.tensor
            def _(e): run("pe", e)

            @block.scalar
            def _(e): run("act", e)

            @block.vector
            def _(e): run("dve", e)

            @block.gpsimd
            def _(e): run("pool", e)

            @block.sync
            def _(e): run("sp", e)


def make_consts(S):
    bf = ml_dtypes.bfloat16
    NCC = max(1, S // 2048)
    p = np.arange(128)[:, None]
    c = {}
    c["ident"] = np.eye(128, dtype=np.float32).astype(bf)
    c["ones"] = np.ones((128, 128), np.float32).astype(bf)
    bo = np.zeros((128, 128), np.float32); bo[:64, :64] = 1; bo[64:, 64:] = 1
    c["bones"] = bo.astype(bf)
    x = np.arange(S)[None, :]
    c["wind"] = (x // 64 == p).astype(np.float32).astype(bf)
    y = np.arange(384)[None, :]
    c["cw"] = np.where(p > y - 128, NEG, 0.0).astype(np.float32).astype(bf)
    c["dw"] = np.where(y - p >= 128, NEG, 0.0).astype(np.float32).astype(bf)
    y = np.arange(2304)[None, :]
    c["gw"] = np.where(y < 16 * p + 31, NEG, 0.0).astype(np.float32).astype(bf)
    sel = np.zeros((128, 12, 128), np.float32)
    for m in range(4):
        for b in range(3):
            sel[3 * m + b, m * 3 + b, :64] = 1
            sel[3 * (m + 4) + b, m * 3 + b, 64:] = 1
    c["sel"] = sel.astype(bf)
    A = np.zeros((128, NCC, 129), np.float32)
    for cc in range(NCC):
        for pp in range(128):
            cidx = cc * 128 + pp
            for j in range(128):
                if 4 * j - 1 <= cidx <= 4 * j + 3:
                    A[pp, cc, j] = 1
            A[pp, cc, 128] = 1
    c["aaug"] = A.astype(bf)
    xx = np.arange(256)[None, :]
    rel = xx - 128 - p // 64
    c["fm"] = np.where((rel == 0) | (rel == -1), 1e6, np.where(rel > 0, -1e6, 0.0)).astype(np.float32)
    return c


NPRM = 8 * 3 + 4 + 2 + 124 + 12 + 32
PO_MIX, PO_FFN, PO_MEM, PO_NSA, PO_MQK, PO_CW, PO_CB, PO_LG, PO_LB, PO_POS = 0, 8, 16, 24, 28, 30, 154, 158, 162, 166


def pack_params(inp):
    prm = np.zeros((128, NPRM), np.float32)
    prm[:, PO_MIX:PO_MIX + 8] = inp["norm_mix"][0].reshape(8, 128).T
    prm[:, PO_FFN:PO_FFN + 8] = inp["norm_ffn"][0].reshape(8, 128).T
    prm[:, PO_MEM:PO_MEM + 8] = inp["norm_mem"][0].reshape(8, 128).T
    prm[:, PO_NSA:PO_NSA + 4] = np.concatenate([inp["nsa_qk_norm"][0].T, inp["nsa_qk_norm"][0].T], 0)
    prm[:, PO_MQK:PO_MQK + 2] = inp["mem_qk_norm"][0].T
    cw = inp["conv_w"][0][:, 0, :]
    prm[:, PO_CW:PO_CW + 124] = cw.T.reshape(4, 128, 31).transpose(1, 0, 2).reshape(128, 124)
    prm[:, PO_CB:PO_CB + 4] = inp["conv_b"][0].reshape(4, 128).T
    prm[:, PO_LG:PO_LG + 4] = inp["conv_ln_g"][0].reshape(4, 128).T
    prm[:, PO_LB:PO_LB + 4] = inp["conv_ln_b"][0].reshape(4, 128).T
    pos = inp["cmp_pos"][0]
    prm[:, PO_POS:PO_POS + 32] = pos.reshape(2, 16, 2, 64).transpose(2, 3, 0, 1).reshape(128, 32)
    return prm


class _Stop(Exception):
    pass


def build(S, stop=None):
    NT = S // T
    NKT = S // 128
    NCC = max(1, S // 2048)
    nc = bass.Bass("TRN2", target_bir_lowering=False)
    P = Prog()

    def din(name, shape, dt=F32):
        return nc.dram_tensor(name, list(shape), dt, kind="ExternalInput").ap()

    def dscr(name, shape, dt=BF16):
        return nc.dram_tensor(name, list(shape), dt, kind="Internal").ap()

    xT = din("xT", [1024, S]); memT = din("memT", [1024, 256])
    w_in = din("w_in", [1024, IN_COLS]); w_gate = din("w_gate", [1024, DFF]); w_up = din("w_up", [1024, DFF])
    w_down = din("w_down", [DFF, 1024]); w_out = din("w_out", [1024, 1024])
    w_nsa_out = din("w_nsa_out", [512, 1024]); w_conv_out = din("w_conv_out", [512, 1024])
    w_mem_out = din("w_mem_out", [512, 1024]); w_mem_kv = din("w_mem_kv", [1024, 1024])
    cmp_w1 = din("cmp_w1", [2, 2048, 256]); cmp_w2 = din("cmp_w2", [2, 256, 64])
    prm_d = din("prm", [128, NPRM])
    c_ident = din("ident", [128, 128], BF16); c_ones = din("ones", [128, 128], BF16); c_bones = din("bones", [128, 128], BF16)
    c_wind = din("wind", [128, S], BF16); c_cw = din("cw", [128, 384], BF16); c_dw = din("dw", [128, 384], BF16)
    c_gw = din("gw", [128, 2304], BF16); c_sel = din("sel", [128, 12, 128], BF16)
    c_aaug = din("aaug", [128, NCC, 129], BF16); c_fm = din("fm", [128, 256])
    outT = nc.dram_tensor("outT", [1024, S], F32, kind="ExternalOutput").ap()
    wA = dscr("wA", [1024, IN_COLS]); wG = dscr("wG", [1024, DFF]); wU = dscr("wU", [1024, DFF])
    wD = dscr("wD", [DFF, 1024]); wO = dscr("wO", [1024, 1024]); wN = dscr("wN", [512, 1024])
    wC = dscr("wC", [512, 1024]); wM = dscr("wM", [512, 1024]); wKV = dscr("wKV", [1024, 1024])
    w1b = dscr("w1b", [2, 2048, 256]); w2b = dscr("w2b", [2, 256, 64])

    with ExitStack() as st:
        def sb(name, shape, dt):
            return st.enter_context(nc.sbuf_tensor("sb_" + name, list(shape), dt))

        def psb(name):
            return st.enter_context(nc.psum_tensor(name, [128, 512], F32))

        ident = sb("ident", [128, 128], BF16); ones = sb("ones", [128, 128], BF16); bones = sb("bones", [128, 128], BF16)
        wind = sb("wind", [128, S], BF16); cw = sb("cw", [128, 384], BF16); dw = sb("dw", [128, 384], BF16)
        gw = sb("gw", [128, 2304], BF16); sel = sb("sel", [128, 12, 128], BF16)
        aaug = sb("aaug", [128, NCC, 129], BF16); fm = sb("fm", [128, 256], F32)
        prm = sb("prm", [128, NPRM], F32); prm2 = sb("prm2", [128, 16], F32)
        posb = sb("posb", [128, 32], BF16)
        expo = sb("expo", [128, T], F32)
        KS = sb("KS", [128, S], BF16); VS = sb("VS", [128, NKT, 192], BF16)
        KW = sb("KW", [128, 16, 128], BF16); VW = sb("VW", [128, 16, 192], BF16)
        KC = sb("KC", [128, NCC * 128], BF16); VCT = sb("VCT", [128, NCC * 128], BF16); VC = sb("VC", [128, NCC, 192], BF16)
        KM = sb("KM", [128, 4, 256], BF16); VM = sb("VM", [128, 2, 512], BF16)
        w2p = sb("w2p", [128, 8, 128], BF16); w2f = sb("w2f", [128, 4, 64], BF16)
        b1 = sb("b1", [128, 4], F32)
        ring = sb("ring", [128, 2, 4096], BF16)
        xh2 = sb("xh2", [128, 2, 8, T], F32)
        a8x = sb("a8x", [128, 2, 8, T], BF16)
        a8 = sb("a8", [128, 8, T], BF16)
        actb = sb("actb", [128, 22, T], BF16)
        QP = sb("QP", [128, 8, T], BF16); QM = sb("QM", [128, 4, T], BF16)
        onsa = sb("onsa", [128, 4, T], F32); onsab = sb("onsab", [128, 4, T], BF16)
        cu = sb("cu", [128, 4, T], BF16); om = sb("om", [128, 4, T], BF16)
        XX = sb("XX", [128, 4, 16 + T], BF16)
        ub = sb("ub", [128, 4, 30 + T], BF16)
        PB = sb("PB", [128, 4, 2 * T], BF16)
        lnb = sb("lnb", [128, 3, T], F32)
        sq = sb("sq", [128, 2, T], BF16)
        tf = sb("tf", [128, 8, T], F32)
        rs = sb("rs", [128, 2, T], F32)
        sg = sb("sg", [128, T], BF16)
        cacc = sb("cacc", [128, 4, T], F32); mgb = sb("mgb", [128, 4, T], F32)
        negq = sb("negq", [128, 2, 128], BF16); NEGT = sb("NEGT", [128, 2, T], BF16)
        imp = sb("imp", [128, 2, 2, 128], F32); impf = sb("impf", [128, 2, 128], F32)
        m8 = sb("m8", [128, 16], F32); rden = sb("rden", [128, 2], F32)
        hidf = sb("hidf", [128, 8, 16], F32); hidg = sb("hidg", [128, 8, 16], F32); hidb = sb("hidb", [128, 8, 16], BF16)
        kcf = sb("kcf", [128, 16], F32); kcsq = sb("kcsq", [128, 16], BF16)
        memf = sb("memf", [128, 8, 256], F32)
        _b = {n: psb(n) for n in ("pj0", "pj1", "st0", "st1", "oa0", "oa1", "ms0", "ms1")}
        pools = {"pj": ["pj0", "pj1", "st0", "st1"], "st": ["st0", "st1", "pj0"],
                 "oa": ["oa0", "oa1"], "ms": ["ms0", "ms1"]}
        pctr = {k: 0 for k in pools}

        def bank(pool):
            i = pctr[pool] % len(pools[pool]); pctr[pool] += 1
            nm = pools[pool][i]
            return _b[nm], ("bank", nm)

        tctr = [0]

        def tmpf():
            i = tctr[0] % 8; tctr[0] += 1
            return tf[:, i, :], ("tf", i)

        rctr = [0]

        def rsf():
            i = rctr[0] % 2; rctr[0] += 1
            return rs[:, i, :], ("rs", i)

        sqc = [0]

        def sqf():
            i = sqc[0] % 2; sqc[0] += 1
            return sq[:, i, :], ("sq", i)

        pbc = [0]

        def pbf():
            i = pbc[0] % 4; pbc[0] += 1
            return PB[:, i, :], ("PB", i)

        def MM(out, lhsT, rhs, start, stop, reads, writes, sgc=False):
            P.op("pe", lambda e: e.matmul(out, lhsT=lhsT, rhs=rhs, start=start, stop=stop, skip_group_check=sgc), reads, writes)

        def ACT(out, in_, func, reads, writes, scale=None, bias=None):
            kw = {}
            if scale is not None: kw["scale"] = scale
            if bias is not None: kw["bias"] = bias
            P.op("act", lambda e: e.activation(out=out, in_=in_, func=func, **kw), reads, writes)

        def TS(eng, out, in0, s1, s2, op0, op1, reads, writes):
            if op1 is None:
                P.op(eng, lambda e: e.tensor_scalar(out=out, in0=in0, scalar1=s1, scalar2=None, op0=op0), reads, writes)
            else:
                P.op(eng, lambda e: e.tensor_scalar(out=out, in0=in0, scalar1=s1, scalar2=s2, op0=op0, op1=op1), reads, writes)

        def STT(out, in0, scalar, in1, op0, op1, reads, writes):
            P.op("dve", lambda e: e.scalar_tensor_tensor(out=out, in0=in0, scalar=scalar, in1=in1, op0=op0, op1=op1), reads, writes)

        def TT(eng, out, in0, in1, op, reads, writes):
            P.op(eng, lambda e: e.tensor_tensor(out=out, in0=in0, in1=in1, op=op), reads, writes)

        def CP(eng, out, in_, reads, writes):
            if eng == "act":
                P.op("act", lambda e: e.activation(out=out, in_=in_, func=AF.Copy), reads, writes)
            else:
                P.op(eng, lambda e: e.tensor_copy(out=out, in_=in_), reads, writes)

        def MS(eng, ap, val, writes):
            P.op(eng, lambda e: e.memset(ap, val), (), writes)

        def DMA(eng, out, in_, reads, writes, grp, waitall=False, **kw):
            P.op(eng, lambda e: e.dma_start(out=out, in_=in_, **kw), reads, writes, dma=grp, waitall=waitall)

        def rstd_from(ps_ap, pskey, scale, eps, n=T):
            r, rk = rsf()
            t, tk = tmpf()
            ACT(t[:, 0:n], ps_ap, AF.Sqrt, [pskey, "prm2"], [tk], scale=scale, bias=prm2[:, 8:9] if eps == 1e-6 else prm2[:, 9:10])
            P.op("dve", lambda e, o=r[:, 0:n], i_=t[:, 0:n]: e.reciprocal(out=o, in_=i_), [tk], [rk])
            return r, rk

        def chk(name):
            if stop == name:
                raise _Stop()

        def body():
            for dst, src, k in ((ident, c_ident, "ident"), (ones, c_ones, "ones"), (bones, c_bones, "bones"), (wind, c_wind, "wind"),
                                (cw, c_cw, "cw"), (dw, c_dw, "dw"), (gw, c_gw, "gw"), (sel, c_sel, "sel"), (aaug, c_aaug, "aaug"),
                                (fm, c_fm, "fm"), (prm, prm_d, "prm")):
                DMA("sp", dst[:], src, [], [k], "const", waitall=True)
            MS("pool", expo[:], -0.5, ["expo"])
            MS("dve", VS[:], 1.0, [("VS", i) for i in range(NKT)]); MS("dve", VW[:], 1.0, [("VW", i) for i in range(16)]); MS("dve", VC[:], 1.0, ["VC"])
            MS("dve", KC[:], 0.0, ["KC"]); MS("dve", VCT[:], 0.0, ["VCT"]); MS("dve", KW[:], 0.0, [("KW", i) for i in range(16)])
            MS("pool", QP[:], 0.0, [("QP", h) for h in range(8)])
            MS("pool", XX[:], 0.0, [("XX", i) for i in range(4)])
            MS("pool", ub[:], 0.0, [("ub", i) for i in range(4)])
            MS("pool", w2p[:], 0.0, ["w2p"])
            TS("dve", prm2[:, 0:1], prm[:, PO_NSA:PO_NSA + 1], 0.125, None, ALU.mult, None, ["prm"], ["prm2"])
            TS("dve", prm2[:, 1:2], prm[:, PO_MQK:PO_MQK + 1], 128.0 ** -0.5, None, ALU.mult, None, ["prm"], ["prm2"])
            TS("dve", prm2[:, 2:6], prm[:, PO_LG:PO_LG + 4], 0.5, None, ALU.mult, None, ["prm"], ["prm2"])
            TS("dve", prm2[:, 10:14], prm[:, PO_LB:PO_LB + 4], 0.5, None, ALU.mult, None, ["prm"], ["prm2"])
            MS("dve", prm2[:, 8:9], 1e-6, ["prm2"]); MS("dve", prm2[:, 9:10], 1e-5, ["prm2"])
            TS("dve", prm[:, PO_CW:PO_CW + 124], prm[:, PO_CW:PO_CW + 124], 0.5, None, ALU.mult, None, ["prm"], ["prm"])
            CP("dve", posb[:], prm[:, PO_POS:PO_POS + 32], ["prm"], ["posb"])

            stage32 = [memf[:].rearrange("p k n -> p (k n)"), xh2[:, 1, :, :].rearrange("p k n -> p (k n)"),
                       xh2[:, 0, :, :].rearrange("p k n -> p (k n)"), tf[:].rearrange("p k n -> p (k n)")]
            stage16 = [actb[:, 0:8, :].rearrange("p k n -> p (k n)"), actb[:, 8:16, :].rearrange("p k n -> p (k n)"),
                       a8x[:, 0, :, :].rearrange("p k n -> p (k n)"), a8x[:, 1, :, :].rearrange("p k n -> p (k n)")]
            k32 = [[("st32", 0)], [("st32", 1)], [("st32", 2), ("xh", 0)], [("st32", 3)] + [("tf", q) for q in range(8)]]
            k16 = [[("st16", 0)], [("st16", 1)], [("st16", 2)] + [("a8x", 0, q) for q in range(8)],
                   [("st16", 3)] + [("a8x", 1, q) for q in range(8)]]
            cst = [0]

            cast_list = []

            def cast_piece(dst_ap, src_ap, rows, cols, key, perm=False):
                cast_list.append((dst_ap, src_ap, rows, cols, key, perm))

            def cast_load(idx):
                dst_ap, src_ap, rows, cols, key, perm = cast_list[idx]
                i = idx % 4
                DMA("sp", stage32[i][0:rows, 0:cols], src_ap, [], k32[i], ("st32", i))

            def cast_cp_store(idx):
                dst_ap, src_ap, rows, cols, key, perm = cast_list[idx]
                i = idx % 4
                s32 = stage32[i][0:rows, 0:cols]
                s16 = stage16[i][0:rows, 0:cols]
                eng = ("dve", "pool", "dve", "dve", "pool")[idx % 5]
                if perm:
                    CP("dve", s16.rearrange("p (m hf d) -> p m hf d", m=4, hf=2), s32.rearrange("p (hf m d) -> p m hf d", m=4, hf=2),
                       k32[i], k16[i])
                else:
                    CP(eng, s16, s32, k32[i], k16[i])
                DMA("act", dst_ap, s16, k16[i], [(key, i)], ("st16o", i))

            def cast_flush(depth=3):
                n = len(cast_list)
                for idx in range(min(depth, n)):
                    cast_load(idx)
                for idx in range(n):
                    cast_cp_store(idx)
                    if idx + depth < n:
                        cast_load(idx + depth)

            def cast2d(dst, src, R, key, c_lo=0, c_hi=None):
                C = src.shape[1] if c_hi is None else c_hi
                for r0 in range(0, R, 128):
                    r1 = min(R, r0 + 128)
                    for c0 in range(c_lo, C, 2048):
                        c1 = min(C, c0 + 2048)
                        cast_piece(dst[r0:r1, c0:c1], src[r0:r1, c0:c1], r1 - r0, c1 - c0, key)

            for r0 in range(0, 1024, 128):
                cast_piece(wA[r0:r0 + 128, 0:512], w_in[r0:r0 + 128, 0:512], 128, 512, "wA", perm=True)
            cast2d(wA, w_in, 1024, "wA", 512, IN_COLS)
            cast2d(wKV, w_mem_kv, 1024, "wKV")
            for kv in range(2):
                cast2d(w1b[kv], cmp_w1[kv], 2048, "w1b")
                cast2d(w2b[kv], cmp_w2[kv], 256, "w2b")
            for m in range(4):
                cast_piece(wN[m * 128:m * 128 + 64, :], w_nsa_out[m * 64:(m + 1) * 64, :], 64, 1024, "wN")
                cast_piece(wN[m * 128 + 64:m * 128 + 128, :], w_nsa_out[(m + 4) * 64:(m + 5) * 64, :], 64, 1024, "wN")
            cast2d(wC, w_conv_out, 512, "wC"); cast2d(wM, w_mem_out, 512, "wM")
            cast2d(wO, w_out, 1024, "wO")
            cast2d(wG, w_gate, 1024, "wG"); cast2d(wU, w_up, 1024, "wU"); cast2d(wD, w_down, DFF, "wD")
            cast_flush()
            chk("cast")

            rctr2 = [0]

            ring_n = [2]
            memb = memf[:].rearrange("p k n -> p (k n)").bitcast(BF16)

            def wload(src_ap, KC_, ncols, srckey):
                assert KC_ * ncols <= 4096
                i = rctr2[0] % ring_n[0]; rctr2[0] += 1
                if i < 2:
                    view = ring[:, i, 0:KC_ * ncols].rearrange("p (k n) -> p k n", k=KC_)
                    wk = [("ring", i)]
                else:
                    view = memb[:, 0:KC_ * ncols].rearrange("p (k n) -> p k n", k=KC_)
                    wk = [("ring", i), "memf", ("st32", 0)]
                DMA("sp", view, src_ap.rearrange("(k p) n -> p k n", p=128), [(srckey, q) for q in range(4)], wk, ("ring", i))
                return view, ("ring", i)

            def linear(wscr, wkey, KC_, col0, ncols, rhs_fn, rhs_keys, epi, piece=512, chunk=128):
                j = 0
                for c0 in range(col0, col0 + ncols, piece):
                    pc = min(piece, col0 + ncols - c0)
                    view, rk = wload(wscr[0:KC_ * 128, c0:c0 + pc], KC_, pc, wkey)
                    for cc0 in range(0, pc, chunk):
                        ps, pk = bank("pj")
                        for k in range(KC_):
                            MM(ps[:, 0:T], view[:, k, cc0:cc0 + chunk], rhs_fn(k), k == 0, k == KC_ - 1, [rk] + rhs_keys(k), [pk])
                        epi(j, ps, pk)
                        j += 1

            for kv in range(2):
                for ncn in range(2):
                    DMA("sp", tf[:, kv * 2 + ncn, 0:64], cmp_w2[kv, ncn * 128:(ncn + 1) * 128, :], [], [("tf", kv * 2 + ncn)], "w2f", waitall=True)
                    CP("dve", w2f[:, kv * 2 + ncn, :], tf[:, kv * 2 + ncn, 0:64], [("tf", kv * 2 + ncn)], ["w2f"])
            chk("s0")
            for kv in range(2):
                for g in range(2):
                    for ncn in range(2):
                        CP("dve", w2p[:, kv * 4 + g * 2 + ncn, g * 64:(g + 1) * 64], w2f[:, kv * 2 + ncn, :], ["w2f"], ["w2p"])
            chk("s1")
            for kv in range(2):
                psb1, pkb1 = bank("ms")
                for half in range(2):
                    view, rk = wload(w1b[kv, half * 1024:(half + 1) * 1024, :], 8, 256, "w1b")
                    for jp8 in range(8):
                        jp = half * 8 + jp8
                        for ncn in range(2):
                            MM(psb1[:, ncn:ncn + 1], view[:, jp8, ncn * 128:(ncn + 1) * 128], posb[:, kv * 16 + jp:kv * 16 + jp + 1],
                               jp == 0 and ncn == 0, jp == 15, [rk, "posb"], [pkb1], sgc=True)
                CP("dve", b1[:, kv * 2:kv * 2 + 2], psb1[:, 0:2], [pkb1], ["b1"])
            chk("s2")
            DMA("sp", memf[:], memT.rearrange("(k p) n -> p k n", p=128), [], ["memf", ("st32", 0)], "memf")
            psm, pkm = bank("ms")
            for k in range(8):
                s_, sk = sqf()
                ACT(s_, memf[:, k, :], AF.Square, ["memf"], [sk])
                MM(psm[:, 0:256], ones[:], s_, k == 0, k == 7, [sk, "ones"], [pkm])
            rm, rmk = rstd_from(psm[:, 0:256], pkm, 1.0 / 1024, 1e-6, 256)
            for k in range(8):
                STT(a8[:, k, :], memf[:, k, :], prm[:, PO_MEM + k:PO_MEM + k + 1], rm[:, 0:256], ALU.mult, ALU.mult,
                    ["memf", "prm", rmk], [("a8", k)])

            chk("s3")
            def epi_km(j, ps, pk):
                s_, sk = sqf()
                ACT(s_, ps[:, 0:256], AF.Square, [pk], [sk])
                ps2, pk2 = bank("ms")
                MM(ps2[:, 0:256], ones[:], s_, True, True, [sk, "ones"], [pk2])
                r, rk_ = rstd_from(ps2[:, 0:256], pk2, 1.0 / 128, 1e-6, 256)
                STT(KM[:, j, :], ps[:, 0:256], prm[:, PO_MQK + 1:PO_MQK + 2], r[:, 0:256], ALU.mult, ALU.mult, [pk, "prm", rk_], ["KM"])
            linear(wKV, "wKV", 8, 0, 512, lambda k: a8[:, k, :], lambda k: [("a8", k)], epi_km)
            chk("s4")
            for c0 in range(0, 512, 256):
                view, rk = wload(wKV[0:1024, 512 + c0:512 + c0 + 256], 8, 256, "wKV")
                for mc in range(2):
                    ps, pk = bank("pj")
                    for k in range(8):
                        MM(ps[:, 0:256], a8[:, k, mc * 128:(mc + 1) * 128], view[:, k, :], k == 0, k == 7, [rk, ("a8", k)], [pk])
                    CP("dve", VM[:, mc, c0:c0 + 256], ps[:, 0:256], [pk], ["VM"])

            chk("setup")
            ring_n[0] = 3
            def xload(tj):
                par_ = tj % 2
                wk = [("xh", par_)] + ([("st32", 1)] if par_ == 1 else [])
                DMA("sp", xh2[:, par_], xT[:, tj * T:(tj + 1) * T].rearrange("(k p) n -> p k n", p=128), [], wk, ("xh", par_))

            def xnorm(tj):
                par_ = tj % 2
                psn, pkn = bank("ms")
                for k in range(8):
                    s_, sk = sqf()
                    ACT(s_, xh2[:, par_, k, :], AF.Square, [("xh", par_)], [sk])
                    MM(psn[:, 0:T], ones[:], s_, k == 0, k == 7, [sk, "ones"], [pkn])
                rx, rxk = rstd_from(psn[:, 0:T], pkn, 1.0 / 1024, 1e-6)
                for k in range(8):
                    STT(a8x[:, par_, k, :], xh2[:, par_, k, :], prm[:, PO_MIX + k:PO_MIX + k + 1], rx, ALU.mult, ALU.mult,
                        [("xh", par_), "prm", rxk], [("a8x", par_, k)])
            xload(0)
            xnorm(0)
            for ti in range(NT):
                t0 = ti * T
                par = ti % 2
                xh = xh2[:, par]
                xhk = ("xh", par)
                xr = lambda k, par=par: a8x[:, par, k, :]
                xk = lambda k, par=par: [("a8x", par, k)]

                chk("A%d" % ti)
                def epi_q(j, ps, pk):
                    s_, sk = sqf()
                    ACT(s_, ps[:, 0:T], AF.Square, [pk], [sk])
                    ps2, pk2 = bank("ms")
                    MM(ps2[:, 0:T], bones[:], s_, True, True, [sk, "bones"], [pk2])
                    r, rk_ = rstd_from(ps2[:, 0:T], pk2, 1.0 / 64, 1e-6)
                    STT(QP[0:64, j, :], ps[0:64, 0:T], prm2[0:64, 0:1], r[0:64, :], ALU.mult, ALU.mult, [pk, "prm2", rk_], [("QP", j)])
                    STT(QP[64:128, j + 4, :], ps[64:128, 0:T], prm2[64:128, 0:1], r[64:128, :], ALU.mult, ALU.mult, [pk, "prm2", rk_], [("QP", j + 4)])
                linear(wA, "wA", 8, 0, 512, xr, xk, epi_q)

                CP("pool", XX[:, :, 0:16], XX[:, :, T:T + 16], [("XX", i) for i in range(4)], [("XX", i) for i in range(4)])

                def epi_kv(j, ps, pk):
                    if j < 2:
                        kv = j
                        CP("act", XX[0:64, kv * 2 + 0, 16:16 + T], ps[0:64, 0:T], [pk], [("XX", kv * 2)])
                        CP("dve", XX[64:128, kv * 2 + 0, 15:15 + T], ps[0:64, 0:T], [pk], [("XX", kv * 2)])
                        CP("dve", XX[0:64, kv * 2 + 1, 16:16 + T], ps[64:128, 0:T], [pk], [("XX", kv * 2 + 1)])
                        CP("act", XX[64:128, kv * 2 + 1, 15:15 + T], ps[64:128, 0:T], [pk], [("XX", kv * 2 + 1)])
                    else:
                        s_, sk = sqf()
                        ACT(s_, ps[:, 0:T], AF.Square, [pk], [sk])
                        ps2, pk2 = bank("ms")
                        MM(ps2[:, 0:T], bones[:], s_, True, True, [sk, "bones"], [pk2])
                        r, rk_ = rstd_from(ps2[:, 0:T], pk2, 1.0 / 64, 1e-6)
                        if j == 2:
                            STT(KS[:, t0:t0 + T], ps[:, 0:T], prm[:, PO_NSA + 2:PO_NSA + 3], r, ALU.mult, ALU.mult, [pk, "prm", rk_], [("KS", 2 * ti), ("KS", 2 * ti + 1)])
                        else:
                            for s2 in range(2):
                                slot = (2 * ti + s2) % 16
                                STT(KW[:, slot, :], ps[:, s2 * 128:(s2 + 1) * 128], prm[:, PO_NSA + 3:PO_NSA + 4], r[:, s2 * 128:(s2 + 1) * 128],
                                    ALU.mult, ALU.mult, [pk, "prm", rk_], [("KW", slot)])
                viewA, rkA = wload(wA[0:1024, 512:1024], 8, 512, "wA")
                viewB, rkB = wload(wA[0:1024, 1024:1408], 8, 384, "wA")

                def fm_chunk(view, rk, c, epi, j):
                    ps, pk = bank("pj")
                    for k in range(8):
                        MM(ps[:, 0:T], view[:, k, c:c + 128], xr(k), k == 0, k == 7, [rk] + xk(k), [pk])
                    epi(j, ps, pk)

                def vtok(view, rk, c, isw):
                    for s2 in range(2):
                        ps, pk = bank("pj")
                        for k in range(8):
                            MM(ps[:, 0:128], a8x[:, par, k, s2 * 128:(s2 + 1) * 128], view[:, k, c:c + 128], k == 0, k == 7, [rk, ("a8x", par, k)], [pk])
                        kt = 2 * ti + s2
                        if isw:
                            CP("dve", VW[:, kt % 16, 0:64], ps[:, 0:64], [pk], [("VW", kt % 16)])
                            CP("act", VW[:, kt % 16, 128:192], ps[:, 64:128], [pk], [("VW", kt % 16)])
                        else:
                            CP("dve", VS[:, kt, 0:64], ps[:, 0:64], [pk], [("VS", kt)])
                            CP("act", VS[:, kt, 128:192], ps[:, 64:128], [pk], [("VS", kt)])
                fm_chunk(viewA, rkA, 0, epi_kv, 0)
                fm_chunk(viewA, rkA, 128, epi_kv, 1)
                fm_chunk(viewA, rkA, 256, epi_kv, 2)
                vtok(viewA, rkA, 384, False)
                fm_chunk(viewB, rkB, 0, epi_kv, 3)
                vtok(viewB, rkB, 128, True)

                def epi_g(j, ps, pk):
                    t, tk = tmpf()
                    ACT(t, ps[:, 0:T], AF.Tanh, [pk], [tk], scale=0.5)
                    TS("dve", sg[:], t, 0.5, 0.5, ALU.mult, ALU.add, [tk], ["sg"])
                fm_chunk(viewB, rkB, 256, epi_g, 0)

                psh, pkh = bank("ms")
                for kv in range(2):
                    for half in range(1):
                        view, rk = wload(w1b[kv, :, :], 16, 256, "w1b")
                        for g in range(2):
                            for ncn in range(2):
                                col = ((kv * 2 + g) * 2 + ncn) * 16
                                for jp8 in range(16):
                                    jp = jp8
                                    MM(psh[:, col:col + 16], view[:, jp8, ncn * 128:(ncn + 1) * 128],
                                       XX[:, kv * 2 + g, 2 * jp:2 * jp + T - 15:16], jp == 0 and col == 0, jp == 15, [rk, ("XX", kv * 2 + g)], [pkh], sgc=True)
                hv = psh[:, 0:128].rearrange("p (a g n c) -> p a g n c", a=2, g=2, n=2)
                hf = hidf[:].rearrange("p (a g n) c -> p a g n c", a=2, g=2)
                for kv in range(2):
                    for ncn in range(2):
                        ACT(hf[:, kv, :, ncn, :], hv[:, kv, :, ncn, :], AF.Identity, [pkh, "b1"], ["hidf"], bias=b1[:, kv * 2 + ncn:kv * 2 + ncn + 1])
                lo = 1 if ti == 0 else 0
                cb = 16 * ti - 1

                def cstep_B():
                    TT("dve", hidg[:], hidf[:], hidf[:], ALU.mult, ["hidf"], ["hidg"])
                    TS("dve", hidg[:], hidg[:], 0.044715, 1.0, ALU.mult, ALU.add, ["hidg"], ["hidg"])
                    TT("dve", hidg[:], hidg[:], hidf[:], ALU.mult, ["hidg", "hidf"], ["hidg"])

                def cstep_C():
                    ACT(hidg[:], hidg[:], AF.Tanh, ["hidg"], ["hidg"], scale=0.7978845608028654)
                    STT(hidg[:], hidg[:], 1.0, hidf[:], ALU.add, ALU.mult, ["hidg", "hidf"], ["hidg"])
                    TS("dve", hidb[:], hidg[:], 0.5, None, ALU.mult, None, ["hidg"], ["hidb"])

                def cstep_kv(kv):
                    psc, pkc = bank("ms")
                    n = 0
                    for g in range(2):
                        for ncn in range(2):
                            MM(psc[:, 0:16], w2p[:, kv * 4 + g * 2 + ncn, :], hidb[:, (kv * 2 + g) * 2 + ncn, :], n == 0, n == 3, ["w2p", "hidb"], [pkc])
                            n += 1
                    if kv == 0:
                        ACT(kcsq[:], psc[:, 0:16], AF.Square, [pkc], ["kcsq"])
                        ps2, pk2 = bank("ms")
                        MM(ps2[:, 0:16], bones[:], kcsq[:], True, True, ["kcsq", "bones"], [pk2])
                        r, rk_ = rstd_from(ps2[:, 0:16], pk2, 1.0 / 64, 1e-6, 16)
                        STT(KC[:, cb + lo:cb + 16], psc[:, lo:16], prm[:, PO_NSA + 1:PO_NSA + 2], r[:, lo:16], ALU.mult, ALU.mult, [pkc, "prm", rk_], ["KC"])
                    else:
                        CP("dve", VCT[:, cb + lo:cb + 16], psc[:, lo:16], [pkc], ["VCT"])

                def cstep_F():
                    for ccn in sorted({max(0, 16 * ti - 1) // 128, (16 * ti + 14) // 128}):
                        pst, pkt = bank("ms")
                        ptv = pst[:].bitcast(BF16)[:, 0:128]
                        P.op("pe", lambda e, o=ptv, i=VCT[:, ccn * 128:(ccn + 1) * 128]: e.transpose(out=o, in_=i, identity=ident[:]), ["VCT", "ident"], [pkt])
                        CP("dve", VC[:, ccn, 0:64], ptv[:, 0:64], [pkt], ["VC"])
                        CP("act", VC[:, ccn, 128:192], ptv[:, 64:128], [pkt], ["VC"])
                csteps = [cstep_B, cstep_C, lambda: cstep_kv(0), lambda: cstep_kv(1), cstep_F]

                csteps[0]()

                CP("pool", ub[:, :, 0:30], ub[:, :, T:T + 30], [("ub", i) for i in range(4)], [("ub", i) for i in range(4)])
                tg = []

                def epi_ag(j, ps, pk):
                    t, tk = tmpf()
                    ACT(t, ps[:, 0:T], AF.Tanh, [pk], [tk], scale=0.5)
                    tg.append((t, tk))
                linear(wA, "wA", 8, 1816, 512, xr, xk, epi_ag)
                csteps[1]()

                def epi_a(j, ps, pk):
                    t, tk = tg[j]
                    STT(ub[:, j, 30:30 + T], t, 1.0, ps[:, 0:T], ALU.add, ALU.mult, [tk, pk], [("ub", j)])
                linear(wA, "wA", 8, 1304, 512, xr, xk, epi_a)
                csteps[2]()

                def epi_qm(j, ps, pk):
                    s_, sk = sqf()
                    ACT(s_, ps[:, 0:T], AF.Square, [pk], [sk])
                    ps2, pk2 = bank("ms")
                    MM(ps2[:, 0:T], ones[:], s_, True, True, [sk, "ones"], [pk2])
                    r, rk_ = rstd_from(ps2[:, 0:T], pk2, 1.0 / 128, 1e-6)
                    STT(QM[:, j, :], ps[:, 0:T], prm2[:, 1:2], r, ALU.mult, ALU.mult, [pk, "prm2", rk_], [("QM", j)])
                linear(wA, "wA", 8, 2328, 512, xr, xk, epi_qm)
                csteps[3]()
                csteps[4]()

                chk("B%d" % ti)
                chk("C%d" % ti)
                def attn_stream(h, tiles, bkey):
                    po, pko = bank("oa")
                    n = len(tiles)
                    groups = [tiles[i:i + 2] for i in range(0, n, 2)]

                    def issue_qk(grp):
                        ps, pk = bank("st")
                        for gi, (kl, kk, masks, vl, vk, extra) in enumerate(grp):
                            o = ps[:, gi * T:(gi + 1) * T]
                            MM(o, kl, QP[:, h, :], True, len(masks) == 0, kk + [("QP", h)], [pk])
                            for mi, (ml, mr, mk) in enumerate(masks):
                                MM(o, ml, mr, False, mi == len(masks) - 1, mk, [pk])
                        return ps, pk
                    nxt = issue_qk(groups[0])
                    cnt = 0
                    for gidx, grp in enumerate(groups):
                        ps, pk = nxt
                        if gidx + 1 < len(groups):
                            nxt = issue_qk(groups[gidx + 1])
                        pb, pbk = pbf()
                        w = len(grp) * T
                        ACT(pb[:, 0:w], ps[:, 0:w], AF.Exp, [pk], [pbk])
                        for gi, (kl, kk, masks, vl, vk, extra) in enumerate(grp):
                            ph = pb[:, gi * T:(gi + 1) * T]
                            MM(po[:, 0:T], vl, ph, cnt == 0, cnt == n - 1, vk + [pbk], [pko])
                            if extra is not None:
                                extra(ph, pbk, cnt, n)
                            cnt += 1
                    return po, pko

                onsa_init = set()

                def finish(h, br, po, pko):
                    m = h % 4
                    lo_ = h < 4
                    nr = slice(0, 64) if lo_ else slice(64, 128)
                    dr = slice(64, 128) if lo_ else slice(0, 64)
                    t, tk = tmpf()
                    if ti == 0 and br == 0:
                        TS("dve", t[nr, :], po[dr, 0:T], 1e-30, None, ALU.add, None, [pko], [tk])
                        P.op("dve", lambda e, o=t[nr, :]: e.reciprocal(out=o, in_=o), [tk], [tk])
                    else:
                        P.op("dve", lambda e, o=t[nr, :], i_=po[dr, 0:T]: e.reciprocal(out=o, in_=i_), [pko], [tk])
                    pg, pkg = bank("ms")
                    MM(pg[:, 0:T], sel[:, m * 3 + br, :], sg[:], True, True, ["sel", "sg"], [pkg])
                    TT("dve", t[nr, :], t[nr, :], pg[nr, 0:T], ALU.mult, [tk, pkg], [tk])
                    if h not in onsa_init:
                        onsa_init.add(h)
                        TT("dve", onsa[nr, m, :], po[nr, 0:T], t[nr, :], ALU.mult, [pko, tk], [("onsa", m, lo_)])
                    else:
                        t2, tk2 = tmpf()
                        TT("dve", t2[nr, :], po[nr, 0:T], t[nr, :], ALU.mult, [pko, tk], [tk2])
                        TT("pool", onsa[nr, m, :], onsa[nr, m, :], t2[nr, :], ALU.add, [tk2, ("onsa", m, lo_)], [("onsa", m, lo_)])

                def vsel(arr, idx, h):
                    return arr[:, idx, 0:128] if h < 4 else arr[:, idx, 64:192]

                mean, mk_ = lnb[:, 0, :], ("lnb", 0)
                var, vk_ = lnb[:, 1, :], ("lnb", 1)
                rl, rlk = lnb[:, 2, :], ("lnb", 2)
                psE = {}

                def conv_tap(c4, kk):
                    acc, ak = cacc[:, c4, :], ("cacc", c4)
                    wcol = prm[:, PO_CW + c4 * 31 + kk:PO_CW + c4 * 31 + kk + 1]
                    if kk == 0:
                        TS("dve", acc, ub[:, c4, 0:T], wcol, prm[:, PO_CB + c4:PO_CB + c4 + 1], ALU.mult, ALU.add, [("ub", c4), "prm"], [ak])
                    else:
                        STT(acc, ub[:, c4, kk:kk + T], wcol, acc, ALU.mult, ALU.add, [("ub", c4), "prm", ak], [ak])

                def conv_post(c4):
                    acc, ak = cacc[:, c4, :], ("cacc", c4)
                    CP("act", cu[:, c4, :], acc, [ak], [("cu", c4)])
                    ACT(actb[:, 16 + c4, :], acc, AF.Square, [ak], [("actb", 16 + c4)])
                conv_q = []
                for c4_ in range(4):
                    for kk_ in range(31):
                        conv_q.append(lambda c4=c4_, kk=kk_: conv_tap(c4, kk))
                    conv_q.append(("post", c4_))
                conv_def = []

                def conv_drain(k):
                    while conv_def:
                        conv_post(conv_def.pop(0))
                    for _ in range(min(k, len(conv_q))):
                        it = conv_q.pop(0)
                        if isinstance(it, tuple):
                            conv_def.append(it[1])
                        else:
                            it()
                _nw = len(range(max(0, 2 * ti - 4), 2 * ti + 2))
                _ncp = (16 * ti + 14) // 128 + 1
                _ns = 2 * ti + 2
                _units = 5 * _ns
                _done = [0, 0]

                def conv_share(n):
                    _done[0] += n
                    tgt = (128 * _done[0] + _units - 1) // _units
                    conv_drain(tgt - _done[1]); _done[1] = tgt

                def conv_E2():
                    psE1, pkE1 = bank("ms")
                    psE2, pkE2 = bank("ms")
                    for c4 in range(4):
                        MM(psE1[:, 0:T], ones[:], cu[:, c4, :], c4 == 0, c4 == 3, [("cu", c4), "ones"], [pkE1])
                    for c4 in range(4):
                        MM(psE2[:, 0:T], ones[:], actb[:, 16 + c4, :], c4 == 0, c4 == 3, [("actb", 16 + c4), "ones"], [pkE2])
                    ACT(mean, psE1[:, 0:T], AF.Copy, [pkE1], [mk_], scale=1.0 / 512)
                    STT(var, mean, -1.0, mean, ALU.mult, ALU.mult, [mk_], [vk_])
                    STT(var, psE2[:, 0:T], 1.0 / 512, var, ALU.mult, ALU.add, [pkE2, vk_], [vk_])
                    ACT(var, var, AF.Sqrt, [vk_, "prm2"], [vk_], bias=prm2[:, 9:10])
                    P.op("dve", lambda e, o=rl, i_=var: e.reciprocal(out=o, in_=i_), [vk_], [rlk])

                def conv_E3(c4):
                    acc, ak = cacc[:, c4, :], ("cacc", c4)
                    TT("dve", acc, acc, mean, ALU.subtract, [ak, mk_], [ak])
                    TT("dve", acc, acc, rl, ALU.mult, [ak, rlk], [ak])
                    t, tk = tmpf()
                    ACT(t, acc, AF.Tanh, [ak, "prm2"], [tk], scale=prm2[:, 2 + c4:3 + c4], bias=prm2[:, 10 + c4:11 + c4])
                    TS("dve", acc, acc, prm2[:, 2 + c4:3 + c4], prm2[:, 10 + c4:11 + c4], ALU.mult, ALU.add, [ak, "prm2"], [ak])
                    STT(cu[:, c4, :], t, 1.0, acc, ALU.add, ALU.mult, [tk, ak], [("cu", c4)])

                chk("Dc%d" % ti)
                def w_job(h):
                    tiles = []
                    for kt in range(max(0, 2 * ti - 4), 2 * ti + 2):
                        r_ = kt - 2 * ti
                        masks = []
                        if r_ >= 0:
                            masks = [(ident[:], cw[:, 128 - 128 * r_:128 - 128 * r_ + T], ["ident", "cw"])]
                        elif r_ == -3:
                            masks = [(ident[:], dw[:, 0:T], ["ident", "dw"])]
                        elif r_ == -4:
                            masks = [(ident[:], dw[:, 128:128 + T], ["ident", "dw"])]
                        tiles.append((KW[:, kt % 16, :], [("KW", kt % 16)], masks, vsel(VW, kt % 16, h), [("VW", kt % 16)], None))
                    po, pko = yield (h, tiles)
                    finish(h, 2, po, pko)
                    if h >= 8:
                        conv_share(_nw)
                ncc_t = (16 * ti + 14) // 128 + 1

                def c_job(h):
                    g = h // 4
                    pu_box = {}

                    def extra_u(pb, pbk, i, n):
                        if i == 0:
                            pu, pku_ = bank("ms")
                            pu_box["uv"] = pu[:].rearrange("p (s c) -> p s c", s=2)
                            pu_box["k"] = pku_
                        uv, pku = pu_box["uv"], pu_box["k"]
                        for s2 in range(2):
                            MM(uv[:, s2, 0:129], pb[:, s2 * 128:(s2 + 1) * 128], aaug[:, i, :], i == 0 and s2 == 0, i == n - 1, [pbk, "aaug"], [pku], sgc=True)
                    tiles = []
                    for cc in range(ncc_t):
                        s_ = ti - 8 * cc
                        masks = [(ident[:], gw[:, 256 * s_:256 * s_ + T], ["ident", "gw"])] if s_ <= 8 else []
                        tiles.append((KC[:, cc * 128:(cc + 1) * 128], ["KC"], masks, vsel(VC, cc, h), ["VC"], extra_u))
                    po, pko = yield (h, tiles)
                    uv, pku = pu_box["uv"], pu_box["k"]
                    finish(h, 0, po, pko)
                    TS("dve", rden[:], uv[:, :, 128], 1e-30, None, ALU.add, None, [pku], ["rden"])
                    P.op("dve", lambda e: e.reciprocal(out=rden[:], in_=rden[:]), ["rden"], ["rden"])
                    for s2 in range(2):
                        if h % 4 == 0:
                            TS("dve", imp[:, g, s2, :], uv[:, s2, 0:128], rden[:, s2:s2 + 1], None, ALU.mult, None, [pku, "rden"], [("imp", g)])
                        else:
                            STT(imp[:, g, s2, :], uv[:, s2, 0:128], rden[:, s2:s2 + 1], imp[:, g, s2, :], ALU.mult, ALU.add,
                                [pku, "rden", ("imp", g)], [("imp", g)])
                    if h % 4 == 3:
                        for s2 in range(2):
                            B = 4 * ti + 2 * s2
                            TT("dve", impf[:, 0, :], imp[:, g, s2, :], fm[:, 128 - B:256 - B], ALU.add, [("imp", g), "fm"], ["impf"])
                            MS("dve", impf[:, 0, 0:1], 2e6, ["impf"])
                            P.op("dve", lambda e: e.max(out=m8[:, 0:8], in_=impf[:, 0, :]), ["impf"], ["m8"])
                            P.op("dve", lambda e: e.match_replace(out=impf[:, 1, :], in_to_replace=m8[:, 0:8], in_values=impf[:, 0, :],
                                                                  imm_value=-3.0e38), ["impf", "m8"], ["impf2"])
                            P.op("dve", lambda e: e.max(out=m8[:, 8:16], in_=impf[:, 1, :]), ["impf2"], ["m8"])
                            TS("dve", negq[:, s2, :], impf[:, 0, :], m8[:, 15:16], NEG, ALU.is_lt, ALU.mult, ["impf", "m8"], [("negq", s2)])
                            pt2, pkt2 = bank("ms")
                            pv = pt2[:].bitcast(BF16)[:, 0:128]
                            P.op("pe", lambda e, o=pv, i_=negq[:, s2, :]: e.transpose(out=o, in_=i_, identity=ident[:]), [("negq", s2), "ident"], [pkt2])
                            CP("act", NEGT[:, g, s2 * 128:(s2 + 1) * 128], pv, [pkt2], [("NEGT", g)])
                    if h >= 8:
                        conv_share(_ncp)
                chk("Dw%d" % ti)
                def s_job(h):
                    g = h // 4
                    tiles = []
                    for kt in range(0, 2 * ti + 2):
                        r_ = kt - 2 * ti
                        masks = [(wind[:, kt * 128:(kt + 1) * 128], NEGT[:, g, :], ["wind", ("NEGT", g)])]
                        if r_ >= 0:
                            masks.append((ident[:], cw[:, 128 - 128 * r_:128 - 128 * r_ + T], ["ident", "cw"]))
                        tiles.append((KS[:, kt * 128:(kt + 1) * 128], [("KS", kt)], masks, vsel(VS, kt, h), [("VS", kt)], None))
                    po, pko = yield (h, tiles)
                    finish(h, 1, po, pko)

                jobs = [("C", 0), ("W", 0), ("C", 1), ("W", 1), ("C", 2), ("W", 2), ("C", 3), ("W", 3),
                        ("C", 4), ("W", 4), ("C", 5), ("W", 5), ("C", 6), ("W", 6), ("C", 7), ("W", 7),
                        ("S", 0), ("S", 1), ("S", 2), ("S", 3), ("S", 4), ("S", 5), ("S", 6), ("S", 7)]
                nS = [0]

                def job_gen(kind, h):
                    if kind == "W":
                        yield from w_job(h)
                    elif kind == "C":
                        yield from c_job(h)
                    else:
                        yield from s_job(h)
                        nS[0] += 1
                        if nS[0] <= 4:
                            conv_share(_ns)
                        elif nS[0] == 5:
                            conv_drain(1000)
                        elif nS[0] == 6:
                            conv_drain(0)
                            conv_E2()
                        elif nS[0] == 7:
                            conv_E3(0); conv_E3(1)
                        else:
                            conv_E3(2); conv_E3(3)

                reqs = []
                for kind, h in jobs:
                    g_ = job_gen(kind, h)
                    h_, tiles_ = next(g_)
                    reqs.append({"h": h_, "tiles": tiles_, "gen": g_, "po": None, "cnt": 0, "n": len(tiles_)})
                seq = []
                for ji, rq in enumerate(reqs):
                    for i0 in range(0, rq["n"], 2):
                        seq.append((ji, rq["tiles"][i0:i0 + 2]))

                def issue_qk2(ji, grp):
                    hq = reqs[ji]["h"]
                    ps, pk = bank("st")
                    for gi, (kl, kk, masks, vl, vk, extra) in enumerate(grp):
                        o = ps[:, gi * T:(gi + 1) * T]
                        MM(o, kl, QP[:, hq, :], True, len(masks) == 0, kk + [("QP", hq)], [pk])
                        for mi, (ml, mr, mk) in enumerate(masks):
                            MM(o, ml, mr, False, mi == len(masks) - 1, mk, [pk])
                    return ps, pk
                nxt = issue_qk2(*seq[0])
                for si, (ji, grp) in enumerate(seq):
                    ps, pk = nxt
                    if si + 1 < len(seq):
                        nxt = issue_qk2(*seq[si + 1])
                    rq = reqs[ji]
                    pb, pbk = pbf()
                    w = len(grp) * T
                    ACT(pb[:, 0:w], ps[:, 0:w], AF.Exp, [pk], [pbk])
                    if rq["po"] is None:
                        rq["po"], rq["pko"] = bank("oa")
                    for gi, (kl, kk, masks, vl, vk, extra) in enumerate(grp):
                        ph = pb[:, gi * T:(gi + 1) * T]
                        MM(rq["po"][:, 0:T], vl, ph, rq["cnt"] == 0, rq["cnt"] == rq["n"] - 1, vk + [pbk], [rq["pko"]])
                        if extra is not None:
                            extra(ph, pbk, rq["cnt"], rq["n"])
                        rq["cnt"] += 1
                    if rq["cnt"] == rq["n"]:
                        try:
                            rq["gen"].send((rq["po"], rq["pko"]))
                        except StopIteration:
                            pass
                for m in range(4):
                    CP("pool", onsab[:, m, :], onsa[:, m, :], [("onsa", m, True), ("onsa", m, False)], [("onsab", m)])
                chk("E%d" % ti)
                for hd in range(4):
                    po, pko = bank("oa")
                    pd, pkd = bank("ms")
                    for mc in range(2):
                        ps, pk = bank("st")
                        MM(ps[:, 0:T], KM[:, hd, mc * 128:(mc + 1) * 128], QM[:, hd, :], True, True, ["KM", ("QM", hd)], [pk])
                        pb, pbk = pbf()
                        ACT(pb[:, 0:T], ps[:, 0:T], AF.Exp, [pk], [pbk])
                        MM(po[:, 0:T], VM[:, mc, hd * 128:(hd + 1) * 128], pb[:, 0:T], mc == 0, mc == 1, ["VM", pbk], [pko])
                        MM(pd[:, 0:T], ones[:], pb[:, 0:T], mc == 0, mc == 1, ["ones", pbk], [pkd])
                    t, tk = tmpf()
                    P.op("dve", lambda e, o=t, i_=pd[:, 0:T]: e.reciprocal(out=o, in_=i_), [pkd], [tk])
                    TT("dve", om[:, hd, :], po[:, 0:T], t, ALU.mult, [pko, tk], [("om", hd)])

                chk("F%d" % ti)
                if ti + 1 < NT:
                    xload(ti + 1)
                br_src = [(wN, "wN", lambda k: onsab[:, k, :], lambda k: [("onsab", k)]),
                          (wC, "wC", lambda k: cu[:, k, :], lambda k: [("cu", k)]),
                          (wM, "wM", lambda k: om[:, k, :], lambda k: [("om", k)])]
                for half in range(2):
                    mg = {}
                    for b in range(3):
                        gts = []

                        def epi_gm(j, ps, pk, gts=gts):
                            t, tk = tmpf()
                            ACT(t, ps[:, 0:T], AF.Tanh, [pk], [tk], scale=0.5)
                            gts.append((t, tk))
                        linear(wA, "wA", 8, 2840 + b * 1024 + half * 512, 512, xr, xk, epi_gm)
                        wsc, wk_, rf, rkf = br_src[b]

                        def epi_br(j, ps, pk, gts=gts, b=b):
                            t, tk = gts[j]
                            oc = half * 4 + j
                            if b == 0:
                                STT(mgb[:, j, :], t, 1.0, ps[:, 0:T], ALU.add, ALU.mult, [tk, pk], [("mgb", j)])
                            else:
                                STT(t, t, 1.0, ps[:, 0:T], ALU.add, ALU.mult, [tk, pk], [tk])
                                if b == 1:
                                    TT("pool", mgb[:, j, :], mgb[:, j, :], t, ALU.add, [("mgb", j), tk], [("mgb", j)])
                                else:
                                    TT("pool", actb[:, oc, :], mgb[:, j, :], t, ALU.add, [("mgb", j), tk], [("actb", oc), ("st16", 0), ("st16", 1)] if ti == 0 else [("actb", oc)])
                        linear(wsc, wk_, 4, half * 512, 512, rf, rkf, epi_br, piece=512)
                def epi_out(j, ps, pk):
                    STT(xh[:, j, :], ps[:, 0:T], 0.5, xh[:, j, :], ALU.mult, ALU.add, [pk, xhk], [xhk])
                linear(wO, "wO", 8, 0, 1024, lambda k: actb[:, k, :], lambda k: [("actb", k)], epi_out)

                chk("G%d" % ti)
                if ti + 1 < NT:
                    xnorm(ti + 1)
                psn, pkn = bank("ms")
                for k in range(8):
                    s_, sk = sqf()
                    ACT(s_, xh[:, k, :], AF.Square, [xhk], [sk])
                    MM(psn[:, 0:T], ones[:], s_, k == 0, k == 7, [sk, "ones"], [pkn])
                rh, rhk = rstd_from(psn[:, 0:T], pkn, 1.0 / 1024, 1e-6)
                for k in range(8):
                    STT(a8[:, k, :], xh[:, k, :], prm[:, PO_FFN + k:PO_FFN + k + 1], rh, ALU.mult, ALU.mult, [xhk, "prm", rhk], [("a8", k)])
                hr = lambda k: a8[:, k, :]
                hk = lambda k: [("a8", k)]
                for c0 in range(0, DFF, 512):
                    pc_ = min(512, DFF - c0)
                    gl = []

                    def epi_gate(j, ps, pk, gl=gl):
                        t, tk = tmpf()
                        ACT(t, ps[:, 0:T], AF.Tanh, [pk], [tk], scale=0.5)
                        STT(t, t, 1.0, ps[:, 0:T], ALU.add, ALU.mult, [tk, pk], [tk])
                        gl.append((t, tk))
                    linear(wG, "wG", 8, c0, pc_, hr, hk, epi_gate)

                    def epi_up(j, ps, pk, gl=gl, c0=c0):
                        t, tk = gl[j]
                        f = c0 // 128 + j
                        TT("dve", actb[:, f, :], t, ps[:, 0:T], ALU.mult, [tk, pk], [("actb", f)])
                    linear(wU, "wU", 8, c0, pc_, hr, hk, epi_up)
                for half in range(2):
                    bks = [bank("pj") for _ in range(4)]
                    for (r0, kc_) in ((0, 8), (1024, 8), (2048, 6)):
                        view, rk = wload(wD[r0:r0 + kc_ * 128, half * 512:(half + 1) * 512], kc_, 512, "wD")
                        for j in range(4):
                            ps, pk = bks[j]
                            for k in range(kc_):
                                f = r0 // 128 + k
                                MM(ps[:, 0:T], view[:, k, j * 128:(j + 1) * 128], actb[:, f, :], f == 0, f == 21, [rk, ("actb", f)], [pk])
                    for j in range(4):
                        oc = half * 4 + j
                        ps, pk = bks[j]
                        STT(xh[:, oc, :], ps[:, 0:T], 0.5, xh[:, oc, :], ALU.mult, ALU.add, [pk, xhk], [xhk])
                DMA("act", outT[:, t0:t0 + T].rearrange("(k p) n -> p k n", p=128), xh, [xhk], [], ("out", par))

        try:
            body()
        except _Stop:
            pass
        P.emit(nc, st)
    return nc


_CACHE = {}


def kernel(**inputs):
    x = np.asarray(inputs["x"], np.float32)
    B, S, D = x.shape
    mem = np.asarray(inputs["mem"], np.float32)
    if S not in _CACHE:
        _CACHE[S] = build(S)
    nc = _CACHE[S]
    consts = make_consts(S)
    prm = pack_params({k: np.asarray(v, np.float32) for k, v in inputs.items()})
    shared = {
        "w_in": np.ascontiguousarray(inputs["w_in"][0], np.float32), "w_gate": np.ascontiguousarray(inputs["w_gate"][0], np.float32),
        "w_up": np.ascontiguousarray(inputs["w_up"][0], np.float32), "w_down": np.ascontiguousarray(inputs["w_down"][0], np.float32),
        "w_out": np.ascontiguousarray(inputs["w_out"][0], np.float32), "w_nsa_out": np.ascontiguousarray(inputs["w_nsa_out"][0], np.float32),
        "w_conv_out": np.ascontiguousarray(inputs["w_conv_out"][0], np.float32), "w_mem_out": np.ascontiguousarray(inputs["w_mem_out"][0], np.float32),
        "w_mem_kv": np.ascontiguousarray(inputs["w_mem_kv"][0], np.float32), "cmp_w1": np.ascontiguousarray(inputs["cmp_w1"][0], np.float32),
        "cmp_w2": np.ascontiguousarray(inputs["cmp_w2"][0], np.float32), "prm": prm,
    }
    shared.update(consts)
    in_maps = []
    for b in range(B):
        m = dict(shared)
        m["xT"] = np.ascontiguousarray(x[b].T)
        m["memT"] = np.ascontiguousarray(mem[b].T)
        in_maps.append(m)
    res = run_bass_kernel_spmd(nc, in_maps, core_ids=list(range(B)))
    out = np.empty((B, S, D), np.float32)
    for b in range(B):
        out[b] = res.results[b]["outT"].T
    return out
```

```python
import numpy as np
import ml_dtypes
from contextlib import ExitStack
import concourse.bass as bass
import concourse.mybir as mybir
from concourse.bass_utils import run_bass_kernel_spmd

F32 = mybir.dt.float32
BF16 = mybir.dt.bfloat16
AF = mybir.ActivationFunctionType
ALU = mybir.AluOpType
ENGS = ("pe", "act", "dve", "pool", "sp")
T = 256
NEG = -30000.0
IN_COLS = 5912
DFF = 2816


class Op:
    __slots__ = ("eng", "fn", "reads", "writes", "dma", "deps", "flag", "ordinal", "dval", "idx", "waitall")

    def __init__(self, eng, fn, reads, writes, dma, idx):
        self.eng = eng; self.fn = fn; self.reads = reads; self.writes = writes
        self.dma = dma; self.deps = []; self.flag = False; self.ordinal = 0
        self.dval = 0; self.idx = idx; self.waitall = False


class Prog:
    def __init__(self):
        self.ops = []; self.last_w = {}; self.readers = {}; self.dma_groups = {}

    def op(self, eng, fn, reads=(), writes=(), dma=None, waitall=False):
        o = Op(eng, fn, tuple(reads), tuple(writes), dma, len(self.ops))
        o.waitall = waitall
        deps = {}
        wset = set(o.writes)
        for k in o.reads:
            d = self.last_w.get(k)
            if d is not None:
                deps[d.idx] = (d, True)
        for k in o.writes:
            d = self.last_w.get(k)
            if d is not None and d.idx not in deps:
                deps[d.idx] = (d, False)
            for r in self.readers.get(k, ()):
                if r.idx not in deps:
                    deps[r.idx] = (r, False)
        for d, raw in deps.values():
            if d.dma is None and o.dma is None and d.eng == o.eng:
                if o.eng == "pe":
                    continue
            o.deps.append(d)
        for k in o.reads:
            if k not in wset:
                self.readers.setdefault(k, []).append(o)
        for k in o.writes:
            self.last_w[k] = o
            self.readers[k] = []
        if dma is not None:
            g = self.dma_groups.setdefault(dma, [0])
            g[0] += 1
            o.dval = 16 * g[0]
        self.ops.append(o)
        return o

    def emit(self, nc, stack):
        for o in self.ops:
            for d in o.deps:
                if d.dma is None:
                    d.flag = True
        cnt = {e: 0 for e in ENGS}
        for o in self.ops:
            if o.dma is None and o.flag:
                cnt[o.eng] += 1
                o.ordinal = cnt[o.eng]
        esem = {e: stack.enter_context(nc.semaphore("s_" + e)) for e in ENGS}
        dsem = {}
        for g in self.dma_groups:
            dsem[g] = stack.enter_context(nc.semaphore("d_%d" % len(dsem)))
        per = {e: [] for e in ENGS}
        for o in self.ops:
            per[o.eng].append(o)
        groups = self.dma_groups

        def run(engname, eng):
            seen = {}
            for o in per[engname]:
                need = {}
                for d in o.deps:
                    if d.dma is not None:
                        key = ("d", d.dma)
                        val = 16 * groups[d.dma][0] if d.waitall else d.dval
                    else:
                        key = ("e", d.eng)
                        val = d.ordinal
                    if val > need.get(key, 0):
                        need[key] = val
                for key, val in need.items():
                    if seen.get(key, 0) >= val:
                        continue
                    seen[key] = val
                    s = dsem[key[1]] if key[0] == "d" else esem[key[1]]
                    eng.wait_ge(s, val)
                ins = o.fn(eng)
                if o.dma is not None:
                    ins.then_inc(dsem[o.dma], 16)
                elif o.flag:
                    ins.then_inc(esem[o.eng], 1)
            fin = {}
            for o in per[engname]:
                if o.dma is not None:
                    v = 16 * groups[o.dma][0] if o.waitall else o.dval
                    fin[o.dma] = max(fin.get(o.dma, 0), v)
            for g, v in fin.items():
                eng.wait_ge(dsem[g], v)

        with nc.Block() as block:
            @block.tensor
            def _(e): run("pe", e)

            @block.scalar
            def _(e): run("act", e)

            @block.vector
            def _(e): run("dve", e)

            @block.gpsimd
            def _(e): run("pool", e)

            @block.sync
            def _(e): run("sp", e)


def make_consts(S):
    bf = ml_dtypes.bfloat16
    NCC = max(1, S // 2048)
    p = np.arange(128)[:, None]
    c = {}
    c["ident"] = np.eye(128, dtype=np.float32).astype(bf)
    c["ones"] = np.ones((128, 128), np.float32).astype(bf)
    bo = np.zeros((128, 128), np.float32); bo[:64, :64] = 1; bo[64:, 64:] = 1
    c["bones"] = bo.astype(bf)
    x = np.arange(S)[None, :]
    c["wind"] = (x // 64 == p).astype(np.float32).astype(bf)
    y = np.arange(384)[None, :]
    c["cw"] = np.where(p > y - 128, NEG, 0.0).astype(np.float32).astype(bf)
    c["dw"] = np.where(y - p >= 128, NEG, 0.0).astype(np.float32).astype(bf)
    y = np.arange(2304)[None, :]
    c["gw"] = np.where(y < 16 * p + 31, NEG, 0.0).astype(np.float32).astype(bf)
    sel = np.zeros((128, 12, 128), np.float32)
    for m in range(4):
        for b in range(3):
            sel[3 * m + b, m * 3 + b, :64] = 1
            sel[3 * (m + 4) + b, m * 3 + b, 64:] = 1
    c["sel"] = sel.astype(bf)
    A = np.zeros((128, NCC, 129), np.float32)
    for cc in range(NCC):
        for pp in range(128):
            cidx = cc * 128 + pp
            for j in range(128):
                if 4 * j - 1 <= cidx <= 4 * j + 3:
                    A[pp, cc, j] = 1
            A[pp, cc, 128] = 1
    c["aaug"] = A.astype(bf)
    xx = np.arange(256)[None, :]
    rel = xx - 128 - p // 64
    c["fm"] = np.where((rel == 0) | (rel == -1), 1e6, np.where(rel > 0, -1e6, 0.0)).astype(np.float32)
    return c


NPRM = 8 * 3 + 4 + 2 + 124 + 12 + 32
PO_MIX, PO_FFN, PO_MEM, PO_NSA, PO_MQK, PO_CW, PO_CB, PO_LG, PO_LB, PO_POS = 0, 8, 16, 24, 28, 30, 154, 158, 162, 166


def pack_params(inp):
    prm = np.zeros((128, NPRM), np.float32)
    prm[:, PO_MIX:PO_MIX + 8] = inp["norm_mix"][0].reshape(8, 128).T
    prm[:, PO_FFN:PO_FFN + 8] = inp["norm_ffn"][0].reshape(8, 128).T
    prm[:, PO_MEM:PO_MEM + 8] = inp["norm_mem"][0].reshape(8, 128).T
    prm[:, PO_NSA:PO_NSA + 4] = np.concatenate([inp["nsa_qk_norm"][0].T, inp["nsa_qk_norm"][0].T], 0)
    prm[:, PO_MQK:PO_MQK + 2] = inp["mem_qk_norm"][0].T
    cw = inp["conv_w"][0][:, 0, :]
    prm[:, PO_CW:PO_CW + 124] = cw.T.reshape(4, 128, 31).transpose(1, 0, 2).reshape(128, 124)
    prm[:, PO_CB:PO_CB + 4] = inp["conv_b"][0].reshape(4, 128).T
    prm[:, PO_LG:PO_LG + 4] = inp["conv_ln_g"][0].reshape(4, 128).T
    prm[:, PO_LB:PO_LB + 4] = inp["conv_ln_b"][0].reshape(4, 128).T
    pos = inp["cmp_pos"][0]
    prm[:, PO_POS:PO_POS + 32] = pos.reshape(2, 16, 2, 64).transpose(2, 3, 0, 1).reshape(128, 32)
    return prm


class _Stop(Exception):
    pass


def build(S, stop=None):
    NT = S // T
    NKT = S // 128
    NCC = max(1, S // 2048)
    nc = bass.Bass("TRN2", target_bir_lowering=False)
    P = Prog()

    def din(name, shape, dt=F32):
        return nc.dram_tensor(name, list(shape), dt, kind="ExternalInput").ap()

    def dscr(name, shape, dt=BF16):
        return nc.dram_tensor(name, list(shape), dt, kind="Internal").ap()

    xT = din("xT", [1024, S]); memT = din("memT", [1024, 256])
    w_in = din("w_in", [1024, IN_COLS]); w_gate = din("w_gate", [1024, DFF]); w_up = din("w_up", [1024, DFF])
    w_down = din("w_down", [DFF, 1024]); w_out = din("w_out", [1024, 1024])
    w_nsa_out = din("w_nsa_out", [512, 1024]); w_conv_out = din("w_conv_out", [512, 1024])
    w_mem_out = din("w_mem_out", [512, 1024]); w_mem_kv = din("w_mem_kv", [1024, 1024])
    cmp_w1 = din("cmp_w1", [2, 2048, 256]); cmp_w2 = din("cmp_w2", [2, 256, 64])
    prm_d = din("prm", [128, NPRM])
    c_ident = din("ident", [128, 128], BF16); c_ones = din("ones", [128, 128], BF16); c_bones = din("bones", [128, 128], BF16)
    c_wind = din("wind", [128, S], BF16); c_cw = din("cw", [128, 384], BF16); c_dw = din("dw", [128, 384], BF16)
    c_gw = din("gw", [128, 2304], BF16); c_sel = din("sel", [128, 12, 128], BF16)
    c_aaug = din("aaug", [128, NCC, 129], BF16); c_fm = din("fm", [128, 256])
    outT = nc.dram_tensor("outT", [1024, S], F32, kind="ExternalOutput").ap()
    wA = dscr("wA", [1024, IN_COLS]); wG = dscr("wG", [1024, DFF]); wU = dscr("wU", [1024, DFF])
    wD = dscr("wD", [DFF, 1024]); wO = dscr("wO", [1024, 1024]); wN = dscr("wN", [512, 1024])
    wC = dscr("wC", [512, 1024]); wM = dscr("wM", [512, 1024]); wKV = dscr("wKV", [1024, 1024])
    w1b = dscr("w1b", [2, 2048, 256]); w2b = dscr("w2b", [2, 256, 64])

    with ExitStack() as st:
        def sb(name, shape, dt):
            return st.enter_context(nc.sbuf_tensor("sb_" + name, list(shape), dt))

        def psb(name):
            return st.enter_context(nc.psum_tensor(name, [128, 512], F32))

        ident = sb("ident", [128, 128], BF16); ones = sb("ones", [128, 128], BF16); bones = sb("bones", [128, 128], BF16)
        wind = sb("wind", [128, S], BF16); cw = sb("cw", [128, 384], BF16); dw = sb("dw", [128, 384], BF16)
        gw = sb("gw", [128, 2304], BF16); sel = sb("sel", [128, 12, 128], BF16)
        aaug = sb("aaug", [128, NCC, 129], BF16); fm = sb("fm", [128, 256], F32)
        prm = sb("prm", [128, NPRM], F32); prm2 = sb("prm2", [128, 16], F32)
        posb = sb("posb", [128, 32], BF16)
        expo = sb("expo", [128, T], F32)
        KS = sb("KS", [128, S], BF16); VS = sb("VS", [128, NKT, 192], BF16)
        KW = sb("KW", [128, 16, 128], BF16); VW = sb("VW", [128, 16, 192], BF16)
        KC = sb("KC", [128, NCC * 128], BF16); VCT = sb("VCT", [128, NCC * 128], BF16); VC = sb("VC", [128, NCC, 192], BF16)
        KM = sb("KM", [128, 4, 256], BF16); VM = sb("VM", [128, 2, 512], BF16)
        w2p = sb("w2p", [128, 8, 128], BF16); w2f = sb("w2f", [128, 4, 64], BF16)
        b1 = sb("b1", [128, 4], F32)
        ring = sb("ring", [128, 2, 4096], BF16)
        xh2 = sb("xh2", [128, 2, 8, T], F32)
        a8x = sb("a8x", [128, 2, 8, T], BF16)
        a8 = sb("a8", [128, 8, T], BF16)
        actb = sb("actb", [128, 22, T], BF16)
        QP = sb("QP", [128, 8, T], BF16); QM = sb("QM", [128, 4, T], BF16)
        onsa = sb("onsa", [128, 4, T], F32); onsab = sb("onsab", [128, 4, T], BF16)
        cu = sb("cu", [128, 4, T], BF16); om = sb("om", [128, 4, T], BF16)
        XX = sb("XX", [128, 4, 16 + T], BF16)
        ub = sb("ub", [128, 4, 30 + T], BF16)
        PB = sb("PB", [128, 4, 2 * T], BF16)
        lnb = sb("lnb", [128, 3, T], F32)
        sq = sb("sq", [128, 2, T], BF16)
        tf = sb("tf", [128, 8, T], F32)
        rs = sb("rs", [128, 2, T], F32)
        sg = sb("sg", [128, T], BF16)
        cacc = sb("cacc", [128, 4, T], F32); mgb = sb("mgb", [128, 4, T], F32)
        negq = sb("negq", [128, 2, 128], BF16); NEGT = sb("NEGT", [128, 2, T], BF16)
        imp = sb("imp", [128, 2, 2, 128], F32); impf = sb("impf", [128, 2, 128], F32)
        m8 = sb("m8", [128, 16], F32); rden = sb("rden", [128, 2], F32)
        hidf = sb("hidf", [128, 8, 16], F32); hidg = sb("hidg", [128, 8, 16], F32); hidb = sb("hidb", [128, 8, 16], BF16)
        kcf = sb("kcf", [128, 16], F32); kcsq = sb("kcsq", [128, 16], BF16)
        memf = sb("memf", [128, 8, 256], F32)
        _b = {n: psb(n) for n in ("pj0", "pj1", "st0", "st1", "oa0", "oa1", "ms0", "ms1")}
        pools = {"pj": ["pj0", "pj1", "st0", "st1"], "st": ["st0", "st1", "pj0"],
                 "oa": ["oa0", "oa1"], "ms": ["ms0", "ms1"]}
        pctr = {k: 0 for k in pools}

        def bank(pool):
            i = pctr[pool] % len(pools[pool]); pctr[pool] += 1
            nm = pools[pool][i]
            return _b[nm], ("bank", nm)

        tctr = [0]

        def tmpf():
            i = tctr[0] % 8; tctr[0] += 1
            return tf[:, i, :], ("tf", i)

        rctr = [0]

        def rsf():
            i = rctr[0] % 2; rctr[0] += 1
            return rs[:, i, :], ("rs", i)

        sqc = [0]

        def sqf():
            i = sqc[0] % 2; sqc[0] += 1
            return sq[:, i, :], ("sq", i)

        pbc = [0]

        def pbf():
            i = pbc[0] % 4; pbc[0] += 1
            return PB[:, i, :], ("PB", i)

        def MM(out, lhsT, rhs, start, stop, reads, writes, sgc=False):
            P.op("pe", lambda e: e.matmul(out, lhsT=lhsT, rhs=rhs, start=start, stop=stop, skip_group_check=sgc), reads, writes)

        def ACT(out, in_, func, reads, writes, scale=None, bias=None):
            kw = {}
            if scale is not None: kw["scale"] = scale
            if bias is not None: kw["bias"] = bias
            P.op("act", lambda e: e.activation(out=out, in_=in_, func=func, **kw), reads, writes)

        def TS(eng, out, in0, s1, s2, op0, op1, reads, writes):
            if op1 is None:
                P.op(eng, lambda e: e.tensor_scalar(out=out, in0=in0, scalar1=s1, scalar2=None, op0=op0), reads, writes)
            else:
                P.op(eng, lambda e: e.tensor_scalar(out=out, in0=in0, scalar1=s1, scalar2=s2, op0=op0, op1=op1), reads, writes)

        def STT(out, in0, scalar, in1, op0, op1, reads, writes):
            P.op("dve", lambda e: e.scalar_tensor_tensor(out=out, in0=in0, scalar=scalar, in1=in1, op0=op0, op1=op1), reads, writes)

        def TT(eng, out, in0, in1, op, reads, writes):
            P.op(eng, lambda e: e.tensor_tensor(out=out, in0=in0, in1=in1, op=op), reads, writes)

        def CP(eng, out, in_, reads, writes):
            if eng == "act":
                P.op("act", lambda e: e.activation(out=out, in_=in_, func=AF.Copy), reads, writes)
            else:
                P.op(eng, lambda e: e.tensor_copy(out=out, in_=in_), reads, writes)

        def MS(eng, ap, val, writes):
            P.op(eng, lambda e: e.memset(ap, val), (), writes)

        def DMA(eng, out, in_, reads, writes, grp, waitall=False, **kw):
            P.op(eng, lambda e: e.dma_start(out=out, in_=in_, **kw), reads, writes, dma=grp, waitall=waitall)

        def rstd_from(ps_ap, pskey, scale, eps, n=T):
            r, rk = rsf()
            t, tk = tmpf()
            ACT(t[:, 0:n], ps_ap, AF.Sqrt, [pskey, "prm2"], [tk], scale=scale, bias=prm2[:, 8:9] if eps == 1e-6 else prm2[:, 9:10])
            P.op("dve", lambda e, o=r[:, 0:n], i_=t[:, 0:n]: e.reciprocal(out=o, in_=i_), [tk], [rk])
            return r, rk

        def chk(name):
            if stop == name:
                raise _Stop()

        def body():
            for dst, src, k in ((ident, c_ident, "ident"), (ones, c_ones, "ones"), (bones, c_bones, "bones"), (wind, c_wind, "wind"),
                                (cw, c_cw, "cw"), (dw, c_dw, "dw"), (gw, c_gw, "gw"), (sel, c_sel, "sel"), (aaug, c_aaug, "aaug"),
                                (fm, c_fm, "fm"), (prm, prm_d, "prm")):
                DMA("sp", dst[:], src, [], [k], "const", waitall=True)
            MS("pool", expo[:], -0.5, ["expo"])
            MS("dve", VS[:], 1.0, [("VS", i) for i in range(NKT)]); MS("dve", VW[:], 1.0, [("VW", i) for i in range(16)]); MS("dve", VC[:], 1.0, ["VC"])
            MS("dve", KC[:], 0.0, ["KC"]); MS("dve", VCT[:], 0.0, ["VCT"]); MS("dve", KW[:], 0.0, [("KW", i) for i in range(16)])
            MS("pool", QP[:], 0.0, [("QP", h) for h in range(8)])
            MS("pool", XX[:], 0.0, [("XX", i) for i in range(4)])
            MS("pool", ub[:], 0.0, [("ub", i) for i in range(4)])
            MS("pool", w2p[:], 0.0, ["w2p"])
            TS("dve", prm2[:, 0:1], prm[:, PO_NSA:PO_NSA + 1], 0.125, None, ALU.mult, None, ["prm"], ["prm2"])
            TS("dve", prm2[:, 1:2], prm[:, PO_MQK:PO_MQK + 1], 128.0 ** -0.5, None, ALU.mult, None, ["prm"], ["prm2"])
            TS("dve", prm2[:, 2:6], prm[:, PO_LG:PO_LG + 4], 0.5, None, ALU.mult, None, ["prm"], ["prm2"])
            TS("dve", prm2[:, 10:14], prm[:, PO_LB:PO_LB + 4], 0.5, None, ALU.mult, None, ["prm"], ["prm2"])
            MS("dve", prm2[:, 8:9], 1e-6, ["prm2"]); MS("dve", prm2[:, 9:10], 1e-5, ["prm2"])
            TS("dve", prm[:, PO_CW:PO_CW + 124], prm[:, PO_CW:PO_CW + 124], 0.5, None, ALU.mult, None, ["prm"], ["prm"])
            CP("dve", posb[:], prm[:, PO_POS:PO_POS + 32], ["prm"], ["posb"])

            stage32 = [memf[:].rearrange("p k n -> p (k n)"), xh2[:, 1, :, :].rearrange("p k n -> p (k n)"),
                       xh2[:, 0, :, :].rearrange("p k n -> p (k n)"), tf[:].rearrange("p k n -> p (k n)")]
            stage16 = [actb[:, 0:8, :].rearrange("p k n -> p (k n)"), actb[:, 8:16, :].rearrange("p k n -> p (k n)"),
                       a8x[:, 0, :, :].rearrange("p k n -> p (k n)"), a8x[:, 1, :, :].rearrange("p k n -> p (k n)")]
            k32 = [[("st32", 0)], [("st32", 1)], [("st32", 2), ("xh", 0)], [("st32", 3)] + [("tf", q) for q in range(8)]]
            k16 = [[("st16", 0)], [("st16", 1)], [("st16", 2)] + [("a8x", 0, q) for q in range(8)],
                   [("st16", 3)] + [("a8x", 1, q) for q in range(8)]]
            cst = [0]

            cast_list = []

            def cast_piece(dst_ap, src_ap, rows, cols, key, perm=False):
                cast_list.append((dst_ap, src_ap, rows, cols, key, perm))

            def cast_load(idx):
                dst_ap, src_ap, rows, cols, key, perm = cast_list[idx]
                i = idx % 4
                DMA("sp", stage32[i][0:rows, 0:cols], src_ap, [], k32[i], ("st32", i))

            def cast_cp_store(idx):
                dst_ap, src_ap, rows, cols, key, perm = cast_list[idx]
                i = idx % 4
                s32 = stage32[i][0:rows, 0:cols]
                s16 = stage16[i][0:rows, 0:cols]
                eng = ("dve", "act", "dve", "act", "pool")[idx % 5]
                if perm:
                    CP("dve", s16.rearrange("p (m hf d) -> p m hf d", m=4, hf=2), s32.rearrange("p (hf m d) -> p m hf d", m=4, hf=2),
                       k32[i], k16[i])
                else:
                    CP(eng, s16, s32, k32[i], k16[i])
                DMA("sp", dst_ap, s16, k16[i], [(key, i)], ("st16o", i))

            def cast_flush(depth=3):
                n = len(cast_list)
                for idx in range(min(depth, n)):
                    cast_load(idx)
                for idx in range(n):
                    cast_cp_store(idx)
                    if idx + depth < n:
                        cast_load(idx + depth)

            def cast2d(dst, src, R, key, c_lo=0, c_hi=None):
                C = src.shape[1] if c_hi is None else c_hi
                for r0 in range(0, R, 128):
                    r1 = min(R, r0 + 128)
                    for c0 in range(c_lo, C, 2048):
                        c1 = min(C, c0 + 2048)
                        cast_piece(dst[r0:r1, c0:c1], src[r0:r1, c0:c1], r1 - r0, c1 - c0, key)

            for r0 in range(0, 1024, 128):
                cast_piece(wA[r0:r0 + 128, 0:512], w_in[r0:r0 + 128, 0:512], 128, 512, "wA", perm=True)
            cast2d(wA, w_in, 1024, "wA", 512, IN_COLS)
            cast2d(wKV, w_mem_kv, 1024, "wKV")
            for kv in range(2):
                cast2d(w1b[kv], cmp_w1[kv], 2048, "w1b")
                cast2d(w2b[kv], cmp_w2[kv], 256, "w2b")
            for m in range(4):
                cast_piece(wN[m * 128:m * 128 + 64, :], w_nsa_out[m * 64:(m + 1) * 64, :], 64, 1024, "wN")
                cast_piece(wN[m * 128 + 64:m * 128 + 128, :], w_nsa_out[(m + 4) * 64:(m + 5) * 64, :], 64, 1024, "wN")
            cast2d(wC, w_conv_out, 512, "wC"); cast2d(wM, w_mem_out, 512, "wM")
            cast2d(wO, w_out, 1024, "wO")
            cast2d(wG, w_gate, 1024, "wG"); cast2d(wU, w_up, 1024, "wU"); cast2d(wD, w_down, DFF, "wD")
            cast_flush()
            chk("cast")

            rctr2 = [0]

            ring_n = [2]
            memb = memf[:].rearrange("p k n -> p (k n)").bitcast(BF16)

            def wload(src_ap, KC_, ncols, srckey):
                assert KC_ * ncols <= 4096
                i = rctr2[0] % ring_n[0]; rctr2[0] += 1
                if i < 2:
                    view = ring[:, i, 0:KC_ * ncols].rearrange("p (k n) -> p k n", k=KC_)
                    wk = [("ring", i)]
                else:
                    view = memb[:, 0:KC_ * ncols].rearrange("p (k n) -> p k n", k=KC_)
                    wk = [("ring", i), "memf", ("st32", 0)]
                DMA("sp", view, src_ap.rearrange("(k p) n -> p k n", p=128), [(srckey, q) for q in range(4)], wk, ("ring", i))
                return view, ("ring", i)

            def linear(wscr, wkey, KC_, col0, ncols, rhs_fn, rhs_keys, epi, piece=512, chunk=128):
                j = 0
                for c0 in range(col0, col0 + ncols, piece):
                    pc = min(piece, col0 + ncols - c0)
                    view, rk = wload(wscr[0:KC_ * 128, c0:c0 + pc], KC_, pc, wkey)
                    for cc0 in range(0, pc, chunk):
                        ps, pk = bank("pj")
                        for k in range(KC_):
                            MM(ps[:, 0:T], view[:, k, cc0:cc0 + chunk], rhs_fn(k), k == 0, k == KC_ - 1, [rk] + rhs_keys(k), [pk])
                        epi(j, ps, pk)
                        j += 1

            for kv in range(2):
                for ncn in range(2):
                    DMA("sp", tf[:, kv * 2 + ncn, 0:64], cmp_w2[kv, ncn * 128:(ncn + 1) * 128, :], [], [("tf", kv * 2 + ncn)], "w2f", waitall=True)
                    CP("dve", w2f[:, kv * 2 + ncn, :], tf[:, kv * 2 + ncn, 0:64], [("tf", kv * 2 + ncn)], ["w2f"])
            chk("s0")
            for kv in range(2):
                for g in range(2):
                    for ncn in range(2):
                        CP("dve", w2p[:, kv * 4 + g * 2 + ncn, g * 64:(g + 1) * 64], w2f[:, kv * 2 + ncn, :], ["w2f"], ["w2p"])
            chk("s1")
            for kv in range(2):
                psb1, pkb1 = bank("ms")
                for half in range(2):
                    view, rk = wload(w1b[kv, half * 1024:(half + 1) * 1024, :], 8, 256, "w1b")
                    for jp8 in range(8):
                        jp = half * 8 + jp8
                        for ncn in range(2):
                            MM(psb1[:, ncn:ncn + 1], view[:, jp8, ncn * 128:(ncn + 1) * 128], posb[:, kv * 16 + jp:kv * 16 + jp + 1],
                               jp == 0 and ncn == 0, jp == 15, [rk, "posb"], [pkb1], sgc=True)
                CP("dve", b1[:, kv * 2:kv * 2 + 2], psb1[:, 0:2], [pkb1], ["b1"])
            chk("s2")
            DMA("sp", memf[:], memT.rearrange("(k p) n -> p k n", p=128), [], ["memf", ("st32", 0)], "memf")
            psm, pkm = bank("ms")
            for k in range(8):
                s_, sk = sqf()
                ACT(s_, memf[:, k, :], AF.Square, ["memf"], [sk])
                MM(psm[:, 0:256], ones[:], s_, k == 0, k == 7, [sk, "ones"], [pkm])
            rm, rmk = rstd_from(psm[:, 0:256], pkm, 1.0 / 1024, 1e-6, 256)
            for k in range(8):
                STT(a8[:, k, :], memf[:, k, :], prm[:, PO_MEM + k:PO_MEM + k + 1], rm[:, 0:256], ALU.mult, ALU.mult,
                    ["memf", "prm", rmk], [("a8", k)])

            chk("s3")
            def epi_km(j, ps, pk):
                s_, sk = sqf()
                ACT(s_, ps[:, 0:256], AF.Square, [pk], [sk])
                ps2, pk2 = bank("ms")
                MM(ps2[:, 0:256], ones[:], s_, True, True, [sk, "ones"], [pk2])
                r, rk_ = rstd_from(ps2[:, 0:256], pk2, 1.0 / 128, 1e-6, 256)
                STT(KM[:, j, :], ps[:, 0:256], prm[:, PO_MQK + 1:PO_MQK + 2], r[:, 0:256], ALU.mult, ALU.mult, [pk, "prm", rk_], ["KM"])
            linear(wKV, "wKV", 8, 0, 512, lambda k: a8[:, k, :], lambda k: [("a8", k)], epi_km)
            chk("s4")
            for c0 in range(0, 512, 256):
                view, rk = wload(wKV[0:1024, 512 + c0:512 + c0 + 256], 8, 256, "wKV")
                for mc in range(2):
                    ps, pk = bank("pj")
                    for k in range(8):
                        MM(ps[:, 0:256], a8[:, k, mc * 128:(mc + 1) * 128], view[:, k, :], k == 0, k == 7, [rk, ("a8", k)], [pk])
                    CP("dve", VM[:, mc, c0:c0 + 256], ps[:, 0:256], [pk], ["VM"])

            chk("setup")
            ring_n[0] = 3
            def xload(tj):
                par_ = tj % 2
                wk = [("xh", par_)] + ([("st32", 1)] if par_ == 1 else [])
                DMA("sp", xh2[:, par_], xT[:, tj * T:(tj + 1) * T].rearrange("(k p) n -> p k n", p=128), [], wk, ("xh", par_))

            def xnorm(tj):
                par_ = tj % 2
                psn, pkn = bank("ms")
                for k in range(8):
                    s_, sk = sqf()
                    ACT(s_, xh2[:, par_, k, :], AF.Square, [("xh", par_)], [sk])
                    MM(psn[:, 0:T], ones[:], s_, k == 0, k == 7, [sk, "ones"], [pkn])
                rx, rxk = rstd_from(psn[:, 0:T], pkn, 1.0 / 1024, 1e-6)
                for k in range(8):
                    STT(a8x[:, par_, k, :], xh2[:, par_, k, :], prm[:, PO_MIX + k:PO_MIX + k + 1], rx, ALU.mult, ALU.mult,
                        [("xh", par_), "prm", rxk], [("a8x", par_, k)])
            xload(0)
            xnorm(0)
            for ti in range(NT):
                t0 = ti * T
                par = ti % 2
                xh = xh2[:, par]
                xhk = ("xh", par)
                xr = lambda k, par=par: a8x[:, par, k, :]
                xk = lambda k, par=par: [("a8x", par, k)]

                chk("A%d" % ti)
                def epi_q(j, ps, pk):
                    s_, sk = sqf()
                    ACT(s_, ps[:, 0:T], AF.Square, [pk], [sk])
                    ps2, pk2 = bank("ms")
                    MM(ps2[:, 0:T], bones[:], s_, True, True, [sk, "bones"], [pk2])
                    r, rk_ = rstd_from(ps2[:, 0:T], pk2, 1.0 / 64, 1e-6)
                    STT(QP[0:64, j, :], ps[0:64, 0:T], prm2[0:64, 0:1], r[0:64, :], ALU.mult, ALU.mult, [pk, "prm2", rk_], [("QP", j)])
                    STT(QP[64:128, j + 4, :], ps[64:128, 0:T], prm2[64:128, 0:1], r[64:128, :], ALU.mult, ALU.mult, [pk, "prm2", rk_], [("QP", j + 4)])
                linear(wA, "wA", 8, 0, 512, xr, xk, epi_q)

                CP("pool", XX[:, :, 0:16], XX[:, :, T:T + 16], [("XX", i) for i in range(4)], [("XX", i) for i in range(4)])

                def epi_kv(j, ps, pk):
                    if j < 2:
                        kv = j
                        CP("act", XX[0:64, kv * 2 + 0, 16:16 + T], ps[0:64, 0:T], [pk], [("XX", kv * 2)])
                        CP("dve", XX[64:128, kv * 2 + 0, 15:15 + T], ps[0:64, 0:T], [pk], [("XX", kv * 2)])
                        CP("dve", XX[0:64, kv * 2 + 1, 16:16 + T], ps[64:128, 0:T], [pk], [("XX", kv * 2 + 1)])
                        CP("act", XX[64:128, kv * 2 + 1, 15:15 + T], ps[64:128, 0:T], [pk], [("XX", kv * 2 + 1)])
                    else:
                        s_, sk = sqf()
                        ACT(s_, ps[:, 0:T], AF.Square, [pk], [sk])
                        ps2, pk2 = bank("ms")
                        MM(ps2[:, 0:T], bones[:], s_, True, True, [sk, "bones"], [pk2])
                        r, rk_ = rstd_from(ps2[:, 0:T], pk2, 1.0 / 64, 1e-6)
                        if j == 2:
                            STT(KS[:, t0:t0 + T], ps[:, 0:T], prm[:, PO_NSA + 2:PO_NSA + 3], r, ALU.mult, ALU.mult, [pk, "prm", rk_], [("KS", 2 * ti), ("KS", 2 * ti + 1)])
                        else:
                            for s2 in range(2):
                                slot = (2 * ti + s2) % 16
                                STT(KW[:, slot, :], ps[:, s2 * 128:(s2 + 1) * 128], prm[:, PO_NSA + 3:PO_NSA + 4], r[:, s2 * 128:(s2 + 1) * 128],
                                    ALU.mult, ALU.mult, [pk, "prm", rk_], [("KW", slot)])
                viewA, rkA = wload(wA[0:1024, 512:1024], 8, 512, "wA")
                viewB, rkB = wload(wA[0:1024, 1024:1408], 8, 384, "wA")

                def fm_chunk(view, rk, c, epi, j):
                    ps, pk = bank("pj")
                    for k in range(8):
                        MM(ps[:, 0:T], view[:, k, c:c + 128], xr(k), k == 0, k == 7, [rk] + xk(k), [pk])
                    epi(j, ps, pk)

                def vtok(view, rk, c, isw):
                    for s2 in range(2):
                        ps, pk = bank("pj")
                        for k in range(8):
                            MM(ps[:, 0:128], a8x[:, par, k, s2 * 128:(s2 + 1) * 128], view[:, k, c:c + 128], k == 0, k == 7, [rk, ("a8x", par, k)], [pk])
                        kt = 2 * ti + s2
                        if isw:
                            CP("dve", VW[:, kt % 16, 0:64], ps[:, 0:64], [pk], [("VW", kt % 16)])
                            CP("act", VW[:, kt % 16, 128:192], ps[:, 64:128], [pk], [("VW", kt % 16)])
                        else:
                            CP("dve", VS[:, kt, 0:64], ps[:, 0:64], [pk], [("VS", kt)])
                            CP("act", VS[:, kt, 128:192], ps[:, 64:128], [pk], [("VS", kt)])
                fm_chunk(viewA, rkA, 0, epi_kv, 0)
                fm_chunk(viewA, rkA, 128, epi_kv, 1)
                fm_chunk(viewA, rkA, 256, epi_kv, 2)
                vtok(viewA, rkA, 384, False)
                fm_chunk(viewB, rkB, 0, epi_kv, 3)
                vtok(viewB, rkB, 128, True)

                def epi_g(j, ps, pk):
                    t, tk = tmpf()
                    ACT(t, ps[:, 0:T], AF.Tanh, [pk], [tk], scale=0.5)
                    TS("dve", sg[:], t, 0.5, 0.5, ALU.mult, ALU.add, [tk], ["sg"])
                fm_chunk(viewB, rkB, 256, epi_g, 0)

                psh, pkh = bank("ms")
                for kv in range(2):
                    for half in range(1):
                        view, rk = wload(w1b[kv, :, :], 16, 256, "w1b")
                        for g in range(2):
                            for ncn in range(2):
                                col = ((kv * 2 + g) * 2 + ncn) * 16
                                for jp8 in range(16):
                                    jp = jp8
                                    MM(psh[:, col:col + 16], view[:, jp8, ncn * 128:(ncn + 1) * 128],
                                       XX[:, kv * 2 + g, 2 * jp:2 * jp + T - 15:16], jp == 0 and col == 0, jp == 15, [rk, ("XX", kv * 2 + g)], [pkh], sgc=True)
                hv = psh[:, 0:128].rearrange("p (a g n c) -> p a g n c", a=2, g=2, n=2)
                hf = hidf[:].rearrange("p (a g n) c -> p a g n c", a=2, g=2)
                for kv in range(2):
                    for ncn in range(2):
                        ACT(hf[:, kv, :, ncn, :], hv[:, kv, :, ncn, :], AF.Identity, [pkh, "b1"], ["hidf"], bias=b1[:, kv * 2 + ncn:kv * 2 + ncn + 1])
                lo = 1 if ti == 0 else 0
                cb = 16 * ti - 1

                def cstep_B():
                    TT("dve", hidg[:], hidf[:], hidf[:], ALU.mult, ["hidf"], ["hidg"])
                    TS("dve", hidg[:], hidg[:], 0.044715, 1.0, ALU.mult, ALU.add, ["hidg"], ["hidg"])
                    TT("dve", hidg[:], hidg[:], hidf[:], ALU.mult, ["hidg", "hidf"], ["hidg"])

                def cstep_C():
                    ACT(hidg[:], hidg[:], AF.Tanh, ["hidg"], ["hidg"], scale=0.7978845608028654)
                    STT(hidg[:], hidg[:], 1.0, hidf[:], ALU.add, ALU.mult, ["hidg", "hidf"], ["hidg"])
                    TS("dve", hidb[:], hidg[:], 0.5, None, ALU.mult, None, ["hidg"], ["hidb"])

                def cstep_kv(kv):
                    psc, pkc = bank("ms")
                    n = 0
                    for g in range(2):
                        for ncn in range(2):
                            MM(psc[:, 0:16], w2p[:, kv * 4 + g * 2 + ncn, :], hidb[:, (kv * 2 + g) * 2 + ncn, :], n == 0, n == 3, ["w2p", "hidb"], [pkc])
                            n += 1
                    if kv == 0:
                        ACT(kcsq[:], psc[:, 0:16], AF.Square, [pkc], ["kcsq"])
                        ps2, pk2 = bank("ms")
                        MM(ps2[:, 0:16], bones[:], kcsq[:], True, True, ["kcsq", "bones"], [pk2])
                        r, rk_ = rstd_from(ps2[:, 0:16], pk2, 1.0 / 64, 1e-6, 16)
                        STT(KC[:, cb + lo:cb + 16], psc[:, lo:16], prm[:, PO_NSA + 1:PO_NSA + 2], r[:, lo:16], ALU.mult, ALU.mult, [pkc, "prm", rk_], ["KC"])
                    else:
                        CP("dve", VCT[:, cb + lo:cb + 16], psc[:, lo:16], [pkc], ["VCT"])

                def cstep_F():
                    for ccn in sorted({max(0, 16 * ti - 1) // 128, (16 * ti + 14) // 128}):
                        pst, pkt = bank("ms")
                        ptv = pst[:].bitcast(BF16)[:, 0:128]
                        P.op("pe", lambda e, o=ptv, i=VCT[:, ccn * 128:(ccn + 1) * 128]: e.transpose(out=o, in_=i, identity=ident[:]), ["VCT", "ident"], [pkt])
                        CP("dve", VC[:, ccn, 0:64], ptv[:, 0:64], [pkt], ["VC"])
                        CP("act", VC[:, ccn, 128:192], ptv[:, 64:128], [pkt], ["VC"])
                csteps = [cstep_B, cstep_C, lambda: cstep_kv(0), lambda: cstep_kv(1), cstep_F]

                csteps[0]()

                CP("pool", ub[:, :, 0:30], ub[:, :, T:T + 30], [("ub", i) for i in range(4)], [("ub", i) for i in range(4)])
                tg = []

                def epi_ag(j, ps, pk):
                    t, tk = tmpf()
                    ACT(t, ps[:, 0:T], AF.Tanh, [pk], [tk], scale=0.5)
                    tg.append((t, tk))
                linear(wA, "wA", 8, 1816, 512, xr, xk, epi_ag)
                csteps[1]()

                def epi_a(j, ps, pk):
                    t, tk = tg[j]
                    STT(ub[:, j, 30:30 + T], t, 1.0, ps[:, 0:T], ALU.add, ALU.mult, [tk, pk], [("ub", j)])
                linear(wA, "wA", 8, 1304, 512, xr, xk, epi_a)
                csteps[2]()

                def epi_qm(j, ps, pk):
                    s_, sk = sqf()
                    ACT(s_, ps[:, 0:T], AF.Square, [pk], [sk])
                    ps2, pk2 = bank("ms")
                    MM(ps2[:, 0:T], ones[:], s_, True, True, [sk, "ones"], [pk2])
                    r, rk_ = rstd_from(ps2[:, 0:T], pk2, 1.0 / 128, 1e-6)
                    STT(QM[:, j, :], ps[:, 0:T], prm2[:, 1:2], r, ALU.mult, ALU.mult, [pk, "prm2", rk_], [("QM", j)])
                linear(wA, "wA", 8, 2328, 512, xr, xk, epi_qm)
                csteps[3]()
                csteps[4]()

                chk("B%d" % ti)
                chk("C%d" % ti)
                def attn_stream(h, tiles, bkey):
                    po, pko = bank("oa")
                    n = len(tiles)
                    groups = [tiles[i:i + 2] for i in range(0, n, 2)]

                    def issue_qk(grp):
                        ps, pk = bank("st")
                        for gi, (kl, kk, masks, vl, vk, extra) in enumerate(grp):
                            o = ps[:, gi * T:(gi + 1) * T]
                            MM(o, kl, QP[:, h, :], True, len(masks) == 0, kk + [("QP", h)], [pk])
                            for mi, (ml, mr, mk) in enumerate(masks):
                                MM(o, ml, mr, False, mi == len(masks) - 1, mk, [pk])
                        return ps, pk
                    nxt = issue_qk(groups[0])
                    cnt = 0
                    for gidx, grp in enumerate(groups):
                        ps, pk = nxt
                        if gidx + 1 < len(groups):
                            nxt = issue_qk(groups[gidx + 1])
                        pb, pbk = pbf()
                        w = len(grp) * T
                        ACT(pb[:, 0:w], ps[:, 0:w], AF.Exp, [pk], [pbk])
                        for gi, (kl, kk, masks, vl, vk, extra) in enumerate(grp):
                            ph = pb[:, gi * T:(gi + 1) * T]
                            MM(po[:, 0:T], vl, ph, cnt == 0, cnt == n - 1, vk + [pbk], [pko])
                            if extra is not None:
                                extra(ph, pbk, cnt, n)
                            cnt += 1
                    return po, pko

                onsa_init = set()

                def finish(h, br, po, pko):
                    m = h % 4
                    lo_ = h < 4
                    nr = slice(0, 64) if lo_ else slice(64, 128)
                    dr = slice(64, 128) if lo_ else slice(0, 64)
                    t, tk = tmpf()
                    if ti == 0 and br == 0:
                        TS("dve", t[nr, :], po[dr, 0:T], 1e-30, None, ALU.add, None, [pko], [tk])
                        P.op("dve", lambda e, o=t[nr, :]: e.reciprocal(out=o, in_=o), [tk], [tk])
                    else:
                        P.op("dve", lambda e, o=t[nr, :], i_=po[dr, 0:T]: e.reciprocal(out=o, in_=i_), [pko], [tk])
                    pg, pkg = bank("ms")
                    MM(pg[:, 0:T], sel[:, m * 3 + br, :], sg[:], True, True, ["sel", "sg"], [pkg])
                    TT("dve", t[nr, :], t[nr, :], pg[nr, 0:T], ALU.mult, [tk, pkg], [tk])
                    if h not in onsa_init:
                        onsa_init.add(h)
                        TT("dve", onsa[nr, m, :], po[nr, 0:T], t[nr, :], ALU.mult, [pko, tk], [("onsa", m, lo_)])
                    else:
                        t2, tk2 = tmpf()
                        TT("dve", t2[nr, :], po[nr, 0:T], t[nr, :], ALU.mult, [pko, tk], [tk2])
                        TT("pool", onsa[nr, m, :], onsa[nr, m, :], t2[nr, :], ALU.add, [tk2, ("onsa", m, lo_)], [("onsa", m, lo_)])

                def vsel(arr, idx, h):
                    return arr[:, idx, 0:128] if h < 4 else arr[:, idx, 64:192]

                mean, mk_ = lnb[:, 0, :], ("lnb", 0)
                var, vk_ = lnb[:, 1, :], ("lnb", 1)
                rl, rlk = lnb[:, 2, :], ("lnb", 2)
                psE = {}

                def conv_tap(c4, kk):
                    acc, ak = cacc[:, c4, :], ("cacc", c4)
                    wcol = prm[:, PO_CW + c4 * 31 + kk:PO_CW + c4 * 31 + kk + 1]
                    if kk == 0:
                        TS("dve", acc, ub[:, c4, 0:T], wcol, prm[:, PO_CB + c4:PO_CB + c4 + 1], ALU.mult, ALU.add, [("ub", c4), "prm"], [ak])
                    else:
                        STT(acc, ub[:, c4, kk:kk + T], wcol, acc, ALU.mult, ALU.add, [("ub", c4), "prm", ak], [ak])

                def conv_post(c4):
                    acc, ak = cacc[:, c4, :], ("cacc", c4)
                    CP("act", cu[:, c4, :], acc, [ak], [("cu", c4)])
                    ACT(actb[:, 16 + c4, :], acc, AF.Square, [ak], [("actb", 16 + c4)])
                conv_q = []
                for c4_ in range(4):
                    for kk_ in range(31):
                        conv_q.append(lambda c4=c4_, kk=kk_: conv_tap(c4, kk))
                    conv_q.append(("post", c4_))
                conv_def = []

                def conv_drain(k):
                    while conv_def:
                        conv_post(conv_def.pop(0))
                    for _ in range(min(k, len(conv_q))):
                        it = conv_q.pop(0)
                        if isinstance(it, tuple):
                            conv_def.append(it[1])
                        else:
                            it()
                _nw = len(range(max(0, 2 * ti - 4), 2 * ti + 2))
                _ncp = (16 * ti + 14) // 128 + 1
                _ns = 2 * ti + 2
                _units = 5 * _ns
                _done = [0, 0]

                def conv_share(n):
                    _done[0] += n
                    tgt = (128 * _done[0] + _units - 1) // _units
                    conv_drain(tgt - _done[1]); _done[1] = tgt

                def conv_E2():
                    psE1, pkE1 = bank("ms")
                    psE2, pkE2 = bank("ms")
                    for c4 in range(4):
                        MM(psE1[:, 0:T], ones[:], cu[:, c4, :], c4 == 0, c4 == 3, [("cu", c4), "ones"], [pkE1])
                    for c4 in range(4):
                        MM(psE2[:, 0:T], ones[:], actb[:, 16 + c4, :], c4 == 0, c4 == 3, [("actb", 16 + c4), "ones"], [pkE2])
                    ACT(mean, psE1[:, 0:T], AF.Copy, [pkE1], [mk_], scale=1.0 / 512)
                    STT(var, mean, -1.0, mean, ALU.mult, ALU.mult, [mk_], [vk_])
                    STT(var, psE2[:, 0:T], 1.0 / 512, var, ALU.mult, ALU.add, [pkE2, vk_], [vk_])
                    ACT(var, var, AF.Sqrt, [vk_, "prm2"], [vk_], bias=prm2[:, 9:10])
                    P.op("dve", lambda e, o=rl, i_=var: e.reciprocal(out=o, in_=i_), [vk_], [rlk])

                def conv_E3(c4):
                    acc, ak = cacc[:, c4, :], ("cacc", c4)
                    TT("dve", acc, acc, mean, ALU.subtract, [ak, mk_], [ak])
                    TT("dve", acc, acc, rl, ALU.mult, [ak, rlk], [ak])
                    t, tk = tmpf()
                    ACT(t, acc, AF.Tanh, [ak, "prm2"], [tk], scale=prm2[:, 2 + c4:3 + c4], bias=prm2[:, 10 + c4:11 + c4])
                    TS("dve", acc, acc, prm2[:, 2 + c4:3 + c4], prm2[:, 10 + c4:11 + c4], ALU.mult, ALU.add, [ak, "prm2"], [ak])
                    STT(cu[:, c4, :], t, 1.0, acc, ALU.add, ALU.mult, [tk, ak], [("cu", c4)])

                chk("Dc%d" % ti)
                def w_job(h):
                    tiles = []
                    for kt in range(max(0, 2 * ti - 4), 2 * ti + 2):
                        r_ = kt - 2 * ti
                        masks = []
                        if r_ >= 0:
                            masks = [(ident[:], cw[:, 128 - 128 * r_:128 - 128 * r_ + T], ["ident", "cw"])]
                        elif r_ == -3:
                            masks = [(ident[:], dw[:, 0:T], ["ident", "dw"])]
                        elif r_ == -4:
                            masks = [(ident[:], dw[:, 128:128 + T], ["ident", "dw"])]
                        tiles.append((KW[:, kt % 16, :], [("KW", kt % 16)], masks, vsel(VW, kt % 16, h), [("VW", kt % 16)], None))
                    po, pko = yield (h, tiles)
                    finish(h, 2, po, pko)
                    if h >= 8:
                        conv_share(_nw)
                ncc_t = (16 * ti + 14) // 128 + 1

                def c_job(h):
                    g = h // 4
                    pu_box = {}

                    def extra_u(pb, pbk, i, n):
                        if i == 0:
                            pu, pku_ = bank("ms")
                            pu_box["uv"] = pu[:].rearrange("p (s c) -> p s c", s=2)
                            pu_box["k"] = pku_
                        uv, pku = pu_box["uv"], pu_box["k"]
                        for s2 in range(2):
                            MM(uv[:, s2, 0:129], pb[:, s2 * 128:(s2 + 1) * 128], aaug[:, i, :], i == 0 and s2 == 0, i == n - 1, [pbk, "aaug"], [pku], sgc=True)
                    tiles = []
                    for cc in range(ncc_t):
                        s_ = ti - 8 * cc
                        masks = [(ident[:], gw[:, 256 * s_:256 * s_ + T], ["ident", "gw"])] if s_ <= 8 else []
                        tiles.append((KC[:, cc * 128:(cc + 1) * 128], ["KC"], masks, vsel(VC, cc, h), ["VC"], extra_u))
                    po, pko = yield (h, tiles)
                    uv, pku = pu_box["uv"], pu_box["k"]
                    finish(h, 0, po, pko)
                    TS("dve", rden[:], uv[:, :, 128], 1e-30, None, ALU.add, None, [pku], ["rden"])
                    P.op("dve", lambda e: e.reciprocal(out=rden[:], in_=rden[:]), ["rden"], ["rden"])
                    for s2 in range(2):
                        if h % 4 == 0:
                            TS("dve", imp[:, g, s2, :], uv[:, s2, 0:128], rden[:, s2:s2 + 1], None, ALU.mult, None, [pku, "rden"], [("imp", g)])
                        else:
                            STT(imp[:, g, s2, :], uv[:, s2, 0:128], rden[:, s2:s2 + 1], imp[:, g, s2, :], ALU.mult, ALU.add,
                                [pku, "rden", ("imp", g)], [("imp", g)])
                    if h % 4 == 3:
                        for s2 in range(2):
                            B = 4 * ti + 2 * s2
                            TT("dve", impf[:, 0, :], imp[:, g, s2, :], fm[:, 128 - B:256 - B], ALU.add, [("imp", g), "fm"], ["impf"])
                            MS("dve", impf[:, 0, 0:1], 2e6, ["impf"])
                            P.op("dve", lambda e: e.max(out=m8[:, 0:8], in_=impf[:, 0, :]), ["impf"], ["m8"])
                            P.op("dve", lambda e: e.match_replace(out=impf[:, 1, :], in_to_replace=m8[:, 0:8], in_values=impf[:, 0, :],
                                                                  imm_value=-3.0e38), ["impf", "m8"], ["impf2"])
                            P.op("dve", lambda e: e.max(out=m8[:, 8:16], in_=impf[:, 1, :]), ["impf2"], ["m8"])
                            TS("dve", negq[:, s2, :], impf[:, 0, :], m8[:, 15:16], NEG, ALU.is_lt, ALU.mult, ["impf", "m8"], [("negq", s2)])
                            pt2, pkt2 = bank("ms")
                            pv = pt2[:].bitcast(BF16)[:, 0:128]
                            P.op("pe", lambda e, o=pv, i_=negq[:, s2, :]: e.transpose(out=o, in_=i_, identity=ident[:]), [("negq", s2), "ident"], [pkt2])
                            CP("act", NEGT[:, g, s2 * 128:(s2 + 1) * 128], pv, [pkt2], [("NEGT", g)])
                    if h >= 8:
                        conv_share(_ncp)
                chk("Dw%d" % ti)
                def s_job(h):
                    g = h // 4
                    tiles = []
                    for kt in range(0, 2 * ti + 2):
                        r_ = kt - 2 * ti
                        masks = [(wind[:, kt * 128:(kt + 1) * 128], NEGT[:, g, :], ["wind", ("NEGT", g)])]
                        if r_ >= 0:
                            masks.append((ident[:], cw[:, 128 - 128 * r_:128 - 128 * r_ + T], ["ident", "cw"]))
                        tiles.append((KS[:, kt * 128:(kt + 1) * 128], [("KS", kt)], masks, vsel(VS, kt, h), [("VS", kt)], None))
                    po, pko = yield (h, tiles)
                    finish(h, 1, po, pko)

                jobs = [("C", 0), ("W", 0), ("C", 1), ("W", 1), ("C", 2), ("W", 2), ("C", 3), ("W", 3),
                        ("C", 4), ("W", 4), ("C", 5), ("W", 5), ("C", 6), ("W", 6), ("C", 7), ("W", 7),
                        ("S", 0), ("S", 1), ("S", 2), ("S", 3), ("S", 4), ("S", 5), ("S", 6), ("S", 7)]
                nS = [0]

                def job_gen(kind, h):
                    if kind == "W":
                        yield from w_job(h)
                    elif kind == "C":
                        yield from c_job(h)
                    else:
                        yield from s_job(h)
                        nS[0] += 1
                        if nS[0] <= 4:
                            conv_share(_ns)
                        elif nS[0] == 5:
                            conv_drain(1000)
                        elif nS[0] == 6:
                            conv_drain(0)
                            conv_E2()
                        elif nS[0] == 7:
                            conv_E3(0); conv_E3(1)
                        else:
                            conv_E3(2); conv_E3(3)

                reqs = []
                for kind, h in jobs:
                    g_ = job_gen(kind, h)
                    h_, tiles_ = next(g_)
                    reqs.append({"h": h_, "tiles": tiles_, "gen": g_, "po": None, "cnt": 0, "n": len(tiles_)})
                seq = []
                for ji, rq in enumerate(reqs):
                    for i0 in range(0, rq["n"], 2):
                        seq.append((ji, rq["tiles"][i0:i0 + 2]))

                def issue_qk2(ji, grp):
                    hq = reqs[ji]["h"]
                    ps, pk = bank("st")
                    for gi, (kl, kk, masks, vl, vk, extra) in enumerate(grp):
                        o = ps[:, gi * T:(gi + 1) * T]
                        MM(o, kl, QP[:, hq, :], True, len(masks) == 0, kk + [("QP", hq)], [pk])
                        for mi, (ml, mr, mk) in enumerate(masks):
                            MM(o, ml, mr, False, mi == len(masks) - 1, mk, [pk])
                    return ps, pk
                pend = [issue_qk2(*seq[q]) for q in range(min(2, len(seq)))]
                for si, (ji, grp) in enumerate(seq):
                    ps, pk = pend.pop(0)
                    if si + 2 < len(seq):
                        pend.append(issue_qk2(*seq[si + 2]))
                    rq = reqs[ji]
                    pb, pbk = pbf()
                    w = len(grp) * T
                    ACT(pb[:, 0:w], ps[:, 0:w], AF.Exp, [pk], [pbk])
                    if rq["po"] is None:
                        rq["po"], rq["pko"] = bank("oa")
                    for gi, (kl, kk, masks, vl, vk, extra) in enumerate(grp):
                        ph = pb[:, gi * T:(gi + 1) * T]
                        MM(rq["po"][:, 0:T], vl, ph, rq["cnt"] == 0, rq["cnt"] == rq["n"] - 1, vk + [pbk], [rq["pko"]])
                        if extra is not None:
                            extra(ph, pbk, rq["cnt"], rq["n"])
                        rq["cnt"] += 1
                    if rq["cnt"] == rq["n"]:
                        try:
                            rq["gen"].send((rq["po"], rq["pko"]))
                        except StopIteration:
                            pass
                for m in range(4):
                    CP("pool", onsab[:, m, :], onsa[:, m, :], [("onsa", m, True), ("onsa", m, False)], [("onsab", m)])
                chk("E%d" % ti)
                for hd in range(4):
                    po, pko = bank("oa")
                    pd, pkd = bank("ms")
                    for mc in range(2):
                        ps, pk = bank("st")
                        MM(ps[:, 0:T], KM[:, hd, mc * 128:(mc + 1) * 128], QM[:, hd, :], True, True, ["KM", ("QM", hd)], [pk])
                        pb, pbk = pbf()
                        ACT(pb[:, 0:T], ps[:, 0:T], AF.Exp, [pk], [pbk])
                        MM(po[:, 0:T], VM[:, mc, hd * 128:(hd + 1) * 128], pb[:, 0:T], mc == 0, mc == 1, ["VM", pbk], [pko])
                        MM(pd[:, 0:T], ones[:], pb[:, 0:T], mc == 0, mc == 1, ["ones", pbk], [pkd])
                    t, tk = tmpf()
                    P.op("dve", lambda e, o=t, i_=pd[:, 0:T]: e.reciprocal(out=o, in_=i_), [pkd], [tk])
                    TT("dve", om[:, hd, :], po[:, 0:T], t, ALU.mult, [pko, tk], [("om", hd)])

                chk("F%d" % ti)
                if ti + 1 < NT:
                    xload(ti + 1)
                br_src = [(wN, "wN", lambda k: onsab[:, k, :], lambda k: [("onsab", k)]),
                          (wC, "wC", lambda k: cu[:, k, :], lambda k: [("cu", k)]),
                          (wM, "wM", lambda k: om[:, k, :], lambda k: [("om", k)])]
                for half in range(2):
                    mg = {}
                    for b in range(3):
                        gts = []

                        def epi_gm(j, ps, pk, gts=gts):
                            t, tk = tmpf()
                            ACT(t, ps[:, 0:T], AF.Tanh, [pk], [tk], scale=0.5)
                            gts.append((t, tk))
                        linear(wA, "wA", 8, 2840 + b * 1024 + half * 512, 512, xr, xk, epi_gm)
                        wsc, wk_, rf, rkf = br_src[b]

                        def epi_br(j, ps, pk, gts=gts, b=b):
                            t, tk = gts[j]
                            oc = half * 4 + j
                            if b == 0:
                                STT(mgb[:, j, :], t, 1.0, ps[:, 0:T], ALU.add, ALU.mult, [tk, pk], [("mgb", j)])
                            else:
                                STT(t, t, 1.0, ps[:, 0:T], ALU.add, ALU.mult, [tk, pk], [tk])
                                if b == 1:
                                    TT("pool", mgb[:, j, :], mgb[:, j, :], t, ALU.add, [("mgb", j), tk], [("mgb", j)])
                                else:
                                    TT("pool", actb[:, oc, :], mgb[:, j, :], t, ALU.add, [("mgb", j), tk], [("actb", oc), ("st16", 0), ("st16", 1)] if ti == 0 else [("actb", oc)])
                        linear(wsc, wk_, 4, half * 512, 512, rf, rkf, epi_br, piece=512)
                def epi_out(j, ps, pk):
                    STT(xh[:, j, :], ps[:, 0:T], 0.5, xh[:, j, :], ALU.mult, ALU.add, [pk, xhk], [xhk])
                linear(wO, "wO", 8, 0, 1024, lambda k: actb[:, k, :], lambda k: [("actb", k)], epi_out)

                chk("G%d" % ti)
                if ti + 1 < NT:
                    xnorm(ti + 1)
                psn, pkn = bank("ms")
                for k in range(8):
                    s_, sk = sqf()
                    ACT(s_, xh[:, k, :], AF.Square, [xhk], [sk])
                    MM(psn[:, 0:T], ones[:], s_, k == 0, k == 7, [sk, "ones"], [pkn])
                rh, rhk = rstd_from(psn[:, 0:T], pkn, 1.0 / 1024, 1e-6)
                for k in range(8):
                    STT(a8[:, k, :], xh[:, k, :], prm[:, PO_FFN + k:PO_FFN + k + 1], rh, ALU.mult, ALU.mult, [xhk, "prm", rhk], [("a8", k)])
                hr = lambda k: a8[:, k, :]
                hk = lambda k: [("a8", k)]
                for c0 in range(0, DFF, 512):
                    pc_ = min(512, DFF - c0)
                    gl = []

                    def epi_gate(j, ps, pk, gl=gl):
                        t, tk = tmpf()
                        ACT(t, ps[:, 0:T], AF.Tanh, [pk], [tk], scale=0.5)
                        STT(t, t, 1.0, ps[:, 0:T], ALU.add, ALU.mult, [tk, pk], [tk])
                        gl.append((t, tk))
                    linear(wG, "wG", 8, c0, pc_, hr, hk, epi_gate)

                    def epi_up(j, ps, pk, gl=gl, c0=c0):
                        t, tk = gl[j]
                        f = c0 // 128 + j
                        TT("dve", actb[:, f, :], t, ps[:, 0:T], ALU.mult, [tk, pk], [("actb", f)])
                    linear(wU, "wU", 8, c0, pc_, hr, hk, epi_up)
                for half in range(2):
                    bks = [bank("pj") for _ in range(4)]
                    for (r0, kc_) in ((0, 8), (1024, 8), (2048, 6)):
                        view, rk = wload(wD[r0:r0 + kc_ * 128, half * 512:(half + 1) * 512], kc_, 512, "wD")
                        for j in range(4):
                            ps, pk = bks[j]
                            for k in range(kc_):
                                f = r0 // 128 + k
                                MM(ps[:, 0:T], view[:, k, j * 128:(j + 1) * 128], actb[:, f, :], f == 0, f == 21, [rk, ("actb", f)], [pk])
                    for j in range(4):
                        oc = half * 4 + j
                        ps, pk = bks[j]
                        STT(xh[:, oc, :], ps[:, 0:T], 0.5, xh[:, oc, :], ALU.mult, ALU.add, [pk, xhk], [xhk])
                DMA("act", outT[:, t0:t0 + T].rearrange("(k p) n -> p k n", p=128), xh, [xhk], [], ("out", par))

        try:
            body()
        except _Stop:
            pass
        P.emit(nc, st)
    return nc


_CACHE = {}


def kernel(**inputs):
    x = np.asarray(inputs["x"], np.float32)
    B, S, D = x.shape
    mem = np.asarray(inputs["mem"], np.float32)
    if S not in _CACHE:
        _CACHE[S] = build(S)
    nc = _CACHE[S]
    consts = make_consts(S)
    prm = pack_params({k: np.asarray(v, np.float32) for k, v in inputs.items()})
    shared = {
        "w_in": np.ascontiguousarray(inputs["w_in"][0], np.float32), "w_gate": np.ascontiguousarray(inputs["w_gate"][0], np.float32),
        "w_up": np.ascontiguousarray(inputs["w_up"][0], np.float32), "w_down": np.ascontiguousarray(inputs["w_down"][0], np.float32),
        "w_out": np.ascontiguousarray(inputs["w_out"][0], np.float32), "w_nsa_out": np.ascontiguousarray(inputs["w_nsa_out"][0], np.float32),
        "w_conv_out": np.ascontiguousarray(inputs["w_conv_out"][0], np.float32), "w_mem_out": np.ascontiguousarray(inputs["w_mem_out"][0], np.float32),
        "w_mem_kv": np.ascontiguousarray(inputs["w_mem_kv"][0], np.float32), "cmp_w1": np.ascontiguousarray(inputs["cmp_w1"][0], np.float32),
        "cmp_w2": np.ascontiguousarray(inputs["cmp_w2"][0], np.float32), "prm": prm,
    }
    shared.update(consts)
    in_maps = []
    for b in range(B):
        m = dict(shared)
        m["xT"] = np.ascontiguousarray(x[b].T)
        m["memT"] = np.ascontiguousarray(mem[b].T)
        in_maps.append(m)
    res = run_bass_kernel_spmd(nc, in_maps, core_ids=list(range(B)))
    out = np.empty((B, S, D), np.float32)
    for b in range(B):
        out[b] = res.results[b]["outT"].T
    return out
```
